# Optimizing a Trainium2 kernel written in Bass

```python
import jax, jax.numpy as jnp
from jax import lax
import numpy as np

D_MODEL = 1024
BATCH = 4
SEQ = 8192
DEPTH = 2
DEC_BATCH = 8
DEC_SEQ = 64
PAST_LEN = 2048

CHUNK = 64
N_HEADS = 8
HEAD_DIM = 64
KV_HEADS = 2
GROUPS = N_HEADS // KV_HEADS
IDX_HEADS = 4
IDX_DIM = 64
TOPK_MAX = 256
Q_BLOCK = 128
ROPE_THETA = 500000.0
ATTN_SCALE = HEAD_DIM ** -0.5
IDX_SCALE = (IDX_DIM ** -0.5) * (IDX_HEADS ** -0.5)
NEG = -1e30
HGRN_HEADS = 4
HGRN_DK = 128
HGRN_DV = 128
SCONV_WIDTH = 512
CONV_W = 3
BRANCH_WIDTH = 512
N_BRANCH = 3
D_FF = 2816
ALPHA = (2 * DEPTH) ** 0.25
BETA = (8 * DEPTH) ** -0.25
LN_EPS = 1e-5
IN_SIZES = (N_HEADS * HEAD_DIM, KV_HEADS * HEAD_DIM, KV_HEADS * HEAD_DIM,
            IDX_HEADS * IDX_DIM, IDX_DIM, IDX_HEADS,
            HGRN_HEADS * HGRN_DK, HGRN_HEADS * HGRN_DK, HGRN_HEADS * HGRN_DV, HGRN_HEADS * HGRN_DV,
            SCONV_WIDTH, SCONV_WIDTH, SCONV_WIDTH,
            N_BRANCH * D_MODEL)
IN_COLS = sum(IN_SIZES)

kernel_name = "hybrid_dsa_hgrn2_shortconv_stream_step"


def layer_norm(x, g, b):
    xf = x.astype(jnp.float32)
    mu = jnp.mean(xf, axis=-1, keepdims=True)
    var = jnp.mean(jnp.square(xf - mu), axis=-1, keepdims=True)
    return ((xf - mu) * lax.rsqrt(var + LN_EPS) * g + b).astype(x.dtype)


def partial_rotary(x, pos):
    rot = x.shape[-1] // 4
    half = rot // 2
    inv_freq = ROPE_THETA ** (-jnp.arange(half, dtype=jnp.float32) / half)
    ang = pos.astype(jnp.float32)[:, None] * inv_freq[None, :]
    cos = jnp.cos(ang)[:, None, :]
    sin = jnp.sin(ang)[:, None, :]
    x1 = x[..., :half].astype(jnp.float32)
    x2 = x[..., half:rot].astype(jnp.float32)
    return jnp.concatenate([(x1 * cos - x2 * sin).astype(x.dtype),
                            (x2 * cos + x1 * sin).astype(x.dtype),
                            x[..., rot:]], axis=-1)


def causal_dwconv(u, hist, w, b):
    S = u.shape[1]
    up = jnp.concatenate([hist.astype(u.dtype), u], axis=1)
    y = b
    for j in range(CONV_W):
        y = y + w[j] * up[:, j:j + S]
    return y, up[:, S:]


def dsa_attention(q, qi, wi, k_all, v_all, ki_all, q_pos):
    f32 = jnp.float32
    B, S = q.shape[:2]
    L = k_all.shape[1]
    top = min(TOPK_MAX, L // 4)
    qb = min(Q_BLOCK, S)
    nb = S // qb
    key_chunk = jnp.arange(L) // CHUNK
    kif = ki_all.astype(f32)

    def blocks(a):
        return jnp.swapaxes(a.reshape((B, nb, qb) + a.shape[2:]), 0, 1)

    def one_block(args):
        q_blk, qi_blk, wi_blk, qpos = args
        admissible = key_chunk[None, :] <= (qpos // CHUNK)[:, None]
        rel = jax.nn.relu(jnp.einsum('bqhd,bsd->bqhs', qi_blk.astype(f32), kif))
        score = jnp.einsum('bqh,bqhs->bqs', wi_blk.astype(f32), rel) * IDX_SCALE
        score = jnp.where(admissible[None], score, NEG)
        _, idx = lax.top_k(score, top)
        valid = admissible[jnp.arange(qb)[None, :, None], idx]
        k_sel = jax.vmap(lambda kk, ii: kk[ii])(k_all, idx)
        v_sel = jax.vmap(lambda vv, ii: vv[ii])(v_all, idx)
        qg = q_blk.reshape(B, qb, KV_HEADS, GROUPS, HEAD_DIM)
        logits = jnp.einsum('bqhgd,bqnhd->bqhgn', qg, k_sel, preferred_element_type=f32) * ATTN_SCALE
        logits = jnp.where(valid[:, :, None, None, :], logits, NEG)
        p = jax.nn.softmax(logits, axis=-1).astype(v_sel.dtype)
        o = jnp.einsum('bqhgn,bqnhd->bqhgd', p, v_sel)
        return o.reshape(B, qb, N_HEADS * HEAD_DIM)

    out = lax.map(one_block, (blocks(q), blocks(qi), blocks(wi), q_pos.reshape(nb, qb)))
    return jnp.swapaxes(out, 0, 1).reshape(B, S, N_HEADS * HEAD_DIM)


def hgrn2_scan(q, k, v, logf, s0):
    B, S, H = q.shape[:3]
    c = min(CHUNK, S)
    n = S // c
    tri = jnp.tril(jnp.ones((c, c), dtype=bool))[:, :, None]

    def chunks(a):
        return a.reshape(B, n, c, a.shape[2], a.shape[3]).transpose(1, 0, 3, 2, 4)

    def step(state, inp):
        qc, kc, vc, lc = inp
        cum = jnp.cumsum(lc, axis=2)
        diff = cum[:, :, :, None, :] - cum[:, :, None, :, :]
        decay = jnp.where(tri, jnp.exp(jnp.where(tri, diff, 0.0)), 0.0)
        scores = jnp.einsum('bhtd,bhsd,bhtsd->bhts', qc, kc, decay)
        o = (jnp.einsum('bhts,bhsv->bhtv', scores, vc)
             + jnp.einsum('bhtd,bhdv->bhtv', qc * jnp.exp(cum), state))
        last = cum[:, :, -1:, :]
        new_state = (jnp.exp(last[:, :, 0, :, None]) * state
                     + jnp.einsum('bhsd,bhsv->bhdv', kc * jnp.exp(last - cum), vc))
        return new_state, o

    s_fin, o = lax.scan(step, s0, (chunks(q), chunks(k), chunks(v), chunks(logf)))
    return o.transpose(1, 0, 3, 2, 4).reshape(B, S, H, v.shape[3]), s_fin


def trunk_layer(x, pos, k_past, v_past, ki_past, s0, sc_hist, ffn_hist, l, weights):
    (w_in, hgrn_lb_logits, hgrn_norm_g, sconv_w, sconv_b, w_branch, w_out, ln1_g, ln1_b,
     w_up, ffn_conv_w, ffn_conv_b, w_down, ln2_g, ln2_b) = weights
    f32 = jnp.float32
    B, S, _ = x.shape
    proj = x @ w_in[l]
    (a_q, a_k, a_v, i_q, i_k, i_w, h_q, h_f, h_i, h_g, c_b, c_c, c_x, g_pre) = jnp.split(
        proj, np.cumsum(IN_SIZES)[:-1].tolist(), axis=-1)

    q = partial_rotary(a_q.reshape(B, S, N_HEADS, HEAD_DIM), pos)
    k = partial_rotary(a_k.reshape(B, S, KV_HEADS, HEAD_DIM), pos)
    v = a_v.reshape(B, S, KV_HEADS, HEAD_DIM)
    qi = partial_rotary(i_q.reshape(B, S, IDX_HEADS, IDX_DIM), pos)
    ki = partial_rotary(i_k.reshape(B, S, 1, IDX_DIM), pos)[:, :, 0]
    if k_past is None:
        k_all, v_all, ki_all = k, v, ki
    else:
        k_all = jnp.concatenate([k_past.astype(k.dtype), k], axis=1)
        v_all = jnp.concatenate([v_past.astype(v.dtype), v], axis=1)
        ki_all = jnp.concatenate([ki_past.astype(ki.dtype), ki], axis=1)
    y_a = dsa_attention(q, qi, i_w, k_all, v_all, ki_all, pos).astype(x.dtype)

    lbp = jax.nn.softmax(hgrn_lb_logits.astype(f32), axis=0)
    lb = (jnp.cumsum(lbp, axis=0) - lbp[0])[l].reshape(HGRN_HEADS, HGRN_DK)
    f_pre = h_f.reshape(B, S, HGRN_HEADS, HGRN_DK).astype(f32)
    logf = jax.nn.log_sigmoid(f_pre) + jnp.log1p(lb * jnp.exp(-f_pre))
    k_h = (1.0 - lb) * jax.nn.sigmoid(-f_pre)
    q_h = jax.nn.silu(h_q.reshape(B, S, HGRN_HEADS, HGRN_DK).astype(f32))
    v_h = h_i.reshape(B, S, HGRN_HEADS, HGRN_DV).astype(f32)
    o_h, s_new = hgrn2_scan(q_h, k_h, v_h, logf, s0.astype(f32))
    o_h = o_h * lax.rsqrt(jnp.mean(jnp.square(o_h), axis=-1, keepdims=True) + LN_EPS) * hgrn_norm_g[l].astype(f32)
    y_b = (o_h.reshape(B, S, HGRN_HEADS * HGRN_DV) * jax.nn.silu(h_g.astype(f32))).astype(x.dtype)

    u = c_c * c_x
    u_conv, sc_new = causal_dwconv(u, sc_hist, sconv_w[l], sconv_b[l])
    y_c = c_b * u_conv

    gates = jax.nn.sigmoid(g_pre.reshape(B, S, N_BRANCH, D_MODEL).astype(f32)).astype(x.dtype)
    merged = (gates[:, :, 0] * (y_a @ w_branch[l, 0])
              + gates[:, :, 1] * (y_b @ w_branch[l, 1])
              + gates[:, :, 2] * (y_c @ w_branch[l, 2]))
    x = layer_norm(ALPHA * x + merged @ w_out[l], ln1_g[l], ln1_b[l])

    h = x @ w_up[l]
    h_conv, ffn_new = causal_dwconv(h, ffn_hist, ffn_conv_w[l], ffn_conv_b[l])
    a_g, b_v = jnp.split(h_conv, 2, axis=-1)
    ffn = (jax.nn.silu(a_g) * b_v) @ w_down[l]
    x = layer_norm(ALPHA * x + ffn, ln2_g[l], ln2_b[l])
    return x, (k, v, ki, s_new.astype(x.dtype), sc_new, ffn_new)


def setup_inputs(seed: int = 0) -> dict:
    key = jax.random.key(seed)
    ks = jax.random.split(key, 24)

    def nrm(k, shape, scale=1.0):
        return jax.random.normal(k, shape, jnp.float32) * scale

    return {
        'x_prompt': nrm(ks[0], (BATCH, SEQ, D_MODEL)),
        'x_sample': nrm(ks[1], (DEC_BATCH, DEC_SEQ, D_MODEL)),
        'cache_attn_k': nrm(ks[2], (DEPTH, DEC_BATCH, PAST_LEN, KV_HEADS, HEAD_DIM)),
        'cache_attn_v': nrm(ks[3], (DEPTH, DEC_BATCH, PAST_LEN, KV_HEADS, HEAD_DIM)),
        'cache_idx_k': nrm(ks[4], (DEPTH, DEC_BATCH, PAST_LEN, IDX_DIM)),
        'state_hgrn': nrm(ks[5], (DEPTH, DEC_BATCH, HGRN_HEADS, HGRN_DK, HGRN_DV), 0.3),
        'state_sconv': nrm(ks[6], (DEPTH, DEC_BATCH, CONV_W - 1, SCONV_WIDTH)),
        'state_ffn_conv': nrm(ks[7], (DEPTH, DEC_BATCH, CONV_W - 1, 2 * D_FF)),
        'w_in': nrm(ks[8], (DEPTH, D_MODEL, IN_COLS), D_MODEL ** -0.5),
        'hgrn_lb_logits': nrm(ks[9], (DEPTH, HGRN_HEADS * HGRN_DK), 0.5),
        'hgrn_norm_g': 1.0 + nrm(ks[10], (DEPTH, HGRN_DV), 0.02),
        'sconv_w': nrm(ks[11], (DEPTH, CONV_W, SCONV_WIDTH), CONV_W ** -0.5),
        'sconv_b': nrm(ks[12], (DEPTH, SCONV_WIDTH), 0.02),
        'w_branch': nrm(ks[13], (DEPTH, N_BRANCH, BRANCH_WIDTH, D_MODEL), BETA * BRANCH_WIDTH ** -0.5),
        'w_out': nrm(ks[14], (DEPTH, D_MODEL, D_MODEL), BETA * D_MODEL ** -0.5),
        'ln1_g': 1.0 + nrm(ks[15], (DEPTH, D_MODEL), 0.02),
        'ln1_b': nrm(ks[16], (DEPTH, D_MODEL), 0.02),
        'w_up': nrm(ks[17], (DEPTH, D_MODEL, 2 * D_FF), D_MODEL ** -0.5),
        'ffn_conv_w': nrm(ks[18], (DEPTH, CONV_W, 2 * D_FF), CONV_W ** -0.5),
        'ffn_conv_b': nrm(ks[19], (DEPTH, 2 * D_FF), 0.02),
        'w_down': nrm(ks[20], (DEPTH, D_FF, D_MODEL), BETA * D_FF ** -0.5),
        'ln2_g': 1.0 + nrm(ks[21], (DEPTH, D_MODEL), 0.02),
        'ln2_b': nrm(ks[22], (DEPTH, D_MODEL), 0.02),
    }


def reference(x_prompt, x_sample, cache_attn_k, cache_attn_v, cache_idx_k, state_hgrn, state_sconv,
              state_ffn_conv, w_in, hgrn_lb_logits, hgrn_norm_g, sconv_w, sconv_b, w_branch, w_out,
              ln1_g, ln1_b, w_up, ffn_conv_w, ffn_conv_b, w_down, ln2_g, ln2_b):
    weights = (w_in, hgrn_lb_logits, hgrn_norm_g, sconv_w, sconv_b, w_branch, w_out, ln1_g, ln1_b,
               w_up, ffn_conv_w, ffn_conv_b, w_down, ln2_g, ln2_b)
    B, S, _ = x_prompt.shape
    DB, DS, _ = x_sample.shape
    P = cache_attn_k.shape[2]
    pos_p = jnp.arange(S)
    pos_s = P + jnp.arange(DS)
    xp, xs = x_prompt, x_sample
    st_p = [[] for _ in range(6)]
    st_s = [[] for _ in range(6)]
    for l in range(DEPTH):
        xp, new_p = trunk_layer(
            xp, pos_p, None, None, None,
            jnp.zeros((B, HGRN_HEADS, HGRN_DK, HGRN_DV), jnp.float32),
            jnp.zeros((B, CONV_W - 1, SCONV_WIDTH), xp.dtype),
            jnp.zeros((B, CONV_W - 1, 2 * D_FF), xp.dtype), l, weights)
        xs, new_s = trunk_layer(
            xs, pos_s, cache_attn_k[l], cache_attn_v[l], cache_idx_k[l], state_hgrn[l],
            state_sconv[l], state_ffn_conv[l], l, weights)
        for j in range(6):
            st_p[j].append(new_p[j])
            st_s[j].append(new_s[j])
    k_p, v_p, ki_p, h_p, sc_p, ff_p = [jnp.stack(a, axis=0) for a in st_p]
    k_s, v_s, ki_s, h_s, sc_s, ff_s = [jnp.stack(a, axis=0) for a in st_s]
    return (xp, xs, k_p, v_p, ki_p, h_p, sc_p, ff_p, k_s, v_s, ki_s, h_s, sc_s, ff_s)
```

```python
import numpy as np
import concourse.bass as bass
import concourse.mybir as mybir
from concourse.bass_utils import run_bass_kernel_spmd
from contextlib import ExitStack

F32 = mybir.dt.float32
BF16 = mybir.dt.bfloat16
AF = mybir.ActivationFunctionType
ALU = mybir.AluOpType


class Buf:
    __slots__ = ("name", "w", "r", "excl")

    def __init__(self, name):
        self.name = name
        self.w = None
        self.r = {}
        self.excl = False


class V:
    __slots__ = ("ap", "buf")

    def __init__(self, ap, buf):
        self.ap = ap
        self.buf = buf

    def __getitem__(self, idx):
        return V(self.ap[idx], self.buf)

    def re(self, pat, **kw):
        return V(self.ap.rearrange(pat, **kw), self.buf)


class T:
    def __init__(self, k, name, shape, dtype, psum=False, buf=None):
        if psum:
            self.t = k.es.enter_context(k.nc.psum_tensor("t_" + name, shape, dtype))
        else:
            self.t = k.es.enter_context(k.nc.sbuf_tensor("t_" + name, shape, dtype))
        self.buf = buf if buf is not None else Buf(name)
        self.buf.excl = bool(psum)

    def __getitem__(self, idx):
        return V(self.t[idx], self.buf)


class K:
    def __init__(self, nc, es):
        self.nc = nc
        self.es = es
        self.eng = {"pe": nc.tensor, "act": nc.scalar, "dve": nc.vector, "pool": nc.gpsimd, "sp": nc.sync}
        self.sem = {}
        self.cnt = {}
        self.seen = {e: {} for e in self.eng}
        self.ekey = {}
        self.eep = {}
        for e in ("pe", "act", "dve", "pool"):
            self.sem[e] = es.enter_context(nc.semaphore("s_" + e))
            self.cnt[e] = 0
            self.ekey[e] = e
            self.eep[e] = 0
        self.nchan = 0
        self.ninst = 0

    def chan(self, name):
        key = "ch_%d_%s" % (self.nchan, name)
        self.nchan += 1
        self.sem[key] = self.es.enter_context(self.nc.semaphore(key))
        self.cnt[key] = 0
        return key

    def _wait(self, e, deps):
        eng = self.eng[e]
        best = {}
        for key, val in deps:
            if e == "pe" and key.startswith("pe"):
                continue
            if best.get(key, 0) < val:
                best[key] = val
        for key, val in best.items():
            if self.seen[e].get(key, 0) < val:
                eng.wait_ge(self.sem[key], val)
                self.seen[e][key] = val

    def op(self, e, fn, r=(), w=(), chan=None, inc=True):
        rb = [x.buf if not isinstance(x, Buf) else x for x in r if x is not None]
        wb = [x.buf if not isinstance(x, Buf) else x for x in w if x is not None]
        wb = wb + [b for b in rb if b.excl]
        rb = [b for b in rb if not b.excl]
        deps = []
        for b in rb:
            if b.w is not None:
                deps.append(b.w)
        for b in wb:
            if b.w is not None:
                deps.append(b.w)
            deps.extend(b.r.items())
        self._wait(e, deps)
        inst = fn(self.eng[e])
        self.ninst += 1
        if chan is None:
            ek = self.ekey[e]
            if self.cnt[ek] >= 30000:
                self.eep[e] += 1
                ek = "%s_%d" % (e, self.eep[e])
                self.ekey[e] = ek
                self.sem[ek] = self.es.enter_context(self.nc.semaphore("s_" + ek))
                self.cnt[ek] = 0
            if inc:
                self.cnt[ek] += 1
                inst.then_inc(self.sem[ek], 1)
                tk = (ek, self.cnt[ek])
            else:
                tk = (ek, self.cnt[ek] + 1)
        else:
            self.cnt[chan] += 16
            inst.then_inc(self.sem[chan], 16)
            tk = (chan, self.cnt[chan])
        for b in rb:
            if b.r.get(tk[0], 0) < tk[1]:
                b.r[tk[0]] = tk[1]
        for b in wb:
            b.w = tk
            b.r = {}
        return inst

    def mm(self, o, lhsT, rhs, start=True, stop=True, extra_r=(), inc=True, **kw):
        return self.op("pe", lambda g: g.matmul(o.ap, lhsT.ap, rhs.ap, start=start, stop=stop, **kw),
                       r=[lhsT, rhs] + list(extra_r), w=[o], inc=inc)

    def tr(self, o, in_, ident):
        return self.op("pe", lambda g: g.transpose(o.ap, in_.ap, ident.ap), r=[in_, ident], w=[o])

    def act(self, o, in_, func, bias=None, scale=None, accum=None, e="act"):
        kw = {}
        rr = [in_]
        if bias is not None:
            if isinstance(bias, V):
                kw["bias"] = bias.ap
                rr.append(bias)
            else:
                kw["bias"] = bias
        if scale is not None:
            if isinstance(scale, V):
                kw["scale"] = scale.ap
                rr.append(scale)
            else:
                kw["scale"] = scale
        ww = [o]
        if accum is not None:
            kw["accum_out"] = accum.ap
            ww.append(accum)
        return self.op("act", lambda g: g.activation(o.ap, in_.ap, func, **kw), r=rr, w=ww)

    def tt(self, e, o, a, b, op):
        return self.op(e, lambda g: g.tensor_tensor(o.ap, a.ap, b.ap, op), r=[a, b], w=[o])

    def ts(self, e, o, a, s1, op0, s2=None, op1=None, accum=None):
        rr = [a]
        v1 = s1
        v2 = s2
        if isinstance(s1, V):
            rr.append(s1)
            v1 = s1.ap
        if isinstance(s2, V):
            rr.append(s2)
            v2 = s2.ap
        ww = [o]
        kw = {}
        if accum is not None:
            kw["accum_out"] = accum.ap
            ww.append(accum)
        if op1 is None:
            return self.op(e, lambda g: g.tensor_scalar(o.ap, a.ap, v1, None, op0, **kw), r=rr, w=ww)
        return self.op(e, lambda g: g.tensor_scalar(o.ap, a.ap, v1, v2, op0, op1, **kw), r=rr, w=ww)

    def stt(self, o, a, s, b, op0, op1):
        rr = [a, b]
        v = s
        if isinstance(s, V):
            rr.append(s)
            v = s.ap
        return self.op("dve", lambda g: g.scalar_tensor_tensor(o.ap, a.ap, v, b.ap, op0, op1), r=rr, w=[o])

    def copy(self, e, o, a):
        if e == "act":
            return self.op(e, lambda g: g.copy(o.ap, a.ap), r=[a], w=[o])
        return self.op(e, lambda g: g.tensor_copy(o.ap, a.ap), r=[a], w=[o])

    def memset(self, e, o, val):
        return self.op(e, lambda g: g.memset(o.ap, val), r=[], w=[o])

    def dma(self, e, chan, o, i, r=(), w=(), **kw):
        oa = o.ap if isinstance(o, V) else o
        ia = i.ap if isinstance(i, V) else i
        rr = list(r) + ([i] if isinstance(i, V) else [])
        ww = list(w) + ([o] if isinstance(o, V) else [])
        return self.op(e, lambda g: g.dma_start(out=oa, in_=ia, **kw), r=rr, w=ww, chan=chan)

    def wait_all(self, e):
        deps = [(key, c) for key, c in self.cnt.items() if c > 0]
        self._wait(e, deps)

D = 1024
DFF = 2816
ALPHA = 4.0 ** 0.25
LN_EPS = 1e-5
NEG = -1e30
MASKNEG = -30000.0
O_HQ, O_HF, O_HI, O_HG, O_CB, O_CC, O_CX, O_G = 1092, 1604, 2116, 2628, 3140, 3652, 4164, 4676


class Cfg:
    def __init__(self, S=8192, DS=64, P=2048, TOPK=256, NIT=28, stop=9):
        self.S, self.DS, self.P, self.TOPK, self.NIT = S, DS, P, TOPK, NIT
        self.stop = stop
        self.sub = 9
        self.LMAX = max(S, P + DS)
        self.LMAX = ((self.LMAX + 1023) // 1024) * 1024
        self.NKT = self.LMAX // 128


def _tile(W, c0, width, k0=0, nk=None):
    K = W.shape[0]
    if nk is None:
        nk = K // 128
    sub = W[k0 * 128:(k0 + nk) * 128, c0:c0 + width]
    return np.ascontiguousarray(sub.reshape(nk, 128, width).transpose(1, 0, 2).reshape(128, nk * width))


def weight_layout():
    items = []
    for j in range(8):
        items.append(("hqf%d" % j, 1024))
    for j in range(12):
        items.append(("c%d" % j, 1024))
    for j in range(24):
        items.append(("g%d" % j, 1024))
    for j in range(44):
        items.append(("up%d" % j, 1024))
    for b in range(3):
        for j in range(8):
            items.append(("br%d_%d" % (b, j), 512))
    items += [("att0", 4096), ("att1", 4096), ("att2", 8 * 68), ("hig0", 4096), ("hig1", 4096),
              ("wout0", 4096), ("wout1", 4096)]
    for c in range(2):
        for kg, nk in enumerate((8, 8, 6)):
            items.append(("wd%d_%d" % (c, kg), nk * 512))
    off = {}
    o = 0
    for n, w in items:
        off[n] = (o, w)
        o += w
    return items, off, o


def host_weights(w_in, w_branch, w_out, w_up, w_down):
    items, off, tot = weight_layout()
    out = np.empty((2, 128, tot), np.float32)
    for l in range(2):
        parts = []
        for j in range(8):
            parts.append(_tile(w_in[l], O_HQ + 128 * j, 128))
        for j in range(12):
            parts.append(_tile(w_in[l], O_CB + 128 * j, 128))
        for j in range(24):
            parts.append(_tile(w_in[l], O_G + 128 * j, 128))
        for j in range(44):
            parts.append(_tile(w_up[l], 128 * j, 128))
        perm = np.concatenate([np.r_[c * 64:(c + 1) * 64, (4 + c) * 64:(5 + c) * 64] for c in range(4)])
        for b in range(3):
            Wb = w_branch[l, b][perm] if b == 0 else w_branch[l, b]
            for j in range(8):
                parts.append(_tile(Wb, 128 * j, 128))
        parts += [_tile(w_in[l], 0, 512), _tile(w_in[l], 512, 512), _tile(w_in[l], 1024, 68),
                  _tile(w_in[l], O_HI, 512), _tile(w_in[l], O_HG, 512),
                  _tile(w_out[l], 0, 512), _tile(w_out[l], 512, 512)]
        for c in range(2):
            for k0, nk in ((0, 8), (8, 8), (16, 6)):
                parts.append(_tile(w_down[l], 512 * c, 512, k0, nk))
        out[l] = np.concatenate(parts, axis=1)
    return out


def rope_table(pos):
    half = 8
    inv = (500000.0 ** (-np.arange(half, dtype=np.float32) / np.float32(half))).astype(np.float32)
    ang = pos.astype(np.float32)[:, None] * inv[None, :]
    cos = np.cos(ang).astype(np.float32)
    sin = np.sin(ang).astype(np.float32)
    C16 = np.concatenate([cos, cos], axis=1)
    S16 = np.concatenate([-sin, sin], axis=1)
    tab = np.stack([np.tile(C16[:, None, :], (1, 8, 1)), np.tile(S16[:, None, :], (1, 8, 1))], axis=1)
    return np.ascontiguousarray(tab.reshape(len(pos), 256))


def build(cfg):
    S, DS, P = cfg.S, cfg.DS, cfg.P
    LMAX, NKT = cfg.LMAX, cfg.NKT
    nc = bass.Bass("TRN2", target_bir_lowering=False)
    _, woff, WTOT = weight_layout()

    def din(name, shape, dt=F32):
        return nc.dram_tensor(name, list(shape), dt, kind="ExternalInput").ap()

    def dout(name, shape, dt=F32):
        return nc.dram_tensor(name, list(shape), dt, kind="ExternalOutput").ap()

    xp = din("xp", [S, D]); xs = din("xs", [DS, D])
    ck = din("ck", [2, P, 128]); cv = din("cv", [2, P, 128]); cki = din("cki", [2, P, 64])
    sh = din("sh", [2, 4, 128, 128]); ssc = din("ssc", [2, 128, 8]); sff = din("sff", [2, 128, 88])
    wfl = din("wfl", [2, 128, WTOT])
    lbl = din("lbl", [128, 8]); gB_d = din("gB", [2, 128, 512])
    scw_d = din("scw", [2, 128, 16]); fcw_d = din("fcw", [2, 128, 176])
    lnp_d = din("lnp", [2, 128, 4096])
    ropep = din("ropep", [S, 256]); ropes = din("ropes", [DS, 256])

    yp = dout("yp", [S, D]); ys = dout("ys", [DS, D])
    okp = dout("okp", [2, S, 128]); ovp = dout("ovp", [2, S, 128]); okip = dout("okip", [2, S, 64])
    ohp = dout("ohp", [2, 4, 128, 128]); oscp = dout("oscp", [2, 128, 8]); offp = dout("offp", [2, 128, 88])
    oks = dout("oks", [2, DS, 128]); ovs = dout("ovs", [2, DS, 128]); okis = dout("okis", [2, DS, 64])
    ohs = dout("ohs", [2, 4, 128, 128]); oscs = dout("oscs", [2, 128, 8]); offs = dout("offs", [2, 128, 88])

    wbf = nc.dram_tensor("wbf", [2, 128, WTOT], BF16, kind="Internal").ap()
    xmid_p = nc.dram_tensor("xmid_p", [S, D], F32, kind="Internal").ap()
    xmid_s = nc.dram_tensor("xmid_s", [DS, D], F32, kind="Internal").ap()

    es = ExitStack()
    with es:
        k = K(nc, es)
        kT = T(k, "kT", [128, LMAX], BF16)
        kiTp = T(k, "kiTp", [128, LMAX // 2], BF16)
        Vb = T(k, "Vb", [128, NKT, 128], BF16)
        ones_bf = T(k, "ones_bf", [128, 64], BF16)
        AR1 = T(k, "AR1", [128, 8192], F32)
        AR2 = T(k, "AR2", [128, 4096], F32)
        x_sb = T(k, "x_sb", [128, 4, D], F32)
        xs_v = [V(x_sb.t[:, t, :], Buf("xs%d" % t)) for t in range(4)]
        xT = T(k, "xT", [128, 8, 512], BF16)
        wfm = [T(k, "wfm%d" % i, [128, 1024], BF16) for i in range(4)]
        wf4 = [T(k, "wf4%d" % i, [128, 512], BF16) for i in range(2)]
        wtm = [T(k, "wtm%d" % i, [128, 4096], BF16) for i in range(2)]
        wsm = T(k, "wsm", [128, 544], BF16)
        yaT = T(k, "yaT", [128, 4, 512], BF16)
        ybT = T(k, "ybT", [128, 4, 512], BF16)
        ycT = T(k, "ycT", [128, 4, 512], BF16)
        att_f = T(k, "att_f", [128, 1092], F32)
        att_b = T(k, "att_b", [128, 1280], BF16)
        qTs = [T(k, "qT%d" % i, [128, 512], BF16) for i in range(2)]
        junk = T(k, "junk", [128, 4992], mybir.dt.uint8)
        junkA = T(k, "junkA", [128, 3328], mybir.dt.int8)
        smA = T(k, "smA", [128, 4], F32)
        smM = T(k, "smM", [128, 4], F32)
        smC = T(k, "smC", [128, 4], F32)
        smD = T(k, "smD", [128, 4], F32)
        qiT = T(k, "qiT", [128, 512], BF16)
        rr = [T(k, "rr%d" % i, [128, 512], F32) for i in range(2)]
        pT = [T(k, "pT%d" % i, [128, 512], BF16) for i in range(3)]
        rope = T(k, "rope", [128, 256], F32)
        ra = T(k, "ra", [128, 128], F32)
        rb = T(k, "rb", [128, 128], F32)
        xbfs = [T(k, "xbf%d" % i, [128, D], BF16) for i in range(2)]
        sm = T(k, "sm", [128, 64], F32)
        bst = T(k, "bst", [128, 16], F32)
        Sst = T(k, "Sst", [128, 4, 128], F32)
        Sbf = T(k, "Sbf", [128, 128], BF16)
        scm = T(k, "scm", [128, 128], BF16)
        Ktok = T(k, "Ktok", [128, 128], BF16)
        uh = T(k, "uh", [128, 8], F32)
        fh = T(k, "fh", [128, 88], F32)
        lb = T(k, "lb", [128, 8], F32)
        oml = T(k, "oml", [128, 8], F32)
        lbraw = T(k, "lbraw", [128, 8], F32)
        gB = T(k, "gB", [128, 512], F32)
        scw = T(k, "scw", [128, 16], F32)
        fcw = T(k, "fcw", [128, 176], F32)
        lnp = T(k, "lnp", [128, 4096], F32)
        ident = T(k, "ident", [128, 128], BF16)
        I4p = T(k, "I4p", [128, 512], BF16)
        I4s = T(k, "I4s", [128, 256], BF16)
        bmask = T(k, "bmask", [128, 128], F32)
        ones = T(k, "ones", [128, 64], F32)
        cst = T(k, "cst", [128, 8], F32)
        psf = [T(k, "psf%d" % i, [128, 512], F32, psum=True) for i in range(6)]
        psb = [T(k, "psb%d" % i, [128, 1024], BF16, psum=True) for i in range(2)]

        def aview(ar, off, n, dt, name):
            ap = ar.t[:, off:off + n]
            if dt == BF16:
                ap = ap.bitcast(BF16)
            return V(ap, Buf(name))

        Sc = V(AR1.t[:, :], Buf("Sc"))
        Mb = V(AR2.t[:, :].bitcast(BF16), Buf("Mb"))
        hs_ = [aview(AR1, 512 * i, 512, F32, "h%d" % i) for i in range(4)]
        sg = aview(AR1, 2048, 2048, F32, "sg")
        vb = aview(AR1, 4096, 1024, BF16, "vb")
        QA = aview(AR1, 5120, 256, BF16, "QA")
        QB = aview(AR1, 5376, 256, BF16, "QB")
        KT = aview(AR1, 5632, 256, BF16, "KT")
        on = aview(AR1, 5888, 512, F32, "on")
        ybb = aview(AR1, 6400, 256, BF16, "ybb")
        gT = aview(AR1, 0, 5632, BF16, "gT")
        ue = [aview(AR1, 5632, 520, F32, "ue0"), aview(AR1, 6152, 520, F32, "ue1")]
        fac = [aview(AR1, 6672, 512, F32, "fac0"), aview(AR1, 7184, 512, F32, "fac1")]
        sa = aview(AR2, 0, 512, F32, "sa")
        mg = aview(AR2, 512, 512, F32, "mg")
        sgm = aview(AR2, 1024, 512, F32, "sgm")
        tmpm = aview(AR2, 1536, 512, F32, "tmpm")
        useA = [Sc, Mb]
        useB = hs_ + [sg, vb, QA, QB, KT, on, ybb]
        useC = [gT] + ue + fac + [sa, mg, sgm, tmpm]

        def claim(new, old):
            deps = {}
            for v in old:
                b = v.buf
                if b.w is not None:
                    deps[b.w[0]] = max(deps.get(b.w[0], 0), b.w[1])
                for kk, vv in b.r.items():
                    deps[kk] = max(deps.get(kk, 0), vv)
            for v in new:
                v.buf.w = None
                v.buf.r = dict(deps)

        chn = {}

        def ch(name):
            if name not in chn:
                chn[name] = k.chan(name)
            return chn[name]

        k.memset("pool", ident[:, :], 1.0)
        k.op("pool", lambda g: g.affine_select(ident.t[:, :], ident.t[:, :], [[1, 128]], ALU.is_equal, 0.0,
                                               base=0, channel_multiplier=-1), r=[ident], w=[ident])
        k.memset("pool", I4s[:, :], 0.0)
        for j in range(4):
            k.copy("pool", I4p[:, j * 128:(j + 1) * 128], ident[:, :])
            k.copy("pool", I4s[0:64, j * 64:(j + 1) * 64], ident[0:64, 0:64])
        k.memset("pool", bmask[:, :], 1.0)
        k.op("pool", lambda g: g.affine_select(bmask.t[:, :], bmask.t[:, :], [[1, 128]], ALU.is_ge, 0.0,
                                               base=0, channel_multiplier=-1), r=[bmask], w=[bmask])
        k.memset("pool", bmask[0:64, 64:128], 0.0)
        k.memset("pool", ones[:, :], 1.0)
        k.memset("pool", cst[:, 0:1], 1.0)
        k.memset("pool", cst[:, 1:2], LN_EPS)
        k.memset("dve", Vb[:, :, :], 0.0)
        k.memset("dve", ones_bf[:, :], 1.0)
        k.memset("pool", kT[:, :], 0.0)
        k.memset("pool", kiTp[:, :], 0.0)
        k.memset("pool", qiT[:, :], 0.0)
        k.memset("dve", QA[:, :], 0.0)
        k.memset("dve", QB[:, :], 0.0)
        k.dma("sp", ch("lbraw"), lbraw[:, :], lbl)
        k.memset("dve", lb[:, 0:4], 0.0)
        k.tt("dve", lb[:, 4:8], lbraw[:, 4:8], lbraw[:, 0:4], ALU.subtract)
        k.act(lb[:, 4:8], lb[:, 4:8], AF.Sigmoid)
        k.ts("dve", oml[:, :], lb[:, :], -1.0, ALU.mult, 1.0, ALU.add)

        wbuf = Buf("wbf")
        NSPL = 16
        step = (WTOT + NSPL - 1) // NSPL
        cw = ch("wcast")
        for l in range(2):
            for i in range(NSPL):
                a, b = i * step, min(WTOT, (i + 1) * step)
                if a >= b:
                    continue
                k.dma("pool", cw, wbf[l, :, a:b], wfl[l, :, a:b], w=[wbuf])

        wq = {"fm": 0, "f4": 0, "tm": 0}

        def loadw(l, name, kind):
            o, w = woff[name]
            if kind == "fm":
                t = wfm[wq["fm"] % 4]; wq["fm"] += 1
            elif kind == "f4":
                t = wf4[wq["f4"] % 2]; wq["f4"] += 1
            elif kind == "tm":
                t = wtm[wq["tm"] % 2]; wq["tm"] += 1
            else:
                t = wsm
            k.dma("sp", ch("w_" + t.buf.name), t[:, 0:w], wbf[l, :, o:o + w], r=[wbuf])
            return t

        rot = {"i": 0, "set": [0, 1, 2, 3]}

        def rbank():
            b = psf[rot["set"][rot["i"] % len(rot["set"])]]
            rot["i"] += 1
            return b

        def make_xT(TP, NT):
            for t in range(NT):
                xbf = xbfs[t % 2]
                k.copy("act", xbf[:TP, :], xs_v[t][:TP, :])
                for half in range(2):
                    pb = psb[half]
                    for c in range(4):
                        k.tr(pb[:, c * 128:c * 128 + TP], xbf[:TP, (half * 4 + c) * 128:(half * 4 + c + 1) * 128],
                             ident[:TP, :TP])
                    src = pb[:, 0:512].re("p (c t) -> p c t", c=4)[:, :, 0:TP]
                    k.copy("act" if half == 0 else "dve", xT[:, half * 4:(half + 1) * 4, t * TP:(t + 1) * TP], src)

        def layer_norm(TP, t, which):
            xv = xs_v[t][:TP, :]
            for c in range(2):
                k.op("dve", lambda g, c=c: g.bn_stats(bst.t[:TP, c * 6:(c + 1) * 6], xs_v[t].ap[:TP, c * 512:(c + 1) * 512]),
                     r=[xs_v[t]], w=[bst])
            k.op("dve", lambda g: g.bn_aggr(sm.t[:TP, 32:34], bst.t[:TP, 0:12]), r=[bst], w=[sm])
            k.act(sm[:TP, 34:35], sm[:TP, 33:34], AF.Sqrt, bias=cst[:TP, 1:2], scale=1.0)
            k.op("dve", lambda g: g.reciprocal(sm.t[:TP, 35:36], sm.t[:TP, 34:35]), r=[sm], w=[sm])
            k.ts("dve", xv, xv, sm[:TP, 32:33], ALU.subtract, sm[:TP, 35:36], ALU.mult)
            k.tt("dve", xv, xv, lnp[:TP, which * 2048:which * 2048 + 1024], ALU.mult)
            k.tt("dve", xv, xv, lnp[:TP, which * 2048 + 1024:which * 2048 + 2048], ALU.add)

        def rotary(TP, ps, out, nh, c16off=0):
            pv = ps.re("p (h d) -> p h d", d=64)
            ov = out.re("p (h d) -> p h d", d=64)
            Cv = rope[:TP, 0:nh * 16].re("p (h d) -> p h d", d=16)
            Sv = rope[:TP, 128:128 + nh * 16].re("p (h d) -> p h d", d=16)
            av = ra[:TP, 0:nh * 16].re("p (h d) -> p h d", d=16)
            bv = rb[:TP, 0:nh * 16].re("p (h d) -> p h d", d=16)
            import os
            rv = os.environ.get("ROTV", "")
            if rv != "noact":
                k.copy("act", ov[:, :, 16:64], pv[:, :, 16:64])
            if rv == "nodve":
                return
            k.tt("dve", av, pv[:, :, 0:16], Cv, ALU.mult)
            k.tt("dve", bv[:, :, 0:8], pv[:, :, 8:16], Sv[:, :, 0:8], ALU.mult)
            k.tt("dve", bv[:, :, 8:16], pv[:, :, 0:8], Sv[:, :, 8:16], ALU.mult)
            k.tt("dve", ov[:, :, 0:16], av, bv, ALU.add)

        def attn_A(l, t, TP, Lprev, masked, rope_src, okd, ovd, okid, row0):
            Lb = Lprev + TP
            kt_new = Lprev // 128
            qTc = qTs[t % 2]
            k.dma("sp", ch("rope"), rope[:TP, :], rope_src)
            w0 = loadw(l, "att0", "tm"); w1 = loadw(l, "att1", "tm"); w2 = loadw(l, "att2", "sm")
            rot["set"] = [0, 1, 2, 3]
            pA, pB, pC = rbank(), rbank(), rbank()
            for (pp, ww, wd) in ((pA, w0, 512), (pB, w1, 512), (pC, w2, 68)):
                for kc in range(8):
                    k.mm(pp[:TP, 0:wd], xT[:, kc, t * TP:(t + 1) * TP], ww[:, kc * wd:(kc + 1) * wd],
                         start=(kc == 0), stop=(kc == 7), inc=(kc == 7))
            rotary(TP, pA[:TP, 0:512], att_f[:TP, 0:512], 8)
            rotary(TP, pB[:TP, 0:512], att_f[:TP, 512:1024], 8)
            k.copy("act", att_f[:TP, 640:768].re("p (h d) -> p h d", d=64)[:, :, 0:16],
                   pB[:TP, 128:256].re("p (h d) -> p h d", d=64)[:, :, 0:16])
            rotary(TP, pC[:TP, 0:64], att_f[:TP, 1024:1088], 1)
            k.copy("act", att_f[:TP, 1088:1092], pC[:TP, 64:68])
            k.dma("pool", ch("st_att"), okd[l, row0:row0 + TP, :], att_f[:TP, 512:640])
            k.dma("pool", ch("st_att"), ovd[l, row0:row0 + TP, :], att_f[:TP, 640:768])
            k.dma("pool", ch("st_att"), okid[l, row0:row0 + TP, :], att_f[:TP, 1024:1088])
            k.copy("pool", att_b[:TP, 0:512].re("p (j g d) -> p j g d", j=4, g=2),
                   att_f[:TP, 0:512].re("p (g j d) -> p j g d", g=2, j=4))
            k.copy("pool", att_b[:TP, 512:640], att_f[:TP, 512:640])
            qdv = att_b[:TP, 640:1152].re("p (h u d) -> p h u d", h=4, u=2)
            qsv = att_f[:TP, 768:1024].re("p (h d) -> p h d", h=4)
            k.copy("pool", qdv[:, :, 0, :], qsv)
            k.copy("pool", qdv[:, :, 1, :], qsv)
            k.copy("pool", att_b[:TP, 1152:1216], att_f[:TP, 1024:1088])
            k.copy("pool", att_b[:TP, 1216:1280], att_f[:TP, 1024:1088])
            k.copy("pool", Vb[:TP, kt_new, :], att_f[:TP, 640:768])
            for j in range(4):
                k.tr(psb[0][:, j * 128:j * 128 + TP], att_b[:TP, j * 128:(j + 1) * 128], ident[:TP, :TP])
            k.act(qTc[:, 0:4 * TP].re("p (j t) -> p j t", j=4),
                  psb[0][:, 0:512].re("p (j t) -> p j t", j=4)[:, :, 0:TP], AF.Copy, scale=0.125)
            k.tr(psb[1][:, 0:TP], att_b[:TP, 512:640], ident[:TP, :TP])
            for h in range(4):
                k.tr(psb[1][:, 128 * (1 + h):128 * (1 + h) + TP], att_b[:TP, 640 + 128 * h:640 + 128 * (h + 1)],
                     ident[:TP, :TP])
            k.tr(psb[1][:, 640:640 + TP], att_b[:TP, 1152:1280], ident[:TP, :TP])
            k.copy("dve", kT[:, Lprev:Lprev + TP], psb[1][:, 0:TP])
            k.copy("act", qiT[:, 0:4 * TP].re("p (j t) -> p j t", j=4),
                   psb[1][:, 128:640].re("p (j t) -> p j t", j=4)[:, :, 0:TP])
            Bn = Lprev // 512
            hf = Bn % 2
            c0 = (Bn // 2) * 512 + (Lprev % 512)
            k.copy("dve", kiTp[hf * 64:hf * 64 + 64, c0:c0 + TP], psb[1][hf * 64:hf * 64 + 64, 640:640 + TP])
            nblk = (Lb + 511) // 512
            for B in range(nblk):
                wB = min(512, Lb - 512 * B)
                hb = B % 2
                cb = (B // 2) * 512
                for h in range(4):
                    pb = rbank()
                    k.mm(pb[:, 0:wB], qiT[hb * 64:hb * 64 + 64, h * TP:h * TP + 128],
                         kiTp[hb * 64:hb * 64 + 64, cb:cb + wB])
                    r_ = rr[(B * 4 + h) % 2]
                    k.act(r_[:TP, 0:wB], pb[:TP, 0:wB], AF.Relu)
                    if h == 0:
                        k.ts("dve", Sc[:TP, 512 * B:512 * B + wB], r_[:TP, 0:wB], att_f[:TP, 1088:1089], ALU.mult)
                    else:
                        k.stt(Sc[:TP, 512 * B:512 * B + wB], r_[:TP, 0:wB], att_f[:TP, 1088 + h:1089 + h],
                              Sc[:TP, 512 * B:512 * B + wB], ALU.mult, ALU.add)
            if masked:
                k.memset("dve", Sc[0:64, Lb - 64:Lb], NEG)
            return dict(t=t, TP=TP, Lb=Lb, qT=qTc)

        def attn_B(c):
            TP, Lb = c["TP"], c["Lb"]
            R = 512.0
            NIT = cfg.NIT
            m, lo = smM[:TP, 0:1], smM[:TP, 1:2]
            cnt, dd, cn2 = smC[:TP, 0:1], smD[:TP, 0:1], smD[:TP, 1:2]
            accA = smA[:TP, 0:1]
            thr = cfg.TOPK - 0.5
            import os as _os
            Ld = Lb if Lb < int(_os.environ.get("SPLIT_MIN", "1536")) else ((int(0.60 * Lb) + 63) // 64) * 64
            nA = Lb - Ld
            nch = 0 if nA == 0 else (1 if nA < int(_os.environ.get("ACT_MIN", "1024")) else int(_os.environ.get("ACT_CH", "2")))
            bounds = []
            if nch:
                stepc = ((nA + nch - 1) // nch + 63) // 64 * 64
                a = Ld
                while a < Lb:
                    bounds.append((a, min(Lb, a + stepc)))
                    a += stepc
            c["ny"] = NIT * (len(bounds) + 1)
            k.memset("dve", m, 0.0)
            for i in range(NIT):
                k.ts("dve", junk[:TP, 0:Ld], Sc[:TP, 0:Ld], m, ALU.is_ge, None, ALU.add, accum=cnt)
                cuse, tuse = cnt, thr
                if nA > 0:
                    for ci_, (a, b) in enumerate(bounds):
                        k.act(junkA[:TP, 0:b - a], Sc[:TP, a:b], AF.Sign, bias=m, scale=-1.0, accum=smA[:TP, ci_:ci_ + 1])
                        yield ("c", i)
                    prev = cnt
                    for ci_ in range(len(bounds)):
                        k.stt(cn2, smA[:TP, ci_:ci_ + 1], -0.5, prev, ALU.mult, ALU.add)
                        prev = cn2
                    cuse, tuse = cn2, thr - 0.5 * nA
                if i < NIT - 1:
                    cn = R / (2.0 ** (i + 1))
                    k.ts("dve", dd, cuse, tuse, ALU.is_ge, 2.0 * cn, ALU.mult)
                    k.stt(m, dd, -cn, m, ALU.add, ALU.add)
                else:
                    ci = R / (2.0 ** i)
                    k.ts("dve", dd, cuse, tuse, ALU.is_lt, -ci, ALU.mult)
                    k.tt("dve", lo, m, dd, ALU.add)
                yield ("i", i)

        def attn_Bfinal(c):
            TP, Lb = c["TP"], c["Lb"]
            lo = smM[:TP, 1:2]
            k.ts("dve", Mb[:TP, 0:Lb], Sc[:TP, 0:Lb], lo, ALU.is_lt, MASKNEG, ALU.mult)
            if TP < 128:
                k.memset("dve", Mb[TP:128, 0:Lb], 0.0)

        def attn_Cmain(c):
            TP, Lb, qTc = c["TP"], c["Lb"], c["qT"]
            I4 = I4p if TP == 128 else I4s
            nkt = (Lb + 127) // 128
            po = [psf[4], psf[5]]
            rot["set"] = [0, 1, 2, 3]

            def qk_pair(kt):
                kw = min(128, Lb - 128 * kt)
                pls = [rbank(), rbank()]
                for g in range(2):
                    k.mm(pls[g][:, 0:4 * TP], kT[g * 64:(g + 1) * 64, kt * 128:kt * 128 + 128],
                         qTc[g * 64:(g + 1) * 64, 0:4 * TP], start=True, stop=True)
                for g in range(2):
                    k.mm(pls[g][:kw, 0:4 * TP], Mb[:, kt * 128:kt * 128 + kw], I4[:, 0:4 * TP], start=False, stop=True,
                         skip_group_check=True)
                return pls

            cur = qk_pair(0)
            for kt in range(nkt):
                kw = min(128, Lb - 128 * kt)
                nxt = qk_pair(kt + 1) if kt + 1 < nkt else None
                ps = []
                for g in range(2):
                    p_ = pT[(2 * kt + g) % 3]
                    k.act(p_[:kw, 0:4 * TP], cur[g][:kw, 0:4 * TP], AF.Exp)
                    ps.append(p_)
                yield kt
                for g in range(2):
                    gs = slice(g * 64, (g + 1) * 64)
                    k.mm(po[0][gs, 0:4 * TP], Vb[:kw, kt, gs], ps[g][:kw, 0:4 * TP], start=(kt == 0), stop=(kt == nkt - 1))
                for g in range(2):
                    gs = slice(g * 64, (g + 1) * 64)
                    k.mm(po[1][gs, 0:4 * TP], ones_bf[:kw, 0:64], ps[g][:kw, 0:4 * TP], start=(kt == 0), stop=(kt == nkt - 1))
                yield kt
                cur = nxt

        def attn_Ctail(c):
            TP, t = c["TP"], c["t"]
            po = [psf[4], psf[5]]
            rd = rr[0]
            k.op("dve", lambda gg: gg.reciprocal(rd.t[:, 0:4 * TP], po[1].t[:, 0:4 * TP]), r=[po[1]], w=[rd])
            k.tt("dve", yaT[:, :, t * TP:(t + 1) * TP], po[0][:, 0:4 * TP].re("p (c q) -> p c q", c=4),
                 rd[:, 0:4 * TP].re("p (c q) -> p c q", c=4), ALU.mult)

        def hgrn(l, TP, NT):
            TT = TP * NT
            nch = TP // 64
            rot["set"] = [0, 1]
            for c, nm in ((0, "hig0"), (1, "hig1")):
                w_ = loadw(l, nm, "tm")
                for t in range(NT):
                    pb = rbank()
                    for kc in range(8):
                        k.mm(pb[:TP, :], xT[:, kc, t * TP:(t + 1) * TP], w_[:, kc * 512:(kc + 1) * 512],
                             start=(kc == 0), stop=(kc == 7), inc=(kc == 7))
                    if c == 0:
                        k.copy("act", vb[:TP, t * 512:(t + 1) * 512], pb[:TP, :])
                    else:
                        k.act(sg[:TP, t * 512:(t + 1) * 512], pb[:TP, :], AF.Silu)
            oacc = [psf[2 + t] for t in range(NT)]
            h0, h1, h2, h3 = hs_
            if nch == 2:
                k.memset("pool", QA[:, 0:TT].re("p (t u s) -> p t u s", u=2, s=64)[:, :, 1, :], 0.0)
                k.memset("pool", QB[:, 0:TT].re("p (t u s) -> p t u s", u=2, s=64)[:, :, 0, :], 0.0)
            for h in range(4):
                wq_ = loadw(l, "hqf%d" % h, "fm")
                wf_ = loadw(l, "hqf%d" % (4 + h), "fm")
                zq = rbank()
                for kc in range(8):
                    k.mm(zq[:, 0:TT], wq_[:, kc * 128:(kc + 1) * 128], xT[:, kc, 0:TT], start=(kc == 0), stop=(kc == 7), inc=(kc == 7))
                zf = rbank()
                for kc in range(8):
                    k.mm(zf[:, 0:TT], wf_[:, kc * 128:(kc + 1) * 128], xT[:, kc, 0:TT], start=(kc == 0), stop=(kc == 7), inc=(kc == 7))
                lbc = lb[:, l * 4 + h:l * 4 + h + 1]
                k.act(h0[:, 0:TT], zf[:, 0:TT], AF.Exp, scale=-1.0)
                k.act(h1[:, 0:TT], h0[:, 0:TT], AF.Ln, bias=cst[:, 0:1], scale=1.0)
                k.act(h0[:, 0:TT], h0[:, 0:TT], AF.Ln, bias=cst[:, 0:1], scale=lbc)
                k.tt("dve", h0[:, 0:TT], h0[:, 0:TT], h1[:, 0:TT], ALU.subtract)
                k.act(h2[:, 0:TT], zf[:, 0:TT], AF.Sigmoid, scale=-1.0)
                k.ts("dve", h2[:, 0:TT], h2[:, 0:TT], oml[:, l * 4 + h:l * 4 + h + 1], ALU.mult)
                k.act(h3[:, 0:TT], zq[:, 0:TT], AF.Silu)
                for c in range(TT // 64):
                    k.op("dve", lambda g, c=c: g.tensor_tensor_scan(h1.ap[:, c * 64:(c + 1) * 64], ones.t[:, 0:64],
                                                                   h0.ap[:, c * 64:(c + 1) * 64], 0.0, ALU.mult, ALU.add),
                         r=[ones, h0], w=[h1])
                k.act(h0[:, 0:TT], h1[:, 0:TT], AF.Exp)
                k.act(h1[:, 0:TT], h1[:, 0:TT], AF.Exp, scale=-1.0)
                if nch == 2:
                    qv = h3[:, 0:TT].re("p (t u s) -> p t u s", u=2, s=64)
                    ev = h0[:, 0:TT].re("p (t u s) -> p t u s", u=2, s=64)
                    k.tt("dve", QA[:, 0:TT].re("p (t u s) -> p t u s", u=2, s=64)[:, :, 0, :], qv[:, :, 0, :], ev[:, :, 0, :], ALU.mult)
                    k.tt("dve", QB[:, 0:TT].re("p (t u s) -> p t u s", u=2, s=64)[:, :, 1, :], qv[:, :, 1, :], ev[:, :, 1, :], ALU.mult)
                else:
                    k.tt("dve", QA[:, 0:TT], h3[:, 0:TT], h0[:, 0:TT], ALU.mult)
                k.tt("dve", KT[:, 0:TT], h2[:, 0:TT], h1[:, 0:TT], ALU.mult)
                k.copy("pool", Sbf[:, :], Sst[:, h, :])
                for t in range(NT):
                    sl = slice(t * TP, (t + 1) * TP)
                    sc = rbank()
                    k.mm(sc[:TP, 0:64], KT[:, sl], QA[:, t * TP:t * TP + 64])
                    if nch == 2:
                        k.mm(sc[:TP, 64:128], KT[:, sl], QB[:, t * TP + 64:t * TP + 128])
                    k.tt("dve", scm[:TP, 0:TP], sc[:TP, 0:TP], bmask[:TP, 0:TP], ALU.mult)
                    k.tr(psb[0][:TP, 0:128], KT[:, sl], ident[:, :])
                    k.copy("act", Ktok[:TP, :], psb[0][:TP, 0:128])
                    ob = oacc[t][:TP, h * 128:(h + 1) * 128]
                    first = (h == 0)
                    for u in range(nch):
                        us = slice(u * 64, (u + 1) * 64)
                        vsl = vb[us, t * 512 + h * 128:t * 512 + (h + 1) * 128]
                        k.mm(ob, scm[us, 0:TP], vsl, start=(first and u == 0), stop=False, skip_group_check=True)
                        qsrc = QA if u == 0 else QB
                        k.mm(ob, qsrc[:, sl], Sbf[:, :], start=False, stop=(u == nch - 1), skip_group_check=True)
                        kv = rbank()
                        k.mm(kv[:, 0:128], Ktok[us, :], vsl)
                        k.tt("dve", Sst[:, h, :], kv[:, 0:128], Sst[:, h, :], ALU.add)
                        ecol = t * TP + u * 64 + 63
                        k.act(Sst[:, h, :], Sst[:, h, :], AF.Identity, scale=h0[:, ecol:ecol + 1])
                        k.copy("pool", Sbf[:, :], Sst[:, h, :])
            for t in range(NT):
                ob = oacc[t]
                for h in range(4):
                    k.act(on[:TP, 0:128], ob[:TP, h * 128:(h + 1) * 128], AF.Square, accum=sm[:TP, 16 + h:17 + h])
                k.act(sm[:TP, 20:24], sm[:TP, 16:20], AF.Sqrt, bias=cst[:TP, 1:2], scale=1.0 / 128.0)
                k.op("dve", lambda g: g.reciprocal(sm.t[:TP, 24:28], sm.t[:TP, 20:24]), r=[sm], w=[sm])
                for h in range(4):
                    k.ts("dve", on[:TP, h * 128:(h + 1) * 128], ob[:TP, h * 128:(h + 1) * 128], sm[:TP, 24 + h:25 + h], ALU.mult)
                k.tt("dve", on[:TP, :], on[:TP, :], gB[:TP, :], ALU.mult)
                k.tt("dve", ybb[:TP, :], on[:TP, :], sg[:TP, t * 512:(t + 1) * 512], ALU.mult)
                for c in range(4):
                    k.tr(psb[1][:, c * 128:c * 128 + TP], ybb[:TP, c * 128:(c + 1) * 128], ident[:TP, :TP])
                k.copy("act", ybT[:, :, t * TP:(t + 1) * TP], psb[1][:, 0:512].re("p (c t) -> p c t", c=4)[:, :, 0:TP])

        def sconv(l, TT):
            rot["set"] = [0, 1, 2, 3]
            h0, h1, h2, h3 = hs_
            for j in range(4):
                wb_ = loadw(l, "c%d" % j, "fm"); wc_ = loadw(l, "c%d" % (4 + j), "fm"); wx_ = loadw(l, "c%d" % (8 + j), "fm")
                pbk, pck, pxk = rbank(), rbank(), rbank()
                for (pp, ww) in ((pbk, wb_), (pck, wc_), (pxk, wx_)):
                    for kc in range(8):
                        k.mm(pp[:, 0:TT], ww[:, kc * 128:(kc + 1) * 128], xT[:, kc, 0:TT], start=(kc == 0), stop=(kc == 7), inc=(kc == 7))
                uext = V(AR1.t[:, 0:1024], h0.buf)
                k.copy("pool", uext[:, 0:2], uh[:, 2 * j:2 * j + 2])
                k.copy("act", h2[:, 0:TT], pxk[:, 0:TT])
                k.op("dve", lambda g: g.tensor_tensor(uext.ap[:, 2:2 + TT], pck.t[:, 0:TT], h2.ap[:, 0:TT], ALU.mult),
                     r=[pck, h2], w=[h0, h1])
                k.op("pool", lambda g, j=j: g.tensor_copy(uh.t[:, 2 * j:2 * j + 2], uext.ap[:, TT:TT + 2]), r=[h0, h1], w=[uh])
                k.op("dve", lambda g, j=j: g.tensor_scalar(h3.ap[:, 0:TT], uext.ap[:, 2:2 + TT], scw.t[:, 4 * j + 2:4 * j + 3],
                                                          scw.t[:, 4 * j + 3:4 * j + 4], ALU.mult, ALU.add),
                     r=[h0, h1, scw], w=[h3])
                k.op("dve", lambda g, j=j: g.scalar_tensor_tensor(h3.ap[:, 0:TT], uext.ap[:, 1:1 + TT], scw.t[:, 4 * j + 1:4 * j + 2],
                                                                 h3.ap[:, 0:TT], ALU.mult, ALU.add),
                     r=[h0, h1, scw, h3], w=[h3])
                k.op("dve", lambda g, j=j: g.scalar_tensor_tensor(h3.ap[:, 0:TT], uext.ap[:, 0:TT], scw.t[:, 4 * j:4 * j + 1],
                                                                 h3.ap[:, 0:TT], ALU.mult, ALU.add),
                     r=[h0, h1, scw, h3], w=[h3])
                k.tt("dve", ycT[:, j, 0:TT], pbk[:, 0:TT], h3[:, 0:TT], ALU.mult)

        def merge(l, TP, NT):
            TT = TP * NT
            rot["set"] = [0, 1, 2, 3]
            ysrc = [yaT, ybT, ycT]
            for cc in range(8):
                for b in range(3):
                    wg_ = loadw(l, "g%d" % (b * 8 + cc), "fm")
                    wb_ = loadw(l, "br%d_%d" % (b, cc), "f4")
                    gp = rbank()
                    for kc in range(8):
                        k.mm(gp[:, 0:TT], wg_[:, kc * 128:(kc + 1) * 128], xT[:, kc, 0:TT], start=(kc == 0), stop=(kc == 7), inc=(kc == 7))
                    bp = rbank()
                    for kc in range(4):
                        k.mm(bp[:, 0:TT], wb_[:, kc * 128:(kc + 1) * 128], ysrc[b][:, kc, 0:TT], start=(kc == 0), stop=(kc == 3), inc=(kc == 3))
                    k.act(sgm[:, 0:TT], gp[:, 0:TT], AF.Sigmoid)
                    if b == 0:
                        k.tt("dve", mg[:, 0:TT], sgm[:, 0:TT], bp[:, 0:TT], ALU.mult)
                    elif b == 1:
                        k.tt("dve", tmpm[:, 0:TT], sgm[:, 0:TT], bp[:, 0:TT], ALU.mult)
                        k.tt("pool", mg[:, 0:TT], mg[:, 0:TT], tmpm[:, 0:TT], ALU.add)
                    else:
                        k.tt("dve", tmpm[:, 0:TT], sgm[:, 0:TT], bp[:, 0:TT], ALU.mult)
                        k.tt("pool", gT[:, cc * 512:cc * 512 + TT], mg[:, 0:TT], tmpm[:, 0:TT], ALU.add)
            for c in range(2):
                wo_ = loadw(l, "wout%d" % c, "tm")
                for t in range(NT):
                    pb = rbank()
                    for kc in range(8):
                        k.mm(pb[:TP, :], gT[:, kc * 512 + t * TP:kc * 512 + (t + 1) * TP], wo_[:, kc * 512:(kc + 1) * 512],
                             start=(kc == 0), stop=(kc == 7), inc=(kc == 7))
                    xv = xs_v[t][:TP, c * 512:(c + 1) * 512]
                    k.stt(xv, xv, ALPHA, pb[:TP, :], ALU.mult, ALU.add)
            for t in range(NT):
                layer_norm(TP, t, 0)

        def ffn(l, TP, NT):
            TT = TP * NT
            rot["set"] = [0, 1]
            for j in range(22):
                for wi, chn_ in enumerate((j, 22 + j)):
                    w_ = loadw(l, "up%d" % chn_, "fm")
                    hp = rbank()
                    for kc in range(8):
                        k.mm(hp[:, 0:TT], w_[:, kc * 128:(kc + 1) * 128], xT[:, kc, 0:TT], start=(kc == 0), stop=(kc == 7), inc=(kc == 7))
                    u_ = ue[wi]
                    a_ = fac[wi]
                    k.copy("pool", u_[:, 0:2], fh[:, 2 * chn_:2 * chn_ + 2])
                    k.copy("act", u_[:, 2:2 + TT], hp[:, 0:TT])
                    k.copy("pool", fh[:, 2 * chn_:2 * chn_ + 2], u_[:, TT:TT + 2])
                    k.act(a_[:, 0:TT], hp[:, 0:TT], AF.Identity, bias=fcw[:, 4 * chn_ + 3:4 * chn_ + 4],
                          scale=fcw[:, 4 * chn_ + 2:4 * chn_ + 3])
                    k.stt(a_[:, 0:TT], u_[:, 1:1 + TT], fcw[:, 4 * chn_ + 1:4 * chn_ + 2], a_[:, 0:TT], ALU.mult, ALU.add)
                    k.stt(a_[:, 0:TT], u_[:, 0:TT], fcw[:, 4 * chn_:4 * chn_ + 1], a_[:, 0:TT], ALU.mult, ALU.add)
                k.act(sa[:, 0:TT], fac[0][:, 0:TT], AF.Silu)
                k.tt("dve", gT[:, j * 512:j * 512 + TT], sa[:, 0:TT], fac[1][:, 0:TT], ALU.mult)
            acc = [psf[2 + t] for t in range(NT)]
            for c in range(2):
                kc0 = 0
                for kg, nk in enumerate((8, 8, 6)):
                    wd_ = loadw(l, "wd%d_%d" % (c, kg), "tm")
                    for t in range(NT):
                        for kk in range(nk):
                            kc = kc0 + kk
                            k.mm(acc[t][:TP, :], gT[:, kc * 512 + t * TP:kc * 512 + (t + 1) * TP], wd_[:, kk * 512:(kk + 1) * 512],
                                 start=(kc == 0), stop=(kc == 21), inc=(kk == nk - 1))
                    kc0 += nk
                for t in range(NT):
                    xv = xs_v[t][:TP, c * 512:(c + 1) * 512]
                    k.stt(xv, xv, ALPHA, acc[t][:TP, :], ALU.mult, ALU.add)
            for t in range(NT):
                layer_norm(TP, t, 1)

        def run_group(l, grp):
            if grp == "p":
                TP, NT, nmac, Lbase = 128, 4, S // 512, 0
                xsrc = xp if l == 0 else xmid_p
                xdst = xmid_p if l == 0 else yp
                okd, ovd, okid, ohd, oscd, offd, ropd = okp, ovp, okip, ohp, oscp, offp, ropep
                xmb = xmbuf_p
            else:
                TP, NT, nmac, Lbase = 64, 1, 1, P
                xsrc = xs if l == 0 else xmid_s
                xdst = xmid_s if l == 0 else ys
                okd, ovd, okid, ohd, oscd, offd, ropd = oks, ovs, okis, ohs, oscs, offs, ropes
                xmb = xmbuf_s
            TT = TP * NT
            k.dma("sp", ch("gB"), gB[:, :], gB_d[l])
            k.dma("sp", ch("scw"), scw[:, :], scw_d[l])
            k.dma("sp", ch("fcw"), fcw[:, :], fcw_d[l])
            k.dma("sp", ch("lnp"), lnp[:, :], lnp_d[l])
            if grp == "p":
                k.memset("pool", Sst[:, :, :], 0.0)
                k.memset("pool", uh[:, :], 0.0)
                k.memset("pool", fh[:, :], 0.0)
            else:
                k.dma("sp", ch("Sst"), Sst[:, :, :], sh[l].rearrange("h d v -> d h v"))
                k.dma("sp", ch("uh"), uh[:, :], ssc[l])
                k.dma("sp", ch("fh"), fh[:, :], sff[l])
                claim(useA, useB + useC)
                for kt in range(P // 128):
                    rows = slice(kt * 128, (kt + 1) * 128)
                    k.dma("sp", ch("cst0"), att_f[:, 0:128], ck[l, rows, :])
                    k.dma("sp", ch("cst1"), att_f[:, 128:256], cv[l, rows, :])
                    k.dma("sp", ch("cst2"), att_f[:, 256:320], cki[l, rows, :])
                    k.copy("pool", att_b[:, 0:128], att_f[:, 0:128])
                    k.copy("pool", att_b[:, 128:192], att_f[:, 256:320])
                    k.copy("pool", att_b[:, 192:256], att_f[:, 256:320])
                    k.copy("pool", Vb[:, kt, :], att_f[:, 128:256])
                    k.tr(psb[0][:, 0:128], att_b[:, 0:128], ident[:, :])
                    k.tr(psb[0][:, 128:256], att_b[:, 128:256], ident[:, :])
                    k.copy("dve", kT[:, kt * 128:(kt + 1) * 128], psb[0][:, 0:128])
                    Bn = (kt * 128) // 512
                    hf = Bn % 2
                    c0 = (Bn // 2) * 512 + (kt * 128) % 512
                    k.copy("act", kiTp[hf * 64:hf * 64 + 64, c0:c0 + 128], psb[0][hf * 64:hf * 64 + 64, 128:256])
            import os
            SL = int(os.environ.get("SL", "9"))
            for mt in range(nmac):
                if grp == "s" and SL < 1:
                    break
                tok0 = mt * TT
                claim(useA, useB + useC)
                for t in range(NT):
                    k.dma("sp", ch("x_sb%d" % t), xs_v[t][:TP, :], xsrc[tok0 + t * TP:tok0 + (t + 1) * TP, :],
                          r=[xmb[mt][t]] if l == 1 else [])
                make_xT(TP, NT)
                if cfg.stop >= 2 and not (grp == "s" and SL < 2):
                    prev = None
                    for t in range(NT):
                        row0 = tok0 + t * TP
                        cur = attn_A(l, t, TP, Lbase + row0, grp == "p", ropd[row0:row0 + TP, :], okd, ovd, okid, row0)
                        gb = attn_B(cur)
                        if prev is not None:
                            gc = attn_Cmain(prev)
                            nC = 2 * ((prev["Lb"] + 127) // 128)
                            done_c = 0
                            yi = 0
                            for _ in gb:
                                yi += 1
                                tgt = min(nC, (yi * nC) // max(1, cur.get("ny", cfg.NIT)))
                                while done_c < tgt:
                                    next(gc, None)
                                    done_c += 1
                            for _ in gc:
                                pass
                            attn_Bfinal(cur)
                            attn_Ctail(prev)
                        else:
                            for _ in gb:
                                pass
                            attn_Bfinal(cur)
                        prev = cur
                    for _ in attn_Cmain(prev):
                        pass
                    attn_Ctail(prev)
                claim(useB, useA)
                if cfg.stop >= 3:
                    hgrn(l, TP, NT)
                if cfg.stop >= 4:
                    sconv(l, TT)
                claim(useC, useA + useB)
                if cfg.stop >= 5:
                    merge(l, TP, NT)
                    make_xT(TP, NT)
                if cfg.stop >= 6:
                    ffn(l, TP, NT)
                for t in range(NT):
                    k.dma("pool", ch("st_x%d" % t), xdst[tok0 + t * TP:tok0 + (t + 1) * TP, :], xs_v[t][:TP, :],
                          w=[xmb[mt][t]] if l == 0 else [])
            k.dma("pool", ch("st_S"), ohd[l].rearrange("h d v -> d h v"), Sst[:, :, :])
            k.dma("pool", ch("st_uh"), oscd[l], uh[:, :])
            k.dma("pool", ch("st_fh"), offd[l], fh[:, :])

        xmbuf_p = [[Buf("xmp%d_%d" % (i, t)) for t in range(4)] for i in range(S // 512)]
        xmbuf_s = [[Buf("xms")]]
        for grp in ("p", "s"):
            for l in range(2):
                import os as _os
                if _os.environ.get("ONLYS") and not (grp == "s" and l == 0):
                    continue
                if cfg.stop >= 9 or (cfg.stop >= 1 and grp == "p" and l == 0) or (cfg.stop >= 7 and grp == "p") or (cfg.stop >= 8 and l == 0):
                    run_group(l, grp)
        k.wait_all("pool")
        k.wait_all("sp")
        print("instructions:", k.ninst, "channels:", k.nchan, "sbuf_left:", nc.sbuf_bytes_remaining)
    return nc


def kernel_cfg(cfg, inputs, n_cores=8):
    f32 = np.float32
    g = lambda n: np.asarray(inputs[n], dtype=f32)
    x_prompt, x_sample = g("x_prompt"), g("x_sample")
    S, DS, P = cfg.S, cfg.DS, cfg.P
    wfl = host_weights(g("w_in"), g("w_branch"), g("w_out"), g("w_up"), g("w_down"))
    lbl = np.ascontiguousarray(g("hgrn_lb_logits").reshape(2, 4, 128).transpose(2, 0, 1).reshape(128, 8))
    gBv = np.ascontiguousarray(np.broadcast_to(np.tile(g("hgrn_norm_g"), (1, 4))[:, None, :], (2, 128, 512)))
    scw = np.concatenate([g("sconv_w"), g("sconv_b")[:, None, :]], axis=1)
    scw = np.ascontiguousarray(scw.reshape(2, 4, 4, 128).transpose(0, 3, 2, 1).reshape(2, 128, 16))
    fcw = np.concatenate([g("ffn_conv_w"), g("ffn_conv_b")[:, None, :]], axis=1)
    fcw = np.ascontiguousarray(fcw.reshape(2, 4, 44, 128).transpose(0, 3, 2, 1).reshape(2, 128, 176))
    lnp = np.stack([g("ln1_g"), g("ln1_b"), g("ln2_g"), g("ln2_b")], axis=1).reshape(2, 1, 4096)
    lnp = np.ascontiguousarray(np.broadcast_to(lnp, (2, 128, 4096)))
    ropep = rope_table(np.arange(S))
    ropes = rope_table(P + np.arange(DS))
    ck = g("cache_attn_k").reshape(2, -1, P, 128)
    cv = g("cache_attn_v").reshape(2, -1, P, 128)
    cki = g("cache_idx_k")
    sh = g("state_hgrn")
    ssc = g("state_sconv")
    sff = g("state_ffn_conv")
    nb = x_prompt.shape[0]
    in_maps = []
    for c in range(n_cores):
        pb = (c // 2) % nb
        sb = c % x_sample.shape[0]
        ssc_t = np.ascontiguousarray(ssc[:, sb].reshape(2, 2, 4, 128).transpose(0, 3, 2, 1).reshape(2, 128, 8))
        sff_t = np.ascontiguousarray(sff[:, sb].reshape(2, 2, 44, 128).transpose(0, 3, 2, 1).reshape(2, 128, 88))
        in_maps.append({
            "xp": np.ascontiguousarray(x_prompt[pb]), "xs": np.ascontiguousarray(x_sample[sb]),
            "ck": np.ascontiguousarray(ck[:, sb]), "cv": np.ascontiguousarray(cv[:, sb]),
            "cki": np.ascontiguousarray(cki[:, sb]), "sh": np.ascontiguousarray(sh[:, sb]),
            "ssc": ssc_t, "sff": sff_t, "wfl": wfl, "lbl": lbl, "gB": gBv, "scw": scw, "fcw": fcw, "lnp": lnp,
            "ropep": ropep, "ropes": ropes,
        })
    nc = build(cfg)
    res = run_bass_kernel_spmd(nc, in_maps, core_ids=list(range(n_cores)))
    R = res.results
    NB, NSB = x_prompt.shape[0], x_sample.shape[0]
    pc = [2 * b for b in range(NB)]

    def st_t(a, nchk):
        return a.reshape(2, 128, nchk, 2).transpose(0, 3, 2, 1).reshape(2, 2, nchk * 128)

    y_p = np.stack([R[c]["yp"] for c in pc], 0)
    y_s = np.stack([R[c]["ys"] for c in range(NSB)], 0)
    k_p = np.stack([R[c]["okp"] for c in pc], 1).reshape(2, NB, S, 2, 64)
    v_p = np.stack([R[c]["ovp"] for c in pc], 1).reshape(2, NB, S, 2, 64)
    ki_p = np.stack([R[c]["okip"] for c in pc], 1)
    h_p = np.stack([R[c]["ohp"] for c in pc], 1)
    sc_p = np.stack([st_t(R[c]["oscp"], 4) for c in pc], 1)
    ff_p = np.stack([st_t(R[c]["offp"], 44) for c in pc], 1)
    k_s = np.stack([R[c]["oks"] for c in range(NSB)], 1).reshape(2, NSB, DS, 2, 64)
    v_s = np.stack([R[c]["ovs"] for c in range(NSB)], 1).reshape(2, NSB, DS, 2, 64)
    ki_s = np.stack([R[c]["okis"] for c in range(NSB)], 1)
    h_s = np.stack([R[c]["ohs"] for c in range(NSB)], 1)
    sc_s = np.stack([st_t(R[c]["oscs"], 4) for c in range(NSB)], 1)
    ff_s = np.stack([st_t(R[c]["offs"], 44) for c in range(NSB)], 1)
    outs = (y_p, y_s, k_p, v_p, ki_p, h_p, sc_p, ff_p, k_s, v_s, ki_s, h_s, sc_s, ff_s)
    return tuple(np.ascontiguousarray(o, dtype=np.float32) for o in outs)


def kernel(**inputs):
    return kernel_cfg(Cfg(), inputs)
```

```python
import numpy as np
import concourse.bass as bass
import concourse.mybir as mybir
from concourse.bass_utils import run_bass_kernel_spmd
from contextlib import ExitStack

F32 = mybir.dt.float32
BF16 = mybir.dt.bfloat16
AF = mybir.ActivationFunctionType
ALU = mybir.AluOpType


class Buf:
    __slots__ = ("name", "w", "r", "excl")

    def __init__(self, name):
        self.name = name
        self.w = None
        self.r = {}
        self.excl = False


class V:
    __slots__ = ("ap", "buf")

    def __init__(self, ap, buf):
        self.ap = ap
        self.buf = buf

    def __getitem__(self, idx):
        return V(self.ap[idx], self.buf)

    def re(self, pat, **kw):
        return V(self.ap.rearrange(pat, **kw), self.buf)


class T:
    def __init__(self, k, name, shape, dtype, psum=False, buf=None):
        if psum:
            self.t = k.es.enter_context(k.nc.psum_tensor("t_" + name, shape, dtype))
        else:
            self.t = k.es.enter_context(k.nc.sbuf_tensor("t_" + name, shape, dtype))
        self.buf = buf if buf is not None else Buf(name)
        self.buf.excl = bool(psum)

    def __getitem__(self, idx):
        return V(self.t[idx], self.buf)


class K:
    def __init__(self, nc, es):
        self.nc = nc
        self.es = es
        self.eng = {"pe": nc.tensor, "act": nc.scalar, "dve": nc.vector, "pool": nc.gpsimd, "sp": nc.sync}
        self.sem = {}
        self.cnt = {}
        self.seen = {e: {} for e in self.eng}
        self.ekey = {}
        self.eep = {}
        for e in ("pe", "act", "dve", "pool"):
            self.sem[e] = es.enter_context(nc.semaphore("s_" + e))
            self.cnt[e] = 0
            self.ekey[e] = e
            self.eep[e] = 0
        self.nchan = 0
        self.ninst = 0

    def chan(self, name):
        key = "ch_%d_%s" % (self.nchan, name)
        self.nchan += 1
        self.sem[key] = self.es.enter_context(self.nc.semaphore(key))
        self.cnt[key] = 0
        return key

    def _wait(self, e, deps):
        eng = self.eng[e]
        best = {}
        for key, val in deps:
            if e == "pe" and key.startswith("pe"):
                continue
            if best.get(key, 0) < val:
                best[key] = val
        for key, val in best.items():
            if self.seen[e].get(key, 0) < val:
                eng.wait_ge(self.sem[key], val)
                self.seen[e][key] = val

    def op(self, e, fn, r=(), w=(), chan=None, inc=True):
        rb = [x.buf if not isinstance(x, Buf) else x for x in r if x is not None]
        wb = [x.buf if not isinstance(x, Buf) else x for x in w if x is not None]
        wb = wb + [b for b in rb if b.excl]
        rb = [b for b in rb if not b.excl]
        deps = []
        for b in rb:
            if b.w is not None:
                deps.append(b.w)
        for b in wb:
            if b.w is not None:
                deps.append(b.w)
            deps.extend(b.r.items())
        self._wait(e, deps)
        inst = fn(self.eng[e])
        self.ninst += 1
        if chan is None:
            ek = self.ekey[e]
            if self.cnt[ek] >= 30000:
                self.eep[e] += 1
                ek = "%s_%d" % (e, self.eep[e])
                self.ekey[e] = ek
                self.sem[ek] = self.es.enter_context(self.nc.semaphore("s_" + ek))
                self.cnt[ek] = 0
            if inc:
                self.cnt[ek] += 1
                inst.then_inc(self.sem[ek], 1)
                tk = (ek, self.cnt[ek])
            else:
                tk = (ek, self.cnt[ek] + 1)
        else:
            self.cnt[chan] += 16
            inst.then_inc(self.sem[chan], 16)
            tk = (chan, self.cnt[chan])
        for b in rb:
            if b.r.get(tk[0], 0) < tk[1]:
                b.r[tk[0]] = tk[1]
        for b in wb:
            b.w = tk
            b.r = {}
        return inst

    def mm(self, o, lhsT, rhs, start=True, stop=True, extra_r=(), inc=True, **kw):
        return self.op("pe", lambda g: g.matmul(o.ap, lhsT.ap, rhs.ap, start=start, stop=stop, **kw),
                       r=[lhsT, rhs] + list(extra_r), w=[o], inc=inc)

    def tr(self, o, in_, ident):
        return self.op("pe", lambda g: g.transpose(o.ap, in_.ap, ident.ap), r=[in_, ident], w=[o])

    def act(self, o, in_, func, bias=None, scale=None, accum=None, e="act"):
        kw = {}
        rr = [in_]
        if bias is not None:
            if isinstance(bias, V):
                kw["bias"] = bias.ap
                rr.append(bias)
            else:
                kw["bias"] = bias
        if scale is not None:
            if isinstance(scale, V):
                kw["scale"] = scale.ap
                rr.append(scale)
            else:
                kw["scale"] = scale
        ww = [o]
        if accum is not None:
            kw["accum_out"] = accum.ap
            ww.append(accum)
        return self.op("act", lambda g: g.activation(o.ap, in_.ap, func, **kw), r=rr, w=ww)

    def tt(self, e, o, a, b, op):
        return self.op(e, lambda g: g.tensor_tensor(o.ap, a.ap, b.ap, op), r=[a, b], w=[o])

    def ts(self, e, o, a, s1, op0, s2=None, op1=None, accum=None):
        rr = [a]
        v1 = s1
        v2 = s2
        if isinstance(s1, V):
            rr.append(s1)
            v1 = s1.ap
        if isinstance(s2, V):
            rr.append(s2)
            v2 = s2.ap
        ww = [o]
        kw = {}
        if accum is not None:
            kw["accum_out"] = accum.ap
            ww.append(accum)
        if op1 is None:
            return self.op(e, lambda g: g.tensor_scalar(o.ap, a.ap, v1, None, op0, **kw), r=rr, w=ww)
        return self.op(e, lambda g: g.tensor_scalar(o.ap, a.ap, v1, v2, op0, op1, **kw), r=rr, w=ww)

    def stt(self, o, a, s, b, op0, op1):
        rr = [a, b]
        v = s
        if isinstance(s, V):
            rr.append(s)
            v = s.ap
        return self.op("dve", lambda g: g.scalar_tensor_tensor(o.ap, a.ap, v, b.ap, op0, op1), r=rr, w=[o])

    def copy(self, e, o, a):
        if e == "act":
            return self.op(e, lambda g: g.copy(o.ap, a.ap), r=[a], w=[o])
        return self.op(e, lambda g: g.tensor_copy(o.ap, a.ap), r=[a], w=[o])

    def memset(self, e, o, val):
        return self.op(e, lambda g: g.memset(o.ap, val), r=[], w=[o])

    def dma(self, e, chan, o, i, r=(), w=(), **kw):
        oa = o.ap if isinstance(o, V) else o
        ia = i.ap if isinstance(i, V) else i
        rr = list(r) + ([i] if isinstance(i, V) else [])
        ww = list(w) + ([o] if isinstance(o, V) else [])
        return self.op(e, lambda g: g.dma_start(out=oa, in_=ia, **kw), r=rr, w=ww, chan=chan)

    def wait_all(self, e):
        deps = [(key, c) for key, c in self.cnt.items() if c > 0]
        self._wait(e, deps)

D = 1024
DFF = 2816
ALPHA = 4.0 ** 0.25
LN_EPS = 1e-5
NEG = -1e30
MASKNEG = -30000.0
O_HQ, O_HF, O_HI, O_HG, O_CB, O_CC, O_CX, O_G = 1092, 1604, 2116, 2628, 3140, 3652, 4164, 4676


class Cfg:
    def __init__(self, S=8192, DS=64, P=2048, TOPK=256, NIT=28, stop=9):
        self.S, self.DS, self.P, self.TOPK, self.NIT = S, DS, P, TOPK, NIT
        self.stop = stop
        self.sub = 9
        self.LMAX = max(S, P + DS)
        self.LMAX = ((self.LMAX + 1023) // 1024) * 1024
        self.NKT = self.LMAX // 128


def _tile(W, c0, width, k0=0, nk=None):
    K = W.shape[0]
    if nk is None:
        nk = K // 128
    sub = W[k0 * 128:(k0 + nk) * 128, c0:c0 + width]
    return np.ascontiguousarray(sub.reshape(nk, 128, width).transpose(1, 0, 2).reshape(128, nk * width))


def weight_layout():
    items = []
    for j in range(8):
        items.append(("hqf%d" % j, 1024))
    for j in range(12):
        items.append(("c%d" % j, 1024))
    for j in range(24):
        items.append(("g%d" % j, 1024))
    for j in range(44):
        items.append(("up%d" % j, 1024))
    for b in range(3):
        for j in range(8):
            items.append(("br%d_%d" % (b, j), 512))
    items += [("att0", 4096), ("att1", 4096), ("att2", 8 * 68), ("hig0", 4096), ("hig1", 4096),
              ("wout0", 4096), ("wout1", 4096)]
    for c in range(2):
        for kg, nk in enumerate((8, 8, 6)):
            items.append(("wd%d_%d" % (c, kg), nk * 512))
    off = {}
    o = 0
    for n, w in items:
        off[n] = (o, w)
        o += w
    return items, off, o


def host_weights(w_in, w_branch, w_out, w_up, w_down):
    items, off, tot = weight_layout()
    out = np.empty((2, 128, tot), np.float32)
    for l in range(2):
        parts = []
        for j in range(8):
            parts.append(_tile(w_in[l], O_HQ + 128 * j, 128))
        for j in range(12):
            parts.append(_tile(w_in[l], O_CB + 128 * j, 128))
        for j in range(24):
            parts.append(_tile(w_in[l], O_G + 128 * j, 128))
        for j in range(44):
            parts.append(_tile(w_up[l], 128 * j, 128))
        perm = np.concatenate([np.r_[c * 64:(c + 1) * 64, (4 + c) * 64:(5 + c) * 64] for c in range(4)])
        for b in range(3):
            Wb = w_branch[l, b][perm] if b == 0 else w_branch[l, b]
            for j in range(8):
                parts.append(_tile(Wb, 128 * j, 128))
        parts += [_tile(w_in[l], 0, 512), _tile(w_in[l], 512, 512), _tile(w_in[l], 1024, 68),
                  _tile(w_in[l], O_HI, 512), _tile(w_in[l], O_HG, 512),
                  _tile(w_out[l], 0, 512), _tile(w_out[l], 512, 512)]
        for c in range(2):
            for k0, nk in ((0, 8), (8, 8), (16, 6)):
                parts.append(_tile(w_down[l], 512 * c, 512, k0, nk))
        out[l] = np.concatenate(parts, axis=1)
    return out


def rope_table(pos):
    half = 8
    inv = (500000.0 ** (-np.arange(half, dtype=np.float32) / np.float32(half))).astype(np.float32)
    ang = pos.astype(np.float32)[:, None] * inv[None, :]
    cos = np.cos(ang).astype(np.float32)
    sin = np.sin(ang).astype(np.float32)
    C16 = np.concatenate([cos, cos], axis=1)
    S16 = np.concatenate([-sin, sin], axis=1)
    tab = np.stack([np.tile(C16[:, None, :], (1, 8, 1)), np.tile(S16[:, None, :], (1, 8, 1))], axis=1)
    return np.ascontiguousarray(tab.reshape(len(pos), 256))


def build(cfg):
    S, DS, P = cfg.S, cfg.DS, cfg.P
    LMAX, NKT = cfg.LMAX, cfg.NKT
    nc = bass.Bass("TRN2", target_bir_lowering=False)
    _, woff, WTOT = weight_layout()

    def din(name, shape, dt=F32):
        return nc.dram_tensor(name, list(shape), dt, kind="ExternalInput").ap()

    def dout(name, shape, dt=F32):
        return nc.dram_tensor(name, list(shape), dt, kind="ExternalOutput").ap()

    xp = din("xp", [S, D]); xs = din("xs", [DS, D])
    ck = din("ck", [2, P, 128]); cv = din("cv", [2, P, 128]); cki = din("cki", [2, P, 64])
    sh = din("sh", [2, 4, 128, 128]); ssc = din("ssc", [2, 128, 8]); sff = din("sff", [2, 128, 88])
    wfl = din("wfl", [2, 128, WTOT])
    lbl = din("lbl", [128, 8]); gB_d = din("gB", [2, 128, 512])
    scw_d = din("scw", [2, 128, 16]); fcw_d = din("fcw", [2, 128, 176])
    lnp_d = din("lnp", [2, 128, 4096])
    ropep = din("ropep", [S, 256]); ropes = din("ropes", [DS, 256])

    yp = dout("yp", [S, D]); ys = dout("ys", [DS, D])
    okp = dout("okp", [2, S, 128]); ovp = dout("ovp", [2, S, 128]); okip = dout("okip", [2, S, 64])
    ohp = dout("ohp", [2, 4, 128, 128]); oscp = dout("oscp", [2, 128, 8]); offp = dout("offp", [2, 128, 88])
    oks = dout("oks", [2, DS, 128]); ovs = dout("ovs", [2, DS, 128]); okis = dout("okis", [2, DS, 64])
    ohs = dout("ohs", [2, 4, 128, 128]); oscs = dout("oscs", [2, 128, 8]); offs = dout("offs", [2, 128, 88])

    wbf = nc.dram_tensor("wbf", [2, 128, WTOT], BF16, kind="Internal").ap()
    xmid_p = nc.dram_tensor("xmid_p", [S, D], F32, kind="Internal").ap()
    xmid_s = nc.dram_tensor("xmid_s", [DS, D], F32, kind="Internal").ap()

    es = ExitStack()
    with es:
        k = K(nc, es)
        kT = T(k, "kT", [128, LMAX], BF16)
        kiTp = T(k, "kiTp", [128, LMAX // 2], BF16)
        Vb = T(k, "Vb", [128, NKT, 128], BF16)
        ones_bf = T(k, "ones_bf", [128, 64], BF16)
        AR1 = T(k, "AR1", [128, 8192], F32)
        AR2 = T(k, "AR2", [128, 4096], F32)
        x_sb = T(k, "x_sb", [128, 4, D], F32)
        xs_v = [V(x_sb.t[:, t, :], Buf("xs%d" % t)) for t in range(4)]
        xT = T(k, "xT", [128, 8, 512], BF16)
        wfm = [T(k, "wfm%d" % i, [128, 1024], BF16) for i in range(4)]
        wf4 = [T(k, "wf4%d" % i, [128, 512], BF16) for i in range(2)]
        wtm = [T(k, "wtm%d" % i, [128, 4096], BF16) for i in range(2)]
        wsm = T(k, "wsm", [128, 544], BF16)
        yaT = T(k, "yaT", [128, 4, 512], BF16)
        ybT = T(k, "ybT", [128, 4, 512], BF16)
        ycT = T(k, "ycT", [128, 4, 512], BF16)
        att_f = T(k, "att_f", [128, 1092], F32)
        att_b = T(k, "att_b", [128, 1280], BF16)
        qTs = [T(k, "qT%d" % i, [128, 512], BF16) for i in range(2)]
        junk = T(k, "junk", [128, 4992], mybir.dt.uint8)
        junkA = T(k, "junkA", [128, 3328], mybir.dt.int8)
        smA = T(k, "smA", [128, 4], F32)
        smM = T(k, "smM", [128, 4], F32)
        smC = T(k, "smC", [128, 4], F32)
        smD = T(k, "smD", [128, 4], F32)
        qiT = T(k, "qiT", [128, 512], BF16)
        rr = [T(k, "rr%d" % i, [128, 512], F32) for i in range(2)]
        pT = [T(k, "pT%d" % i, [128, 512], BF16) for i in range(3)]
        rope = T(k, "rope", [128, 256], F32)
        ra = T(k, "ra", [128, 128], F32)
        rb = T(k, "rb", [128, 128], F32)
        xbfs = [T(k, "xbf%d" % i, [128, D], BF16) for i in range(2)]
        sm = T(k, "sm", [128, 64], F32)
        bst = T(k, "bst", [128, 16], F32)
        Sst = T(k, "Sst", [128, 4, 128], F32)
        Sbf = T(k, "Sbf", [128, 128], BF16)
        scm = T(k, "scm", [128, 128], BF16)
        Ktok = T(k, "Ktok", [128, 128], BF16)
        uh = T(k, "uh", [128, 8], F32)
        fh = T(k, "fh", [128, 88], F32)
        lb = T(k, "lb", [128, 8], F32)
        oml = T(k, "oml", [128, 8], F32)
        lbraw = T(k, "lbraw", [128, 8], F32)
        gB = T(k, "gB", [128, 512], F32)
        scw = T(k, "scw", [128, 16], F32)
        fcw = T(k, "fcw", [128, 176], F32)
        lnp = T(k, "lnp", [128, 4096], F32)
        ident = T(k, "ident", [128, 128], BF16)
        I4p = T(k, "I4p", [128, 512], BF16)
        I4s = T(k, "I4s", [128, 256], BF16)
        bmask = T(k, "bmask", [128, 128], F32)
        ones = T(k, "ones", [128, 64], F32)
        cst = T(k, "cst", [128, 8], F32)
        psf = [T(k, "psf%d" % i, [128, 512], F32, psum=True) for i in range(6)]
        psb = [T(k, "psb%d" % i, [128, 1024], BF16, psum=True) for i in range(2)]

        def aview(ar, off, n, dt, name):
            ap = ar.t[:, off:off + n]
            if dt == BF16:
                ap = ap.bitcast(BF16)
            return V(ap, Buf(name))

        Sc = V(AR1.t[:, :], Buf("Sc"))
        Mb = V(AR2.t[:, :].bitcast(BF16), Buf("Mb"))
        hs_ = [aview(AR1, 512 * i, 512, F32, "h%d" % i) for i in range(4)]
        sg = aview(AR1, 2048, 2048, F32, "sg")
        vb = aview(AR1, 4096, 1024, BF16, "vb")
        QA = aview(AR1, 5120, 256, BF16, "QA")
        QB = aview(AR1, 5376, 256, BF16, "QB")
        KT = aview(AR1, 5632, 256, BF16, "KT")
        on = aview(AR1, 5888, 512, F32, "on")
        ybb = aview(AR1, 6400, 256, BF16, "ybb")
        gT = aview(AR1, 0, 5632, BF16, "gT")
        ue = [aview(AR1, 5632, 520, F32, "ue0"), aview(AR1, 6152, 520, F32, "ue1")]
        fac = [aview(AR1, 6672, 512, F32, "fac0"), aview(AR1, 7184, 512, F32, "fac1")]
        sa = aview(AR2, 0, 512, F32, "sa")
        mg = aview(AR2, 512, 512, F32, "mg")
        sgm = aview(AR2, 1024, 512, F32, "sgm")
        tmpm = aview(AR2, 1536, 512, F32, "tmpm")
        useA = [Sc, Mb]
        useB = hs_ + [sg, vb, QA, QB, KT, on, ybb]
        useC = [gT] + ue + fac + [sa, mg, sgm, tmpm]

        def claim(new, old):
            deps = {}
            for v in old:
                b = v.buf
                if b.w is not None:
                    deps[b.w[0]] = max(deps.get(b.w[0], 0), b.w[1])
                for kk, vv in b.r.items():
                    deps[kk] = max(deps.get(kk, 0), vv)
            for v in new:
                v.buf.w = None
                v.buf.r = dict(deps)

        chn = {}

        def ch(name):
            if name not in chn:
                chn[name] = k.chan(name)
            return chn[name]

        k.memset("pool", ident[:, :], 1.0)
        k.op("pool", lambda g: g.affine_select(ident.t[:, :], ident.t[:, :], [[1, 128]], ALU.is_equal, 0.0,
                                               base=0, channel_multiplier=-1), r=[ident], w=[ident])
        k.memset("pool", I4s[:, :], 0.0)
        for j in range(4):
            k.copy("pool", I4p[:, j * 128:(j + 1) * 128], ident[:, :])
            k.copy("pool", I4s[0:64, j * 64:(j + 1) * 64], ident[0:64, 0:64])
        k.memset("pool", bmask[:, :], 1.0)
        k.op("pool", lambda g: g.affine_select(bmask.t[:, :], bmask.t[:, :], [[1, 128]], ALU.is_ge, 0.0,
                                               base=0, channel_multiplier=-1), r=[bmask], w=[bmask])
        k.memset("pool", bmask[0:64, 64:128], 0.0)
        k.memset("pool", ones[:, :], 1.0)
        k.memset("pool", cst[:, 0:1], 1.0)
        k.memset("pool", cst[:, 1:2], LN_EPS)
        k.memset("dve", Vb[:, :, :], 0.0)
        k.memset("dve", ones_bf[:, :], 1.0)
        k.memset("pool", kT[:, :], 0.0)
        k.memset("pool", kiTp[:, :], 0.0)
        k.memset("pool", qiT[:, :], 0.0)
        k.memset("dve", QA[:, :], 0.0)
        k.memset("dve", QB[:, :], 0.0)
        k.dma("sp", ch("lbraw"), lbraw[:, :], lbl)
        k.memset("dve", lb[:, 0:4], 0.0)
        k.tt("dve", lb[:, 4:8], lbraw[:, 4:8], lbraw[:, 0:4], ALU.subtract)
        k.act(lb[:, 4:8], lb[:, 4:8], AF.Sigmoid)
        k.ts("dve", oml[:, :], lb[:, :], -1.0, ALU.mult, 1.0, ALU.add)

        wbuf = Buf("wbf")
        NSPL = 16
        step = (WTOT + NSPL - 1) // NSPL
        cw = ch("wcast")
        for l in range(2):
            for i in range(NSPL):
                a, b = i * step, min(WTOT, (i + 1) * step)
                if a >= b:
                    continue
                k.dma("pool", cw, wbf[l, :, a:b], wfl[l, :, a:b], w=[wbuf])

        wq = {"fm": 0, "f4": 0, "tm": 0}

        def loadw(l, name, kind):
            o, w = woff[name]
            if kind == "fm":
                t = wfm[wq["fm"] % 4]; wq["fm"] += 1
            elif kind == "f4":
                t = wf4[wq["f4"] % 2]; wq["f4"] += 1
            elif kind == "tm":
                t = wtm[wq["tm"] % 2]; wq["tm"] += 1
            else:
                t = wsm
            k.dma("sp", ch("w_" + t.buf.name), t[:, 0:w], wbf[l, :, o:o + w], r=[wbuf])
            return t

        rot = {"i": 0, "set": [0, 1, 2, 3]}

        def rbank():
            b = psf[rot["set"][rot["i"] % len(rot["set"])]]
            rot["i"] += 1
            return b

        def make_xT(TP, NT):
            for t in range(NT):
                xbf = xbfs[t % 2]
                k.copy("act", xbf[:TP, :], xs_v[t][:TP, :])
                for half in range(2):
                    pb = psb[half]
                    for c in range(4):
                        k.tr(pb[:, c * 128:c * 128 + TP], xbf[:TP, (half * 4 + c) * 128:(half * 4 + c + 1) * 128],
                             ident[:TP, :TP])
                    src = pb[:, 0:512].re("p (c t) -> p c t", c=4)[:, :, 0:TP]
                    k.copy("act" if half == 0 else "dve", xT[:, half * 4:(half + 1) * 4, t * TP:(t + 1) * TP], src)

        def layer_norm(TP, t, which):
            xv = xs_v[t][:TP, :]
            for c in range(2):
                k.op("dve", lambda g, c=c: g.bn_stats(bst.t[:TP, c * 6:(c + 1) * 6], xs_v[t].ap[:TP, c * 512:(c + 1) * 512]),
                     r=[xs_v[t]], w=[bst])
            k.op("dve", lambda g: g.bn_aggr(sm.t[:TP, 32:34], bst.t[:TP, 0:12]), r=[bst], w=[sm])
            k.act(sm[:TP, 34:35], sm[:TP, 33:34], AF.Sqrt, bias=cst[:TP, 1:2], scale=1.0)
            k.op("dve", lambda g: g.reciprocal(sm.t[:TP, 35:36], sm.t[:TP, 34:35]), r=[sm], w=[sm])
            k.ts("dve", xv, xv, sm[:TP, 32:33], ALU.subtract, sm[:TP, 35:36], ALU.mult)
            k.tt("dve", xv, xv, lnp[:TP, which * 2048:which * 2048 + 1024], ALU.mult)
            k.tt("dve", xv, xv, lnp[:TP, which * 2048 + 1024:which * 2048 + 2048], ALU.add)

        def rotary(TP, ps, out, nh, c16off=0):
            pv = ps.re("p (h d) -> p h d", d=64)
            ov = out.re("p (h d) -> p h d", d=64)
            Cv = rope[:TP, 0:nh * 16].re("p (h d) -> p h d", d=16)
            Sv = rope[:TP, 128:128 + nh * 16].re("p (h d) -> p h d", d=16)
            av = ra[:TP, 0:nh * 16].re("p (h d) -> p h d", d=16)
            bv = rb[:TP, 0:nh * 16].re("p (h d) -> p h d", d=16)
            import os
            rv = os.environ.get("ROTV", "")
            if rv != "noact":
                k.copy("act", ov[:, :, 16:64], pv[:, :, 16:64])
            if rv == "nodve":
                return
            k.tt("dve", av, pv[:, :, 0:16], Cv, ALU.mult)
            k.tt("dve", bv[:, :, 0:8], pv[:, :, 8:16], Sv[:, :, 0:8], ALU.mult)
            k.tt("dve", bv[:, :, 8:16], pv[:, :, 0:8], Sv[:, :, 8:16], ALU.mult)
            k.tt("dve", ov[:, :, 0:16], av, bv, ALU.add)

        def attn_A(l, t, TP, Lprev, masked, rope_src, okd, ovd, okid, row0):
            Lb = Lprev + TP
            kt_new = Lprev // 128
            qTc = qTs[t % 2]
            k.dma("sp", ch("rope"), rope[:TP, :], rope_src)
            w0 = loadw(l, "att0", "tm"); w1 = loadw(l, "att1", "tm"); w2 = loadw(l, "att2", "sm")
            rot["set"] = [0, 1, 2, 3]
            pA, pB, pC = rbank(), rbank(), rbank()
            for (pp, ww, wd) in ((pA, w0, 512), (pB, w1, 512), (pC, w2, 68)):
                for kc in range(8):
                    k.mm(pp[:TP, 0:wd], xT[:, kc, t * TP:(t + 1) * TP], ww[:, kc * wd:(kc + 1) * wd],
                         start=(kc == 0), stop=(kc == 7), inc=(kc == 7))
            rotary(TP, pA[:TP, 0:512], att_f[:TP, 0:512], 8)
            rotary(TP, pB[:TP, 0:512], att_f[:TP, 512:1024], 8)
            k.copy("act", att_f[:TP, 640:768].re("p (h d) -> p h d", d=64)[:, :, 0:16],
                   pB[:TP, 128:256].re("p (h d) -> p h d", d=64)[:, :, 0:16])
            rotary(TP, pC[:TP, 0:64], att_f[:TP, 1024:1088], 1)
            k.copy("act", att_f[:TP, 1088:1092], pC[:TP, 64:68])
            k.dma("pool", ch("st_att"), okd[l, row0:row0 + TP, :], att_f[:TP, 512:640])
            k.dma("pool", ch("st_att"), ovd[l, row0:row0 + TP, :], att_f[:TP, 640:768])
            k.dma("pool", ch("st_att"), okid[l, row0:row0 + TP, :], att_f[:TP, 1024:1088])
            k.copy("pool", att_b[:TP, 0:512].re("p (j g d) -> p j g d", j=4, g=2),
                   att_f[:TP, 0:512].re("p (g j d) -> p j g d", g=2, j=4))
            k.copy("pool", att_b[:TP, 512:640], att_f[:TP, 512:640])
            qdv = att_b[:TP, 640:1152].re("p (h u d) -> p h u d", h=4, u=2)
            qsv = att_f[:TP, 768:1024].re("p (h d) -> p h d", h=4)
            k.copy("pool", qdv[:, :, 0, :], qsv)
            k.copy("pool", qdv[:, :, 1, :], qsv)
            k.copy("pool", att_b[:TP, 1152:1216], att_f[:TP, 1024:1088])
            k.copy("pool", att_b[:TP, 1216:1280], att_f[:TP, 1024:1088])
            k.copy("pool", Vb[:TP, kt_new, :], att_f[:TP, 640:768])
            for j in range(4):
                k.tr(psb[0][:, j * 128:j * 128 + TP], att_b[:TP, j * 128:(j + 1) * 128], ident[:TP, :TP])
            k.act(qTc[:, 0:4 * TP].re("p (j t) -> p j t", j=4),
                  psb[0][:, 0:512].re("p (j t) -> p j t", j=4)[:, :, 0:TP], AF.Copy, scale=0.125)
            k.tr(psb[1][:, 0:TP], att_b[:TP, 512:640], ident[:TP, :TP])
            for h in range(4):
                k.tr(psb[1][:, 128 * (1 + h):128 * (1 + h) + TP], att_b[:TP, 640 + 128 * h:640 + 128 * (h + 1)],
                     ident[:TP, :TP])
            k.tr(psb[1][:, 640:640 + TP], att_b[:TP, 1152:1280], ident[:TP, :TP])
            k.copy("dve", kT[:, Lprev:Lprev + TP], psb[1][:, 0:TP])
            k.copy("act", qiT[:, 0:4 * TP].re("p (j t) -> p j t", j=4),
                   psb[1][:, 128:640].re("p (j t) -> p j t", j=4)[:, :, 0:TP])
            Bn = Lprev // 512
            hf = Bn % 2
            c0 = (Bn // 2) * 512 + (Lprev % 512)
            k.copy("dve", kiTp[hf * 64:hf * 64 + 64, c0:c0 + TP], psb[1][hf * 64:hf * 64 + 64, 640:640 + TP])
            nblk = (Lb + 511) // 512
            for B in range(nblk):
                wB = min(512, Lb - 512 * B)
                hb = B % 2
                cb = (B // 2) * 512
                for h in range(4):
                    pb = rbank()
                    k.mm(pb[:, 0:wB], qiT[hb * 64:hb * 64 + 64, h * TP:h * TP + 128],
                         kiTp[hb * 64:hb * 64 + 64, cb:cb + wB])
                    r_ = rr[(B * 4 + h) % 2]
                    k.act(r_[:TP, 0:wB], pb[:TP, 0:wB], AF.Relu)
                    if h == 0:
                        k.ts("dve", Sc[:TP, 512 * B:512 * B + wB], r_[:TP, 0:wB], att_f[:TP, 1088:1089], ALU.mult)
                    else:
                        k.stt(Sc[:TP, 512 * B:512 * B + wB], r_[:TP, 0:wB], att_f[:TP, 1088 + h:1089 + h],
                              Sc[:TP, 512 * B:512 * B + wB], ALU.mult, ALU.add)
            if masked:
                k.memset("dve", Sc[0:64, Lb - 64:Lb], NEG)
            return dict(t=t, TP=TP, Lb=Lb, qT=qTc)

        def attn_B(c):
            TP, Lb = c["TP"], c["Lb"]
            R = 512.0
            NIT = cfg.NIT
            m, lo = smM[:TP, 0:1], smM[:TP, 1:2]
            cnt, dd, cn2 = smC[:TP, 0:1], smD[:TP, 0:1], smD[:TP, 1:2]
            accA = smA[:TP, 0:1]
            thr = cfg.TOPK - 0.5
            import os as _os
            Ld = Lb if Lb < int(_os.environ.get("SPLIT_MIN", "1536")) else ((int(0.60 * Lb) + 63) // 64) * 64
            nA = Lb - Ld
            nch = 0 if nA == 0 else (1 if nA < int(_os.environ.get("ACT_MIN", "1024")) else int(_os.environ.get("ACT_CH", "2")))
            bounds = []
            if nch:
                stepc = ((nA + nch - 1) // nch + 63) // 64 * 64
                a = Ld
                while a < Lb:
                    bounds.append((a, min(Lb, a + stepc)))
                    a += stepc
            c["ny"] = NIT * (len(bounds) + 1)
            k.memset("dve", m, 0.0)
            for i in range(NIT):
                k.ts("dve", junk[:TP, 0:Ld], Sc[:TP, 0:Ld], m, ALU.is_ge, None, ALU.add, accum=cnt)
                cuse, tuse = cnt, thr
                if nA > 0:
                    for ci_, (a, b) in enumerate(bounds):
                        k.act(junkA[:TP, 0:b - a], Sc[:TP, a:b], AF.Sign, bias=m, scale=-1.0, accum=smA[:TP, ci_:ci_ + 1])
                        yield ("c", i)
                    prev = cnt
                    for ci_ in range(len(bounds)):
                        k.stt(cn2, smA[:TP, ci_:ci_ + 1], -0.5, prev, ALU.mult, ALU.add)
                        prev = cn2
                    cuse, tuse = cn2, thr - 0.5 * nA
                if i < NIT - 1:
                    cn = R / (2.0 ** (i + 1))
                    k.ts("dve", dd, cuse, tuse, ALU.is_ge, 2.0 * cn, ALU.mult)
                    k.stt(m, dd, -cn, m, ALU.add, ALU.add)
                else:
                    ci = R / (2.0 ** i)
                    k.ts("dve", dd, cuse, tuse, ALU.is_lt, -ci, ALU.mult)
                    k.tt("dve", lo, m, dd, ALU.add)
                yield ("i", i)

        def attn_Bfinal(c):
            TP, Lb = c["TP"], c["Lb"]
            lo = smM[:TP, 1:2]
            k.ts("dve", Mb[:TP, 0:Lb], Sc[:TP, 0:Lb], lo, ALU.is_lt, MASKNEG, ALU.mult)
            if TP < 128:
                k.memset("dve", Mb[TP:128, 0:Lb], 0.0)

        def attn_Cmain(c):
            TP, Lb, qTc = c["TP"], c["Lb"], c["qT"]
            I4 = I4p if TP == 128 else I4s
            nkt = (Lb + 127) // 128
            po = [psf[4], psf[5]]
            rot["set"] = [0, 1, 2, 3]

            def qk_pair(kt):
                kw = min(128, Lb - 128 * kt)
                pls = [rbank(), rbank()]
                for g in range(2):
                    k.mm(pls[g][:, 0:4 * TP], kT[g * 64:(g + 1) * 64, kt * 128:kt * 128 + 128],
                         qTc[g * 64:(g + 1) * 64, 0:4 * TP], start=True, stop=True)
                for g in range(2):
                    k.mm(pls[g][:kw, 0:4 * TP], Mb[:, kt * 128:kt * 128 + kw], I4[:, 0:4 * TP], start=False, stop=True,
                         skip_group_check=True)
                return pls

            cur = qk_pair(0)
            for kt in range(nkt):
                kw = min(128, Lb - 128 * kt)
                nxt = qk_pair(kt + 1) if kt + 1 < nkt else None
                ps = []
                for g in range(2):
                    p_ = pT[(2 * kt + g) % 3]
                    k.act(p_[:kw, 0:4 * TP], cur[g][:kw, 0:4 * TP], AF.Exp)
                    ps.append(p_)
                yield kt
                for g in range(2):
                    gs = slice(g * 64, (g + 1) * 64)
                    k.mm(po[0][gs, 0:4 * TP], Vb[:kw, kt, gs], ps[g][:kw, 0:4 * TP], start=(kt == 0), stop=(kt == nkt - 1))
                for g in range(2):
                    gs = slice(g * 64, (g + 1) * 64)
                    k.mm(po[1][gs, 0:4 * TP], ones_bf[:kw, 0:64], ps[g][:kw, 0:4 * TP], start=(kt == 0), stop=(kt == nkt - 1))
                yield kt
                cur = nxt

        def attn_Ctail(c):
            TP, t = c["TP"], c["t"]
            po = [psf[4], psf[5]]
            rd = rr[0]
            k.op("dve", lambda gg: gg.reciprocal(rd.t[:, 0:4 * TP], po[1].t[:, 0:4 * TP]), r=[po[1]], w=[rd])
            k.tt("dve", yaT[:, :, t * TP:(t + 1) * TP], po[0][:, 0:4 * TP].re("p (c q) -> p c q", c=4),
                 rd[:, 0:4 * TP].re("p (c q) -> p c q", c=4), ALU.mult)

        def hgrn(l, TP, NT):
            TT = TP * NT
            nch = TP // 64
            rot["set"] = [0, 1]
            for c, nm in ((0, "hig0"), (1, "hig1")):
                w_ = loadw(l, nm, "tm")
                for t in range(NT):
                    pb = rbank()
                    for kc in range(8):
                        k.mm(pb[:TP, :], xT[:, kc, t * TP:(t + 1) * TP], w_[:, kc * 512:(kc + 1) * 512],
                             start=(kc == 0), stop=(kc == 7), inc=(kc == 7))
                    if c == 0:
                        k.copy("act", vb[:TP, t * 512:(t + 1) * 512], pb[:TP, :])
                    else:
                        k.act(sg[:TP, t * 512:(t + 1) * 512], pb[:TP, :], AF.Silu)
            oacc = [psf[2 + t] for t in range(NT)]
            h0, h1, h2, h3 = hs_
            if nch == 2:
                k.memset("pool", QA[:, 0:TT].re("p (t u s) -> p t u s", u=2, s=64)[:, :, 1, :], 0.0)
                k.memset("pool", QB[:, 0:TT].re("p (t u s) -> p t u s", u=2, s=64)[:, :, 0, :], 0.0)
            for h in range(4):
                wq_ = loadw(l, "hqf%d" % h, "fm")
                wf_ = loadw(l, "hqf%d" % (4 + h), "fm")
                zq = rbank()
                for kc in range(8):
                    k.mm(zq[:, 0:TT], wq_[:, kc * 128:(kc + 1) * 128], xT[:, kc, 0:TT], start=(kc == 0), stop=(kc == 7), inc=(kc == 7))
                zf = rbank()
                for kc in range(8):
                    k.mm(zf[:, 0:TT], wf_[:, kc * 128:(kc + 1) * 128], xT[:, kc, 0:TT], start=(kc == 0), stop=(kc == 7), inc=(kc == 7))
                lbc = lb[:, l * 4 + h:l * 4 + h + 1]
                k.act(h0[:, 0:TT], zf[:, 0:TT], AF.Exp, scale=-1.0)
                k.act(h1[:, 0:TT], h0[:, 0:TT], AF.Ln, bias=cst[:, 0:1], scale=1.0)
                k.act(h0[:, 0:TT], h0[:, 0:TT], AF.Ln, bias=cst[:, 0:1], scale=lbc)
                k.tt("dve", h0[:, 0:TT], h0[:, 0:TT], h1[:, 0:TT], ALU.subtract)
                k.act(h2[:, 0:TT], zf[:, 0:TT], AF.Sigmoid, scale=-1.0)
                k.ts("dve", h2[:, 0:TT], h2[:, 0:TT], oml[:, l * 4 + h:l * 4 + h + 1], ALU.mult)
                k.act(h3[:, 0:TT], zq[:, 0:TT], AF.Silu)
                for c in range(TT // 64):
                    k.op("dve", lambda g, c=c: g.tensor_tensor_scan(h1.ap[:, c * 64:(c + 1) * 64], ones.t[:, 0:64],
                                                                   h0.ap[:, c * 64:(c + 1) * 64], 0.0, ALU.mult, ALU.add),
                         r=[ones, h0], w=[h1])
                k.act(h0[:, 0:TT], h1[:, 0:TT], AF.Exp)
                k.act(h1[:, 0:TT], h1[:, 0:TT], AF.Exp, scale=-1.0)
                if nch == 2:
                    qv = h3[:, 0:TT].re("p (t u s) -> p t u s", u=2, s=64)
                    ev = h0[:, 0:TT].re("p (t u s) -> p t u s", u=2, s=64)
                    k.tt("dve", QA[:, 0:TT].re("p (t u s) -> p t u s", u=2, s=64)[:, :, 0, :], qv[:, :, 0, :], ev[:, :, 0, :], ALU.mult)
                    k.tt("dve", QB[:, 0:TT].re("p (t u s) -> p t u s", u=2, s=64)[:, :, 1, :], qv[:, :, 1, :], ev[:, :, 1, :], ALU.mult)
                else:
                    k.tt("dve", QA[:, 0:TT], h3[:, 0:TT], h0[:, 0:TT], ALU.mult)
                k.tt("dve", KT[:, 0:TT], h2[:, 0:TT], h1[:, 0:TT], ALU.mult)
                k.copy("pool", Sbf[:, :], Sst[:, h, :])
                for t in range(NT):
                    sl = slice(t * TP, (t + 1) * TP)
                    sc = rbank()
                    k.mm(sc[:TP, 0:64], KT[:, sl], QA[:, t * TP:t * TP + 64])
                    if nch == 2:
                        k.mm(sc[:TP, 64:128], KT[:, sl], QB[:, t * TP + 64:t * TP + 128])
                    k.tt("dve", scm[:TP, 0:TP], sc[:TP, 0:TP], bmask[:TP, 0:TP], ALU.mult)
                    k.tr(psb[0][:TP, 0:128], KT[:, sl], ident[:, :])
                    k.copy("act", Ktok[:TP, :], psb[0][:TP, 0:128])
                    ob = oacc[t][:TP, h * 128:(h + 1) * 128]
                    first = (h == 0)
                    for u in range(nch):
                        us = slice(u * 64, (u + 1) * 64)
                        vsl = vb[us, t * 512 + h * 128:t * 512 + (h + 1) * 128]
                        k.mm(ob, scm[us, 0:TP], vsl, start=(first and u == 0), stop=False, skip_group_check=True)
                        qsrc = QA if u == 0 else QB
                        k.mm(ob, qsrc[:, sl], Sbf[:, :], start=False, stop=(u == nch - 1), skip_group_check=True)
                        kv = rbank()
                        k.mm(kv[:, 0:128], Ktok[us, :], vsl)
                        k.tt("dve", Sst[:, h, :], kv[:, 0:128], Sst[:, h, :], ALU.add)
                        ecol = t * TP + u * 64 + 63
                        k.act(Sst[:, h, :], Sst[:, h, :], AF.Identity, scale=h0[:, ecol:ecol + 1])
                        k.copy("pool", Sbf[:, :], Sst[:, h, :])
            for t in range(NT):
                ob = oacc[t]
                for h in range(4):
                    k.act(on[:TP, 0:128], ob[:TP, h * 128:(h + 1) * 128], AF.Square, accum=sm[:TP, 16 + h:17 + h])
                k.act(sm[:TP, 20:24], sm[:TP, 16:20], AF.Sqrt, bias=cst[:TP, 1:2], scale=1.0 / 128.0)
                k.op("dve", lambda g: g.reciprocal(sm.t[:TP, 24:28], sm.t[:TP, 20:24]), r=[sm], w=[sm])
                for h in range(4):
                    k.ts("dve", on[:TP, h * 128:(h + 1) * 128], ob[:TP, h * 128:(h + 1) * 128], sm[:TP, 24 + h:25 + h], ALU.mult)
                k.tt("dve", on[:TP, :], on[:TP, :], gB[:TP, :], ALU.mult)
                k.tt("dve", ybb[:TP, :], on[:TP, :], sg[:TP, t * 512:(t + 1) * 512], ALU.mult)
                for c in range(4):
                    k.tr(psb[1][:, c * 128:c * 128 + TP], ybb[:TP, c * 128:(c + 1) * 128], ident[:TP, :TP])
                k.copy("act", ybT[:, :, t * TP:(t + 1) * TP], psb[1][:, 0:512].re("p (c t) -> p c t", c=4)[:, :, 0:TP])

        def sconv(l, TT):
            rot["set"] = [0, 1, 2, 3]
            h0, h1, h2, h3 = hs_
            for j in range(4):
                wb_ = loadw(l, "c%d" % j, "fm"); wc_ = loadw(l, "c%d" % (4 + j), "fm"); wx_ = loadw(l, "c%d" % (8 + j), "fm")
                pbk, pck, pxk = rbank(), rbank(), rbank()
                for (pp, ww) in ((pbk, wb_), (pck, wc_), (pxk, wx_)):
                    for kc in range(8):
                        k.mm(pp[:, 0:TT], ww[:, kc * 128:(kc + 1) * 128], xT[:, kc, 0:TT], start=(kc == 0), stop=(kc == 7), inc=(kc == 7))
                uext = V(AR1.t[:, 0:1024], h0.buf)
                k.copy("pool", uext[:, 0:2], uh[:, 2 * j:2 * j + 2])
                k.copy("act", h2[:, 0:TT], pxk[:, 0:TT])
                k.op("dve", lambda g: g.tensor_tensor(uext.ap[:, 2:2 + TT], pck.t[:, 0:TT], h2.ap[:, 0:TT], ALU.mult),
                     r=[pck, h2], w=[h0, h1])
                k.op("pool", lambda g, j=j: g.tensor_copy(uh.t[:, 2 * j:2 * j + 2], uext.ap[:, TT:TT + 2]), r=[h0, h1], w=[uh])
                k.op("dve", lambda g, j=j: g.tensor_scalar(h3.ap[:, 0:TT], uext.ap[:, 2:2 + TT], scw.t[:, 4 * j + 2:4 * j + 3],
                                                          scw.t[:, 4 * j + 3:4 * j + 4], ALU.mult, ALU.add),
                     r=[h0, h1, scw], w=[h3])
                k.op("dve", lambda g, j=j: g.scalar_tensor_tensor(h3.ap[:, 0:TT], uext.ap[:, 1:1 + TT], scw.t[:, 4 * j + 1:4 * j + 2],
                                                                 h3.ap[:, 0:TT], ALU.mult, ALU.add),
                     r=[h0, h1, scw, h3], w=[h3])
                k.op("dve", lambda g, j=j: g.scalar_tensor_tensor(h3.ap[:, 0:TT], uext.ap[:, 0:TT], scw.t[:, 4 * j:4 * j + 1],
                                                                 h3.ap[:, 0:TT], ALU.mult, ALU.add),
                     r=[h0, h1, scw, h3], w=[h3])
                k.tt("dve", ycT[:, j, 0:TT], pbk[:, 0:TT], h3[:, 0:TT], ALU.mult)

        def merge(l, TP, NT):
            TT = TP * NT
            rot["set"] = [0, 1, 2, 3, 4, 5]
            ysrc = [yaT, ybT, ycT]
            for cc in range(8):
                for b in range(3):
                    wg_ = loadw(l, "g%d" % (b * 8 + cc), "fm")
                    wb_ = loadw(l, "br%d_%d" % (b, cc), "f4")
                    gp = rbank()
                    for kc in range(8):
                        k.mm(gp[:, 0:TT], wg_[:, kc * 128:(kc + 1) * 128], xT[:, kc, 0:TT], start=(kc == 0), stop=(kc == 7), inc=(kc == 7))
                    bp = rbank()
                    for kc in range(4):
                        k.mm(bp[:, 0:TT], wb_[:, kc * 128:(kc + 1) * 128], ysrc[b][:, kc, 0:TT], start=(kc == 0), stop=(kc == 3), inc=(kc == 3))
                    k.act(sgm[:, 0:TT], gp[:, 0:TT], AF.Sigmoid)
                    if b == 0:
                        k.tt("dve", mg[:, 0:TT], sgm[:, 0:TT], bp[:, 0:TT], ALU.mult)
                    elif b == 1:
                        k.tt("dve", tmpm[:, 0:TT], sgm[:, 0:TT], bp[:, 0:TT], ALU.mult)
                        k.tt("pool", mg[:, 0:TT], mg[:, 0:TT], tmpm[:, 0:TT], ALU.add)
                    else:
                        k.tt("dve", tmpm[:, 0:TT], sgm[:, 0:TT], bp[:, 0:TT], ALU.mult)
                        k.tt("pool", gT[:, cc * 512:cc * 512 + TT], mg[:, 0:TT], tmpm[:, 0:TT], ALU.add)
            for c in range(2):
                wo_ = loadw(l, "wout%d" % c, "tm")
                for t in range(NT):
                    pb = rbank()
                    for kc in range(8):
                        k.mm(pb[:TP, :], gT[:, kc * 512 + t * TP:kc * 512 + (t + 1) * TP], wo_[:, kc * 512:(kc + 1) * 512],
                             start=(kc == 0), stop=(kc == 7), inc=(kc == 7))
                    xv = xs_v[t][:TP, c * 512:(c + 1) * 512]
                    k.stt(xv, xv, ALPHA, pb[:TP, :], ALU.mult, ALU.add)
            for t in range(NT):
                layer_norm(TP, t, 0)

        def ffn(l, TP, NT):
            TT = TP * NT
            rot["set"] = [0, 1, 2, 3, 4, 5]
            for j in range(22):
                for wi, chn_ in enumerate((j, 22 + j)):
                    w_ = loadw(l, "up%d" % chn_, "fm")
                    hp = rbank()
                    for kc in range(8):
                        k.mm(hp[:, 0:TT], w_[:, kc * 128:(kc + 1) * 128], xT[:, kc, 0:TT], start=(kc == 0), stop=(kc == 7), inc=(kc == 7))
                    u_ = ue[wi]
                    a_ = fac[wi]
                    k.copy("pool", u_[:, 0:2], fh[:, 2 * chn_:2 * chn_ + 2])
                    k.copy("act", u_[:, 2:2 + TT], hp[:, 0:TT])
                    k.copy("pool", fh[:, 2 * chn_:2 * chn_ + 2], u_[:, TT:TT + 2])
                    k.act(a_[:, 0:TT], hp[:, 0:TT], AF.Identity, bias=fcw[:, 4 * chn_ + 3:4 * chn_ + 4],
                          scale=fcw[:, 4 * chn_ + 2:4 * chn_ + 3])
                    k.stt(a_[:, 0:TT], u_[:, 1:1 + TT], fcw[:, 4 * chn_ + 1:4 * chn_ + 2], a_[:, 0:TT], ALU.mult, ALU.add)
                    k.stt(a_[:, 0:TT], u_[:, 0:TT], fcw[:, 4 * chn_:4 * chn_ + 1], a_[:, 0:TT], ALU.mult, ALU.add)
                k.act(sa[:, 0:TT], fac[0][:, 0:TT], AF.Silu)
                k.tt("dve", gT[:, j * 512:j * 512 + TT], sa[:, 0:TT], fac[1][:, 0:TT], ALU.mult)
            acc = [psf[2 + t] for t in range(NT)]
            for c in range(2):
                kc0 = 0
                for kg, nk in enumerate((8, 8, 6)):
                    wd_ = loadw(l, "wd%d_%d" % (c, kg), "tm")
                    for t in range(NT):
                        for kk in range(nk):
                            kc = kc0 + kk
                            k.mm(acc[t][:TP, :], gT[:, kc * 512 + t * TP:kc * 512 + (t + 1) * TP], wd_[:, kk * 512:(kk + 1) * 512],
                                 start=(kc == 0), stop=(kc == 21), inc=(kk == nk - 1))
                    kc0 += nk
                for t in range(NT):
                    xv = xs_v[t][:TP, c * 512:(c + 1) * 512]
                    k.stt(xv, xv, ALPHA, acc[t][:TP, :], ALU.mult, ALU.add)
            for t in range(NT):
                layer_norm(TP, t, 1)

        def run_group(l, grp):
            if grp == "p":
                TP, NT, nmac, Lbase = 128, 4, S // 512, 0
                xsrc = xp if l == 0 else xmid_p
                xdst = xmid_p if l == 0 else yp
                okd, ovd, okid, ohd, oscd, offd, ropd = okp, ovp, okip, ohp, oscp, offp, ropep
                xmb = xmbuf_p
            else:
                TP, NT, nmac, Lbase = 64, 1, 1, P
                xsrc = xs if l == 0 else xmid_s
                xdst = xmid_s if l == 0 else ys
                okd, ovd, okid, ohd, oscd, offd, ropd = oks, ovs, okis, ohs, oscs, offs, ropes
                xmb = xmbuf_s
            TT = TP * NT
            k.dma("sp", ch("gB"), gB[:, :], gB_d[l])
            k.dma("sp", ch("scw"), scw[:, :], scw_d[l])
            k.dma("sp", ch("fcw"), fcw[:, :], fcw_d[l])
            k.dma("sp", ch("lnp"), lnp[:, :], lnp_d[l])
            if grp == "p":
                k.memset("pool", Sst[:, :, :], 0.0)
                k.memset("pool", uh[:, :], 0.0)
                k.memset("pool", fh[:, :], 0.0)
            else:
                k.dma("sp", ch("Sst"), Sst[:, :, :], sh[l].rearrange("h d v -> d h v"))
                k.dma("sp", ch("uh"), uh[:, :], ssc[l])
                k.dma("sp", ch("fh"), fh[:, :], sff[l])
                claim(useA, useB + useC)
                for kt in range(P // 128):
                    rows = slice(kt * 128, (kt + 1) * 128)
                    k.dma("sp", ch("cst0"), att_f[:, 0:128], ck[l, rows, :])
                    k.dma("sp", ch("cst1"), att_f[:, 128:256], cv[l, rows, :])
                    k.dma("sp", ch("cst2"), att_f[:, 256:320], cki[l, rows, :])
                    k.copy("pool", att_b[:, 0:128], att_f[:, 0:128])
                    k.copy("pool", att_b[:, 128:192], att_f[:, 256:320])
                    k.copy("pool", att_b[:, 192:256], att_f[:, 256:320])
                    k.copy("pool", Vb[:, kt, :], att_f[:, 128:256])
                    k.tr(psb[0][:, 0:128], att_b[:, 0:128], ident[:, :])
                    k.tr(psb[0][:, 128:256], att_b[:, 128:256], ident[:, :])
                    k.copy("dve", kT[:, kt * 128:(kt + 1) * 128], psb[0][:, 0:128])
                    Bn = (kt * 128) // 512
                    hf = Bn % 2
                    c0 = (Bn // 2) * 512 + (kt * 128) % 512
                    k.copy("act", kiTp[hf * 64:hf * 64 + 64, c0:c0 + 128], psb[0][hf * 64:hf * 64 + 64, 128:256])
            import os
            SL = int(os.environ.get("SL", "9"))
            for mt in range(nmac):
                if grp == "s" and SL < 1:
                    break
                tok0 = mt * TT
                claim(useA, useB + useC)
                for t in range(NT):
                    k.dma("sp", ch("x_sb%d" % t), xs_v[t][:TP, :], xsrc[tok0 + t * TP:tok0 + (t + 1) * TP, :],
                          r=[xmb[mt][t]] if l == 1 else [])
                make_xT(TP, NT)
                if cfg.stop >= 2 and not (grp == "s" and SL < 2):
                    prev = None
                    for t in range(NT):
                        row0 = tok0 + t * TP
                        cur = attn_A(l, t, TP, Lbase + row0, grp == "p", ropd[row0:row0 + TP, :], okd, ovd, okid, row0)
                        gb = attn_B(cur)
                        if prev is not None:
                            gc = attn_Cmain(prev)
                            nC = 2 * ((prev["Lb"] + 127) // 128)
                            done_c = 0
                            yi = 0
                            for _ in gb:
                                yi += 1
                                tgt = min(nC, (yi * nC) // max(1, cur.get("ny", cfg.NIT)))
                                while done_c < tgt:
                                    next(gc, None)
                                    done_c += 1
                            for _ in gc:
                                pass
                            attn_Bfinal(cur)
                            attn_Ctail(prev)
                        else:
                            for _ in gb:
                                pass
                            attn_Bfinal(cur)
                        prev = cur
                    for _ in attn_Cmain(prev):
                        pass
                    attn_Ctail(prev)
                claim(useB, useA)
                if cfg.stop >= 3:
                    hgrn(l, TP, NT)
                if cfg.stop >= 4:
                    sconv(l, TT)
                claim(useC, useA + useB)
                if cfg.stop >= 5:
                    merge(l, TP, NT)
                    make_xT(TP, NT)
                if cfg.stop >= 6:
                    ffn(l, TP, NT)
                for t in range(NT):
                    k.dma("pool", ch("st_x%d" % t), xdst[tok0 + t * TP:tok0 + (t + 1) * TP, :], xs_v[t][:TP, :],
                          w=[xmb[mt][t]] if l == 0 else [])
            k.dma("pool", ch("st_S"), ohd[l].rearrange("h d v -> d h v"), Sst[:, :, :])
            k.dma("pool", ch("st_uh"), oscd[l], uh[:, :])
            k.dma("pool", ch("st_fh"), offd[l], fh[:, :])

        xmbuf_p = [[Buf("xmp%d_%d" % (i, t)) for t in range(4)] for i in range(S // 512)]
        xmbuf_s = [[Buf("xms")]]
        for grp in ("p", "s"):
            for l in range(2):
                import os as _os
                if _os.environ.get("ONLYS") and not (grp == "s" and l == 0):
                    continue
                if cfg.stop >= 9 or (cfg.stop >= 1 and grp == "p" and l == 0) or (cfg.stop >= 7 and grp == "p") or (cfg.stop >= 8 and l == 0):
                    run_group(l, grp)
        k.wait_all("pool")
        k.wait_all("sp")
        print("instructions:", k.ninst, "channels:", k.nchan, "sbuf_left:", nc.sbuf_bytes_remaining)
    return nc


def kernel_cfg(cfg, inputs, n_cores=8):
    f32 = np.float32
    g = lambda n: np.asarray(inputs[n], dtype=f32)
    x_prompt, x_sample = g("x_prompt"), g("x_sample")
    S, DS, P = cfg.S, cfg.DS, cfg.P
    wfl = host_weights(g("w_in"), g("w_branch"), g("w_out"), g("w_up"), g("w_down"))
    lbl = np.ascontiguousarray(g("hgrn_lb_logits").reshape(2, 4, 128).transpose(2, 0, 1).reshape(128, 8))
    gBv = np.ascontiguousarray(np.broadcast_to(np.tile(g("hgrn_norm_g"), (1, 4))[:, None, :], (2, 128, 512)))
    scw = np.concatenate([g("sconv_w"), g("sconv_b")[:, None, :]], axis=1)
    scw = np.ascontiguousarray(scw.reshape(2, 4, 4, 128).transpose(0, 3, 2, 1).reshape(2, 128, 16))
    fcw = np.concatenate([g("ffn_conv_w"), g("ffn_conv_b")[:, None, :]], axis=1)
    fcw = np.ascontiguousarray(fcw.reshape(2, 4, 44, 128).transpose(0, 3, 2, 1).reshape(2, 128, 176))
    lnp = np.stack([g("ln1_g"), g("ln1_b"), g("ln2_g"), g("ln2_b")], axis=1).reshape(2, 1, 4096)
    lnp = np.ascontiguousarray(np.broadcast_to(lnp, (2, 128, 4096)))
    ropep = rope_table(np.arange(S))
    ropes = rope_table(P + np.arange(DS))
    ck = g("cache_attn_k").reshape(2, -1, P, 128)
    cv = g("cache_attn_v").reshape(2, -1, P, 128)
    cki = g("cache_idx_k")
    sh = g("state_hgrn")
    ssc = g("state_sconv")
    sff = g("state_ffn_conv")
    nb = x_prompt.shape[0]
    in_maps = []
    for c in range(n_cores):
        pb = (c // 2) % nb
        sb = c % x_sample.shape[0]
        ssc_t = np.ascontiguousarray(ssc[:, sb].reshape(2, 2, 4, 128).transpose(0, 3, 2, 1).reshape(2, 128, 8))
        sff_t = np.ascontiguousarray(sff[:, sb].reshape(2, 2, 44, 128).transpose(0, 3, 2, 1).reshape(2, 128, 88))
        in_maps.append({
            "xp": np.ascontiguousarray(x_prompt[pb]), "xs": np.ascontiguousarray(x_sample[sb]),
            "ck": np.ascontiguousarray(ck[:, sb]), "cv": np.ascontiguousarray(cv[:, sb]),
            "cki": np.ascontiguousarray(cki[:, sb]), "sh": np.ascontiguousarray(sh[:, sb]),
            "ssc": ssc_t, "sff": sff_t, "wfl": wfl, "lbl": lbl, "gB": gBv, "scw": scw, "fcw": fcw, "lnp": lnp,
            "ropep": ropep, "ropes": ropes,
        })
    nc = build(cfg)
    res = run_bass_kernel_spmd(nc, in_maps, core_ids=list(range(n_cores)))
    R = res.results
    NB, NSB = x_prompt.shape[0], x_sample.shape[0]
    pc = [2 * b for b in range(NB)]

    def st_t(a, nchk):
        return a.reshape(2, 128, nchk, 2).transpose(0, 3, 2, 1).reshape(2, 2, nchk * 128)

    y_p = np.stack([R[c]["yp"] for c in pc], 0)
    y_s = np.stack([R[c]["ys"] for c in range(NSB)], 0)
    k_p = np.stack([R[c]["okp"] for c in pc], 1).reshape(2, NB, S, 2, 64)
    v_p = np.stack([R[c]["ovp"] for c in pc], 1).reshape(2, NB, S, 2, 64)
    ki_p = np.stack([R[c]["okip"] for c in pc], 1)
    h_p = np.stack([R[c]["ohp"] for c in pc], 1)
    sc_p = np.stack([st_t(R[c]["oscp"], 4) for c in pc], 1)
    ff_p = np.stack([st_t(R[c]["offp"], 44) for c in pc], 1)
    k_s = np.stack([R[c]["oks"] for c in range(NSB)], 1).reshape(2, NSB, DS, 2, 64)
    v_s = np.stack([R[c]["ovs"] for c in range(NSB)], 1).reshape(2, NSB, DS, 2, 64)
    ki_s = np.stack([R[c]["okis"] for c in range(NSB)], 1)
    h_s = np.stack([R[c]["ohs"] for c in range(NSB)], 1)
    sc_s = np.stack([st_t(R[c]["oscs"], 4) for c in range(NSB)], 1)
    ff_s = np.stack([st_t(R[c]["offs"], 44) for c in range(NSB)], 1)
    outs = (y_p, y_s, k_p, v_p, ki_p, h_p, sc_p, ff_p, k_s, v_s, ki_s, h_s, sc_s, ff_s)
    return tuple(np.ascontiguousarray(o, dtype=np.float32) for o in outs)


def kernel(**inputs):
    return kernel_cfg(Cfg(), inputs)
```

```python
import numpy as np
import concourse.bass as bass
import concourse.mybir as mybir
from concourse.bass_utils import run_bass_kernel_spmd
from contextlib import ExitStack

F32 = mybir.dt.float32
BF16 = mybir.dt.bfloat16
AF = mybir.ActivationFunctionType
ALU = mybir.AluOpType


class Buf:
    __slots__ = ("name", "w", "r", "excl")

    def __init__(self, name):
        self.name = name
        self.w = None
        self.r = {}
        self.excl = False


class V:
    __slots__ = ("ap", "buf")

    def __init__(self, ap, buf):
        self.ap = ap
        self.buf = buf

    def __getitem__(self, idx):
        return V(self.ap[idx], self.buf)

    def re(self, pat, **kw):
        return V(self.ap.rearrange(pat, **kw), self.buf)


class T:
    def __init__(self, k, name, shape, dtype, psum=False, buf=None):
        if psum:
            self.t = k.es.enter_context(k.nc.psum_tensor("t_" + name, shape, dtype))
        else:
            self.t = k.es.enter_context(k.nc.sbuf_tensor("t_" + name, shape, dtype))
        self.buf = buf if buf is not None else Buf(name)
        self.buf.excl = bool(psum)

    def __getitem__(self, idx):
        return V(self.t[idx], self.buf)


class K:
    def __init__(self, nc, es):
        self.nc = nc
        self.es = es
        self.eng = {"pe": nc.tensor, "act": nc.scalar, "dve": nc.vector, "pool": nc.gpsimd, "sp": nc.sync}
        self.sem = {}
        self.cnt = {}
        self.seen = {e: {} for e in self.eng}
        self.ekey = {}
        self.eep = {}
        for e in ("pe", "act", "dve", "pool"):
            self.sem[e] = es.enter_context(nc.semaphore("s_" + e))
            self.cnt[e] = 0
            self.ekey[e] = e
            self.eep[e] = 0
        self.nchan = 0
        self.ninst = 0

    def chan(self, name):
        key = "ch_%d_%s" % (self.nchan, name)
        self.nchan += 1
        self.sem[key] = self.es.enter_context(self.nc.semaphore(key))
        self.cnt[key] = 0
        return key

    def _wait(self, e, deps):
        eng = self.eng[e]
        best = {}
        for key, val in deps:
            if e == "pe" and key.startswith("pe"):
                continue
            if best.get(key, 0) < val:
                best[key] = val
        for key, val in best.items():
            if self.seen[e].get(key, 0) < val:
                eng.wait_ge(self.sem[key], val)
                self.seen[e][key] = val

    def op(self, e, fn, r=(), w=(), chan=None, inc=True):
        rb = [x.buf if not isinstance(x, Buf) else x for x in r if x is not None]
        wb = [x.buf if not isinstance(x, Buf) else x for x in w if x is not None]
        wb = wb + [b for b in rb if b.excl]
        rb = [b for b in rb if not b.excl]
        deps = []
        for b in rb:
            if b.w is not None:
                deps.append(b.w)
        for b in wb:
            if b.w is not None:
                deps.append(b.w)
            deps.extend(b.r.items())
        self._wait(e, deps)
        inst = fn(self.eng[e])
        self.ninst += 1
        if chan is None:
            ek = self.ekey[e]
            if self.cnt[ek] >= 30000:
                self.eep[e] += 1
                ek = "%s_%d" % (e, self.eep[e])
                self.ekey[e] = ek
                self.sem[ek] = self.es.enter_context(self.nc.semaphore("s_" + ek))
                self.cnt[ek] = 0
            if inc:
                self.cnt[ek] += 1
                inst.then_inc(self.sem[ek], 1)
                tk = (ek, self.cnt[ek])
            else:
                tk = (ek, self.cnt[ek] + 1)
        else:
            self.cnt[chan] += 16
            inst.then_inc(self.sem[chan], 16)
            tk = (chan, self.cnt[chan])
        for b in rb:
            if b.r.get(tk[0], 0) < tk[1]:
                b.r[tk[0]] = tk[1]
        for b in wb:
            b.w = tk
            b.r = {}
        return inst

    def mm(self, o, lhsT, rhs, start=True, stop=True, extra_r=(), inc=True, **kw):
        return self.op("pe", lambda g: g.matmul(o.ap, lhsT.ap, rhs.ap, start=start, stop=stop, **kw),
                       r=[lhsT, rhs] + list(extra_r), w=[o], inc=inc)

    def tr(self, o, in_, ident):
        return self.op("pe", lambda g: g.transpose(o.ap, in_.ap, ident.ap), r=[in_, ident], w=[o])

    def act(self, o, in_, func, bias=None, scale=None, accum=None, e="act"):
        kw = {}
        rr = [in_]
        if bias is not None:
            if isinstance(bias, V):
                kw["bias"] = bias.ap
                rr.append(bias)
            else:
                kw["bias"] = bias
        if scale is not None:
            if isinstance(scale, V):
                kw["scale"] = scale.ap
                rr.append(scale)
            else:
                kw["scale"] = scale
        ww = [o]
        if accum is not None:
            kw["accum_out"] = accum.ap
            ww.append(accum)
        return self.op("act", lambda g: g.activation(o.ap, in_.ap, func, **kw), r=rr, w=ww)

    def tt(self, e, o, a, b, op):
        return self.op(e, lambda g: g.tensor_tensor(o.ap, a.ap, b.ap, op), r=[a, b], w=[o])

    def ts(self, e, o, a, s1, op0, s2=None, op1=None, accum=None):
        rr = [a]
        v1 = s1
        v2 = s2
        if isinstance(s1, V):
            rr.append(s1)
            v1 = s1.ap
        if isinstance(s2, V):
            rr.append(s2)
            v2 = s2.ap
        ww = [o]
        kw = {}
        if accum is not None:
            kw["accum_out"] = accum.ap
            ww.append(accum)
        if op1 is None:
            return self.op(e, lambda g: g.tensor_scalar(o.ap, a.ap, v1, None, op0, **kw), r=rr, w=ww)
        return self.op(e, lambda g: g.tensor_scalar(o.ap, a.ap, v1, v2, op0, op1, **kw), r=rr, w=ww)

    def stt(self, o, a, s, b, op0, op1):
        rr = [a, b]
        v = s
        if isinstance(s, V):
            rr.append(s)
            v = s.ap
        return self.op("dve", lambda g: g.scalar_tensor_tensor(o.ap, a.ap, v, b.ap, op0, op1), r=rr, w=[o])

    def copy(self, e, o, a):
        if e == "act":
            return self.op(e, lambda g: g.copy(o.ap, a.ap), r=[a], w=[o])
        return self.op(e, lambda g: g.tensor_copy(o.ap, a.ap), r=[a], w=[o])

    def memset(self, e, o, val):
        return self.op(e, lambda g: g.memset(o.ap, val), r=[], w=[o])

    def dma(self, e, chan, o, i, r=(), w=(), **kw):
        oa = o.ap if isinstance(o, V) else o
        ia = i.ap if isinstance(i, V) else i
        rr = list(r) + ([i] if isinstance(i, V) else [])
        ww = list(w) + ([o] if isinstance(o, V) else [])
        return self.op(e, lambda g: g.dma_start(out=oa, in_=ia, **kw), r=rr, w=ww, chan=chan)

    def wait_all(self, e):
        deps = [(key, c) for key, c in self.cnt.items() if c > 0]
        self._wait(e, deps)

D = 1024
DFF = 2816
ALPHA = 4.0 ** 0.25
LN_EPS = 1e-5
NEG = -1e30
MASKNEG = -30000.0
O_HQ, O_HF, O_HI, O_HG, O_CB, O_CC, O_CX, O_G = 1092, 1604, 2116, 2628, 3140, 3652, 4164, 4676


class Cfg:
    def __init__(self, S=8192, DS=64, P=2048, TOPK=256, NIT=28, stop=9):
        self.S, self.DS, self.P, self.TOPK, self.NIT = S, DS, P, TOPK, NIT
        self.stop = stop
        self.sub = 9
        self.LMAX = max(S, P + DS)
        self.LMAX = ((self.LMAX + 1023) // 1024) * 1024
        self.NKT = self.LMAX // 128


def _tile(W, c0, width, k0=0, nk=None):
    K = W.shape[0]
    if nk is None:
        nk = K // 128
    sub = W[k0 * 128:(k0 + nk) * 128, c0:c0 + width]
    return np.ascontiguousarray(sub.reshape(nk, 128, width).transpose(1, 0, 2).reshape(128, nk * width))


def weight_layout():
    items = []
    for j in range(8):
        items.append(("hqf%d" % j, 1024))
    for j in range(12):
        items.append(("c%d" % j, 1024))
    for j in range(24):
        items.append(("g%d" % j, 1024))
    for j in range(44):
        items.append(("up%d" % j, 1024))
    for b in range(3):
        for j in range(8):
            items.append(("br%d_%d" % (b, j), 512))
    items += [("att0", 4096), ("att1", 4096), ("att2", 8 * 68), ("hig0", 4096), ("hig1", 4096),
              ("wout0", 4096), ("wout1", 4096)]
    for c in range(2):
        for kg, nk in enumerate((8, 8, 6)):
            items.append(("wd%d_%d" % (c, kg), nk * 512))
    off = {}
    o = 0
    for n, w in items:
        off[n] = (o, w)
        o += w
    return items, off, o


def host_weights(w_in, w_branch, w_out, w_up, w_down):
    items, off, tot = weight_layout()
    out = np.empty((2, 128, tot), np.float32)
    for l in range(2):
        parts = []
        for j in range(8):
            parts.append(_tile(w_in[l], O_HQ + 128 * j, 128))
        for j in range(12):
            parts.append(_tile(w_in[l], O_CB + 128 * j, 128))
        for j in range(24):
            parts.append(_tile(w_in[l], O_G + 128 * j, 128))
        for j in range(44):
            parts.append(_tile(w_up[l], 128 * j, 128))
        perm = np.concatenate([np.r_[c * 64:(c + 1) * 64, (4 + c) * 64:(5 + c) * 64] for c in range(4)])
        for b in range(3):
            Wb = w_branch[l, b][perm] if b == 0 else w_branch[l, b]
            for j in range(8):
                parts.append(_tile(Wb, 128 * j, 128))
        parts += [_tile(w_in[l], 0, 512), _tile(w_in[l], 512, 512), _tile(w_in[l], 1024, 68),
                  _tile(w_in[l], O_HI, 512), _tile(w_in[l], O_HG, 512),
                  _tile(w_out[l], 0, 512), _tile(w_out[l], 512, 512)]
        for c in range(2):
            for k0, nk in ((0, 8), (8, 8), (16, 6)):
                parts.append(_tile(w_down[l], 512 * c, 512, k0, nk))
        out[l] = np.concatenate(parts, axis=1)
    return out


def rope_table(pos):
    half = 8
    inv = (500000.0 ** (-np.arange(half, dtype=np.float32) / np.float32(half))).astype(np.float32)
    ang = pos.astype(np.float32)[:, None] * inv[None, :]
    cos = np.cos(ang).astype(np.float32)
    sin = np.sin(ang).astype(np.float32)
    C16 = np.concatenate([cos, cos], axis=1)
    S16 = np.concatenate([-sin, sin], axis=1)
    tab = np.stack([np.tile(C16[:, None, :], (1, 8, 1)), np.tile(S16[:, None, :], (1, 8, 1))], axis=1)
    return np.ascontiguousarray(tab.reshape(len(pos), 256))


def build(cfg):
    S, DS, P = cfg.S, cfg.DS, cfg.P
    LMAX, NKT = cfg.LMAX, cfg.NKT
    nc = bass.Bass("TRN2", target_bir_lowering=False)
    _, woff, WTOT = weight_layout()

    def din(name, shape, dt=F32):
        return nc.dram_tensor(name, list(shape), dt, kind="ExternalInput").ap()

    def dout(name, shape, dt=F32):
        return nc.dram_tensor(name, list(shape), dt, kind="ExternalOutput").ap()

    xp = din("xp", [S, D]); xs = din("xs", [DS, D])
    ck = din("ck", [2, P, 128]); cv = din("cv", [2, P, 128]); cki = din("cki", [2, P, 64])
    sh = din("sh", [2, 4, 128, 128]); ssc = din("ssc", [2, 128, 8]); sff = din("sff", [2, 128, 88])
    wfl = din("wfl", [2, 128, WTOT])
    lbl = din("lbl", [128, 8]); gB_d = din("gB", [2, 128, 512])
    scw_d = din("scw", [2, 128, 16]); fcw_d = din("fcw", [2, 128, 176])
    lnp_d = din("lnp", [2, 128, 4096])
    ropep = din("ropep", [S, 256]); ropes = din("ropes", [DS, 256])

    yp = dout("yp", [S, D]); ys = dout("ys", [DS, D])
    okp = dout("okp", [2, S, 128]); ovp = dout("ovp", [2, S, 128]); okip = dout("okip", [2, S, 64])
    ohp = dout("ohp", [2, 4, 128, 128]); oscp = dout("oscp", [2, 128, 8]); offp = dout("offp", [2, 128, 88])
    oks = dout("oks", [2, DS, 128]); ovs = dout("ovs", [2, DS, 128]); okis = dout("okis", [2, DS, 64])
    ohs = dout("ohs", [2, 4, 128, 128]); oscs = dout("oscs", [2, 128, 8]); offs = dout("offs", [2, 128, 88])

    wbf = nc.dram_tensor("wbf", [2, 128, WTOT], BF16, kind="Internal").ap()
    xmid_p = nc.dram_tensor("xmid_p", [S, D], F32, kind="Internal").ap()
    xmid_s = nc.dram_tensor("xmid_s", [DS, D], F32, kind="Internal").ap()

    es = ExitStack()
    with es:
        k = K(nc, es)
        kT = T(k, "kT", [128, LMAX], BF16)
        kiTp = T(k, "kiTp", [128, LMAX // 2], BF16)
        Vb = T(k, "Vb", [128, NKT, 128], BF16)
        ones_bf = T(k, "ones_bf", [128, 64], BF16)
        AR1 = T(k, "AR1", [128, 8192], F32)
        AR2 = T(k, "AR2", [128, 4096], F32)
        x_sb = T(k, "x_sb", [128, 4, D], F32)
        xs_v = [V(x_sb.t[:, t, :], Buf("xs%d" % t)) for t in range(4)]
        xT = T(k, "xT", [128, 8, 512], BF16)
        wfm = [T(k, "wfm%d" % i, [128, 1024], BF16) for i in range(4)]
        wf4 = [T(k, "wf4%d" % i, [128, 512], BF16) for i in range(2)]
        wtm = [T(k, "wtm%d" % i, [128, 4096], BF16) for i in range(2)]
        wsm = T(k, "wsm", [128, 544], BF16)
        yaT = T(k, "yaT", [128, 4, 512], BF16)
        ybT = T(k, "ybT", [128, 4, 512], BF16)
        ycT = T(k, "ycT", [128, 4, 512], BF16)
        att_f = T(k, "att_f", [128, 1092], F32)
        att_b = T(k, "att_b", [128, 1280], BF16)
        qTs = [T(k, "qT%d" % i, [128, 512], BF16) for i in range(2)]
        junk = T(k, "junk", [128, 4992], mybir.dt.uint8)
        junkA = T(k, "junkA", [128, 3328], mybir.dt.int8)
        smA = T(k, "smA", [128, 4], F32)
        smM = T(k, "smM", [128, 4], F32)
        smC = T(k, "smC", [128, 4], F32)
        smD = T(k, "smD", [128, 4], F32)
        qiT = T(k, "qiT", [128, 512], BF16)
        rr = [T(k, "rr%d" % i, [128, 512], F32) for i in range(2)]
        pT = [T(k, "pT%d" % i, [128, 512], BF16) for i in range(3)]
        rope = T(k, "rope", [128, 256], F32)
        ra = T(k, "ra", [128, 128], F32)
        rb = T(k, "rb", [128, 128], F32)
        xbfs = [T(k, "xbf%d" % i, [128, D], BF16) for i in range(2)]
        sm = T(k, "sm", [128, 64], F32)
        bst = T(k, "bst", [128, 16], F32)
        Sst = T(k, "Sst", [128, 4, 128], F32)
        Sst_v = [V(Sst.t[:, h, :], Buf("Sst%d" % h)) for h in range(4)]
        Sbf = T(k, "Sbf", [128, 128], BF16)
        scm = T(k, "scm", [128, 128], BF16)
        Ktok = T(k, "Ktok", [128, 128], BF16)
        Sbf1 = T(k, "Sbf1", [128, 128], BF16)
        scm1 = T(k, "scm1", [128, 128], BF16)
        Ktok1 = T(k, "Ktok1", [128, 128], BF16)
        uh = T(k, "uh", [128, 8], F32)
        fh = T(k, "fh", [128, 88], F32)
        lb = T(k, "lb", [128, 8], F32)
        oml = T(k, "oml", [128, 8], F32)
        lbraw = T(k, "lbraw", [128, 8], F32)
        gB = T(k, "gB", [128, 512], F32)
        scw = T(k, "scw", [128, 16], F32)
        fcw = T(k, "fcw", [128, 176], F32)
        lnp = T(k, "lnp", [128, 4096], F32)
        ident = T(k, "ident", [128, 128], BF16)
        I4p = T(k, "I4p", [128, 512], BF16)
        I4s = T(k, "I4s", [128, 256], BF16)
        bmask = T(k, "bmask", [128, 128], F32)
        ones = T(k, "ones", [128, 64], F32)
        cst = T(k, "cst", [128, 8], F32)
        psf = [T(k, "psf%d" % i, [128, 512], F32, psum=True) for i in range(6)]
        psb = [T(k, "psb%d" % i, [128, 1024], BF16, psum=True) for i in range(2)]

        def aview(ar, off, n, dt, name):
            ap = ar.t[:, off:off + n]
            if dt == BF16:
                ap = ap.bitcast(BF16)
            return V(ap, Buf(name))

        Sc = V(AR1.t[:, :], Buf("Sc"))
        Mb = V(AR2.t[:, :].bitcast(BF16), Buf("Mb"))
        hs_ = [aview(AR1, 512 * i, 512, F32, "h%d" % i) for i in range(4)]
        sg = aview(AR1, 2048, 2048, F32, "sg")
        vb = aview(AR1, 4096, 1024, BF16, "vb")
        QA = aview(AR1, 5120, 256, BF16, "QA")
        QB = aview(AR1, 5376, 256, BF16, "QB")
        KT = aview(AR1, 5632, 256, BF16, "KT")
        on = aview(AR1, 5888, 512, F32, "on")
        ybb = aview(AR1, 6400, 256, BF16, "ybb")
        E1 = aview(AR1, 6656, 512, F32, "E1")
        QA1 = aview(AR1, 7168, 256, BF16, "QA1")
        QB1 = aview(AR1, 7424, 256, BF16, "QB1")
        KT1 = aview(AR1, 7680, 256, BF16, "KT1")
        gT = aview(AR1, 0, 5632, BF16, "gT")
        ue = [aview(AR1, 5632, 520, F32, "ue0"), aview(AR1, 6152, 520, F32, "ue1")]
        fac = [aview(AR1, 6672, 512, F32, "fac0"), aview(AR1, 7184, 512, F32, "fac1")]
        sa = aview(AR2, 0, 512, F32, "sa")
        mg = aview(AR2, 512, 512, F32, "mg")
        sgm = aview(AR2, 1024, 512, F32, "sgm")
        tmpm = aview(AR2, 1536, 512, F32, "tmpm")
        useA = [Sc, Mb]
        useB = hs_ + [sg, vb, QA, QB, KT, on, ybb, E1, QA1, QB1, KT1]
        useC = [gT] + ue + fac + [sa, mg, sgm, tmpm]

        def claim(new, old):
            deps = {}
            for v in old:
                b = v.buf
                if b.w is not None:
                    deps[b.w[0]] = max(deps.get(b.w[0], 0), b.w[1])
                for kk, vv in b.r.items():
                    deps[kk] = max(deps.get(kk, 0), vv)
            for v in new:
                v.buf.w = None
                v.buf.r = dict(deps)

        chn = {}

        def ch(name):
            if name not in chn:
                chn[name] = k.chan(name)
            return chn[name]

        k.memset("pool", ident[:, :], 1.0)
        k.op("pool", lambda g: g.affine_select(ident.t[:, :], ident.t[:, :], [[1, 128]], ALU.is_equal, 0.0,
                                               base=0, channel_multiplier=-1), r=[ident], w=[ident])
        k.memset("pool", I4s[:, :], 0.0)
        for j in range(4):
            k.copy("pool", I4p[:, j * 128:(j + 1) * 128], ident[:, :])
            k.copy("pool", I4s[0:64, j * 64:(j + 1) * 64], ident[0:64, 0:64])
        k.memset("pool", bmask[:, :], 1.0)
        k.op("pool", lambda g: g.affine_select(bmask.t[:, :], bmask.t[:, :], [[1, 128]], ALU.is_ge, 0.0,
                                               base=0, channel_multiplier=-1), r=[bmask], w=[bmask])
        k.memset("pool", bmask[0:64, 64:128], 0.0)
        k.memset("pool", ones[:, :], 1.0)
        k.memset("pool", cst[:, 0:1], 1.0)
        k.memset("pool", cst[:, 1:2], LN_EPS)
        k.memset("dve", Vb[:, :, :], 0.0)
        k.memset("dve", ones_bf[:, :], 1.0)
        k.memset("pool", kT[:, :], 0.0)
        k.memset("pool", kiTp[:, :], 0.0)
        k.memset("pool", qiT[:, :], 0.0)
        k.memset("dve", QA[:, :], 0.0)
        k.memset("dve", QB[:, :], 0.0)
        k.dma("sp", ch("lbraw"), lbraw[:, :], lbl)
        k.memset("dve", lb[:, 0:4], 0.0)
        k.tt("dve", lb[:, 4:8], lbraw[:, 4:8], lbraw[:, 0:4], ALU.subtract)
        k.act(lb[:, 4:8], lb[:, 4:8], AF.Sigmoid)
        k.ts("dve", oml[:, :], lb[:, :], -1.0, ALU.mult, 1.0, ALU.add)

        wbuf = Buf("wbf")
        NSPL = 16
        step = (WTOT + NSPL - 1) // NSPL
        cw = ch("wcast")
        for l in range(2):
            for i in range(NSPL):
                a, b = i * step, min(WTOT, (i + 1) * step)
                if a >= b:
                    continue
                k.dma("pool", cw, wbf[l, :, a:b], wfl[l, :, a:b], w=[wbuf])

        wq = {"fm": 0, "f4": 0, "tm": 0}

        def loadw(l, name, kind):
            o, w = woff[name]
            if kind == "fm":
                t = wfm[wq["fm"] % 4]; wq["fm"] += 1
            elif kind == "f4":
                t = wf4[wq["f4"] % 2]; wq["f4"] += 1
            elif kind == "tm":
                t = wtm[wq["tm"] % 2]; wq["tm"] += 1
            else:
                t = wsm
            k.dma("sp", ch("w_" + t.buf.name), t[:, 0:w], wbf[l, :, o:o + w], r=[wbuf])
            return t

        rot = {"i": 0, "set": [0, 1, 2, 3]}

        def rbank():
            b = psf[rot["set"][rot["i"] % len(rot["set"])]]
            rot["i"] += 1
            return b

        def make_xT(TP, NT):
            for t in range(NT):
                xbf = xbfs[t % 2]
                k.copy("act", xbf[:TP, :], xs_v[t][:TP, :])
                for half in range(2):
                    pb = psb[half]
                    for c in range(4):
                        k.tr(pb[:, c * 128:c * 128 + TP], xbf[:TP, (half * 4 + c) * 128:(half * 4 + c + 1) * 128],
                             ident[:TP, :TP])
                    src = pb[:, 0:512].re("p (c t) -> p c t", c=4)[:, :, 0:TP]
                    k.copy("act" if half == 0 else "dve", xT[:, half * 4:(half + 1) * 4, t * TP:(t + 1) * TP], src)

        def layer_norm(TP, t, which):
            xv = xs_v[t][:TP, :]
            for c in range(2):
                k.op("dve", lambda g, c=c: g.bn_stats(bst.t[:TP, c * 6:(c + 1) * 6], xs_v[t].ap[:TP, c * 512:(c + 1) * 512]),
                     r=[xs_v[t]], w=[bst])
            k.op("dve", lambda g: g.bn_aggr(sm.t[:TP, 32:34], bst.t[:TP, 0:12]), r=[bst], w=[sm])
            k.act(sm[:TP, 34:35], sm[:TP, 33:34], AF.Sqrt, bias=cst[:TP, 1:2], scale=1.0)
            k.op("dve", lambda g: g.reciprocal(sm.t[:TP, 35:36], sm.t[:TP, 34:35]), r=[sm], w=[sm])
            k.ts("dve", xv, xv, sm[:TP, 32:33], ALU.subtract, sm[:TP, 35:36], ALU.mult)
            k.tt("dve", xv, xv, lnp[:TP, which * 2048:which * 2048 + 1024], ALU.mult)
            k.tt("dve", xv, xv, lnp[:TP, which * 2048 + 1024:which * 2048 + 2048], ALU.add)

        def rotary(TP, ps, out, nh, c16off=0):
            pv = ps.re("p (h d) -> p h d", d=64)
            ov = out.re("p (h d) -> p h d", d=64)
            Cv = rope[:TP, 0:nh * 16].re("p (h d) -> p h d", d=16)
            Sv = rope[:TP, 128:128 + nh * 16].re("p (h d) -> p h d", d=16)
            av = ra[:TP, 0:nh * 16].re("p (h d) -> p h d", d=16)
            bv = rb[:TP, 0:nh * 16].re("p (h d) -> p h d", d=16)
            import os
            rv = os.environ.get("ROTV", "")
            if rv != "noact":
                k.copy("act", ov[:, :, 16:64], pv[:, :, 16:64])
            if rv == "nodve":
                return
            k.tt("dve", av, pv[:, :, 0:16], Cv, ALU.mult)
            k.tt("dve", bv[:, :, 0:8], pv[:, :, 8:16], Sv[:, :, 0:8], ALU.mult)
            k.tt("dve", bv[:, :, 8:16], pv[:, :, 0:8], Sv[:, :, 8:16], ALU.mult)
            k.tt("dve", ov[:, :, 0:16], av, bv, ALU.add)

        def attn_A(l, t, TP, Lprev, masked, rope_src, okd, ovd, okid, row0):
            Lb = Lprev + TP
            kt_new = Lprev // 128
            qTc = qTs[t % 2]
            k.dma("sp", ch("rope"), rope[:TP, :], rope_src)
            w0 = loadw(l, "att0", "tm"); w1 = loadw(l, "att1", "tm"); w2 = loadw(l, "att2", "sm")
            rot["set"] = [0, 1, 2, 3]
            pA, pB, pC = rbank(), rbank(), rbank()
            for (pp, ww, wd) in ((pA, w0, 512), (pB, w1, 512), (pC, w2, 68)):
                for kc in range(8):
                    k.mm(pp[:TP, 0:wd], xT[:, kc, t * TP:(t + 1) * TP], ww[:, kc * wd:(kc + 1) * wd],
                         start=(kc == 0), stop=(kc == 7), inc=(kc == 7))
            rotary(TP, pA[:TP, 0:512], att_f[:TP, 0:512], 8)
            rotary(TP, pB[:TP, 0:512], att_f[:TP, 512:1024], 8)
            k.copy("act", att_f[:TP, 640:768].re("p (h d) -> p h d", d=64)[:, :, 0:16],
                   pB[:TP, 128:256].re("p (h d) -> p h d", d=64)[:, :, 0:16])
            rotary(TP, pC[:TP, 0:64], att_f[:TP, 1024:1088], 1)
            k.copy("act", att_f[:TP, 1088:1092], pC[:TP, 64:68])
            k.dma("pool", ch("st_att"), okd[l, row0:row0 + TP, :], att_f[:TP, 512:640])
            k.dma("pool", ch("st_att"), ovd[l, row0:row0 + TP, :], att_f[:TP, 640:768])
            k.dma("pool", ch("st_att"), okid[l, row0:row0 + TP, :], att_f[:TP, 1024:1088])
            k.copy("pool", att_b[:TP, 0:512].re("p (j g d) -> p j g d", j=4, g=2),
                   att_f[:TP, 0:512].re("p (g j d) -> p j g d", g=2, j=4))
            k.copy("pool", att_b[:TP, 512:640], att_f[:TP, 512:640])
            qdv = att_b[:TP, 640:1152].re("p (h u d) -> p h u d", h=4, u=2)
            qsv = att_f[:TP, 768:1024].re("p (h d) -> p h d", h=4)
            k.copy("pool", qdv[:, :, 0, :], qsv)
            k.copy("pool", qdv[:, :, 1, :], qsv)
            k.copy("pool", att_b[:TP, 1152:1216], att_f[:TP, 1024:1088])
            k.copy("pool", att_b[:TP, 1216:1280], att_f[:TP, 1024:1088])
            k.copy("pool", Vb[:TP, kt_new, :], att_f[:TP, 640:768])
            for j in range(4):
                k.tr(psb[0][:, j * 128:j * 128 + TP], att_b[:TP, j * 128:(j + 1) * 128], ident[:TP, :TP])
            k.act(qTc[:, 0:4 * TP].re("p (j t) -> p j t", j=4),
                  psb[0][:, 0:512].re("p (j t) -> p j t", j=4)[:, :, 0:TP], AF.Copy, scale=0.125)
            k.tr(psb[1][:, 0:TP], att_b[:TP, 512:640], ident[:TP, :TP])
            for h in range(4):
                k.tr(psb[1][:, 128 * (1 + h):128 * (1 + h) + TP], att_b[:TP, 640 + 128 * h:640 + 128 * (h + 1)],
                     ident[:TP, :TP])
            k.tr(psb[1][:, 640:640 + TP], att_b[:TP, 1152:1280], ident[:TP, :TP])
            k.copy("dve", kT[:, Lprev:Lprev + TP], psb[1][:, 0:TP])
            k.copy("act", qiT[:, 0:4 * TP].re("p (j t) -> p j t", j=4),
                   psb[1][:, 128:640].re("p (j t) -> p j t", j=4)[:, :, 0:TP])
            Bn = Lprev // 512
            hf = Bn % 2
            c0 = (Bn // 2) * 512 + (Lprev % 512)
            k.copy("dve", kiTp[hf * 64:hf * 64 + 64, c0:c0 + TP], psb[1][hf * 64:hf * 64 + 64, 640:640 + TP])
            nblk = (Lb + 511) // 512
            for B in range(nblk):
                wB = min(512, Lb - 512 * B)
                hb = B % 2
                cb = (B // 2) * 512
                for h in range(4):
                    pb = rbank()
                    k.mm(pb[:, 0:wB], qiT[hb * 64:hb * 64 + 64, h * TP:h * TP + 128],
                         kiTp[hb * 64:hb * 64 + 64, cb:cb + wB])
                    r_ = rr[(B * 4 + h) % 2]
                    k.act(r_[:TP, 0:wB], pb[:TP, 0:wB], AF.Relu)
                    if h == 0:
                        k.ts("dve", Sc[:TP, 512 * B:512 * B + wB], r_[:TP, 0:wB], att_f[:TP, 1088:1089], ALU.mult)
                    else:
                        k.stt(Sc[:TP, 512 * B:512 * B + wB], r_[:TP, 0:wB], att_f[:TP, 1088 + h:1089 + h],
                              Sc[:TP, 512 * B:512 * B + wB], ALU.mult, ALU.add)
            if masked:
                k.memset("dve", Sc[0:64, Lb - 64:Lb], NEG)
            return dict(t=t, TP=TP, Lb=Lb, qT=qTc)

        def attn_B(c):
            TP, Lb = c["TP"], c["Lb"]
            R = 512.0
            NIT = cfg.NIT
            m, lo = smM[:TP, 0:1], smM[:TP, 1:2]
            cnt, dd, cn2 = smC[:TP, 0:1], smD[:TP, 0:1], smD[:TP, 1:2]
            accA = smA[:TP, 0:1]
            thr = cfg.TOPK - 0.5
            import os as _os
            Ld = Lb if Lb < int(_os.environ.get("SPLIT_MIN", "1536")) else ((int(0.60 * Lb) + 63) // 64) * 64
            nA = Lb - Ld
            nch = 0 if nA == 0 else (1 if nA < int(_os.environ.get("ACT_MIN", "1024")) else int(_os.environ.get("ACT_CH", "2")))
            bounds = []
            if nch:
                stepc = ((nA + nch - 1) // nch + 63) // 64 * 64
                a = Ld
                while a < Lb:
                    bounds.append((a, min(Lb, a + stepc)))
                    a += stepc
            c["ny"] = NIT * (len(bounds) + 1)
            k.memset("dve", m, 0.0)
            for i in range(NIT):
                k.ts("dve", junk[:TP, 0:Ld], Sc[:TP, 0:Ld], m, ALU.is_ge, None, ALU.add, accum=cnt)
                cuse, tuse = cnt, thr
                if nA > 0:
                    for ci_, (a, b) in enumerate(bounds):
                        k.act(junkA[:TP, 0:b - a], Sc[:TP, a:b], AF.Sign, bias=m, scale=-1.0, accum=smA[:TP, ci_:ci_ + 1])
                        yield ("c", i)
                    prev = cnt
                    for ci_ in range(len(bounds)):
                        k.stt(cn2, smA[:TP, ci_:ci_ + 1], -0.5, prev, ALU.mult, ALU.add)
                        prev = cn2
                    cuse, tuse = cn2, thr - 0.5 * nA
                if i < NIT - 1:
                    cn = R / (2.0 ** (i + 1))
                    k.ts("dve", dd, cuse, tuse, ALU.is_ge, 2.0 * cn, ALU.mult)
                    k.stt(m, dd, -cn, m, ALU.add, ALU.add)
                else:
                    ci = R / (2.0 ** i)
                    k.ts("dve", dd, cuse, tuse, ALU.is_lt, -ci, ALU.mult)
                    k.tt("dve", lo, m, dd, ALU.add)
                yield ("i", i)

        def attn_Bfinal(c):
            TP, Lb = c["TP"], c["Lb"]
            lo = smM[:TP, 1:2]
            k.ts("dve", Mb[:TP, 0:Lb], Sc[:TP, 0:Lb], lo, ALU.is_lt, MASKNEG, ALU.mult)
            if TP < 128:
                k.memset("dve", Mb[TP:128, 0:Lb], 0.0)

        def attn_Cmain(c):
            TP, Lb, qTc = c["TP"], c["Lb"], c["qT"]
            I4 = I4p if TP == 128 else I4s
            nkt = (Lb + 127) // 128
            po = [psf[4], psf[5]]
            rot["set"] = [0, 1, 2, 3]

            def qk_pair(kt):
                kw = min(128, Lb - 128 * kt)
                pls = [rbank(), rbank()]
                for g in range(2):
                    k.mm(pls[g][:, 0:4 * TP], kT[g * 64:(g + 1) * 64, kt * 128:kt * 128 + 128],
                         qTc[g * 64:(g + 1) * 64, 0:4 * TP], start=True, stop=True)
                for g in range(2):
                    k.mm(pls[g][:kw, 0:4 * TP], Mb[:, kt * 128:kt * 128 + kw], I4[:, 0:4 * TP], start=False, stop=True,
                         skip_group_check=True)
                return pls

            cur = qk_pair(0)
            for kt in range(nkt):
                kw = min(128, Lb - 128 * kt)
                nxt = qk_pair(kt + 1) if kt + 1 < nkt else None
                ps = []
                for g in range(2):
                    p_ = pT[(2 * kt + g) % 3]
                    k.act(p_[:kw, 0:4 * TP], cur[g][:kw, 0:4 * TP], AF.Exp)
                    ps.append(p_)
                yield kt
                for g in range(2):
                    gs = slice(g * 64, (g + 1) * 64)
                    k.mm(po[0][gs, 0:4 * TP], Vb[:kw, kt, gs], ps[g][:kw, 0:4 * TP], start=(kt == 0), stop=(kt == nkt - 1))
                for g in range(2):
                    gs = slice(g * 64, (g + 1) * 64)
                    k.mm(po[1][gs, 0:4 * TP], ones_bf[:kw, 0:64], ps[g][:kw, 0:4 * TP], start=(kt == 0), stop=(kt == nkt - 1))
                yield kt
                cur = nxt

        def attn_Ctail(c):
            TP, t = c["TP"], c["t"]
            po = [psf[4], psf[5]]
            rd = rr[0]
            k.op("dve", lambda gg: gg.reciprocal(rd.t[:, 0:4 * TP], po[1].t[:, 0:4 * TP]), r=[po[1]], w=[rd])
            k.tt("dve", yaT[:, :, t * TP:(t + 1) * TP], po[0][:, 0:4 * TP].re("p (c q) -> p c q", c=4),
                 rd[:, 0:4 * TP].re("p (c q) -> p c q", c=4), ALU.mult)

        def hgrn(l, TP, NT):
            TT = TP * NT
            nch = TP // 64
            rot["set"] = [0, 1]
            for c, nm in ((0, "hig0"), (1, "hig1")):
                w_ = loadw(l, nm, "tm")
                for t in range(NT):
                    pb = rbank()
                    for kc in range(8):
                        k.mm(pb[:TP, :], xT[:, kc, t * TP:(t + 1) * TP], w_[:, kc * 512:(kc + 1) * 512],
                             start=(kc == 0), stop=(kc == 7), inc=(kc == 7))
                    if c == 0:
                        k.copy("act", vb[:TP, t * 512:(t + 1) * 512], pb[:TP, :])
                    else:
                        k.act(sg[:TP, t * 512:(t + 1) * 512], pb[:TP, :], AF.Silu)
            oacc = [psf[2 + t] for t in range(NT)]
            h0, h1, h2, h3 = hs_
            sets = [dict(E=h0, QA=QA, QB=QB, KT=KT, scm=scm, Ktok=Ktok, Sbf=Sbf, pt=psb[0]),
                    dict(E=E1, QA=QA1, QB=QB1, KT=KT1, scm=scm1, Ktok=Ktok1, Sbf=Sbf1, pt=psb[1])]
            if nch == 2:
                for st_ in sets:
                    k.memset("pool", st_["QA"][:, 0:TT].re("p (t u s) -> p t u s", u=2, s=64)[:, :, 1, :], 0.0)
                    k.memset("pool", st_["QB"][:, 0:TT].re("p (t u s) -> p t u s", u=2, s=64)[:, :, 0, :], 0.0)

            def head_pre(h):
                st_ = sets[h % 2]
                E, QAh, QBh, KTh = st_["E"], st_["QA"], st_["QB"], st_["KT"]
                wq_ = loadw(l, "hqf%d" % h, "fm")
                wf_ = loadw(l, "hqf%d" % (4 + h), "fm")
                zq = rbank()
                for kc in range(8):
                    k.mm(zq[:, 0:TT], wq_[:, kc * 128:(kc + 1) * 128], xT[:, kc, 0:TT], start=(kc == 0), stop=(kc == 7), inc=(kc == 7))
                zf = rbank()
                for kc in range(8):
                    k.mm(zf[:, 0:TT], wf_[:, kc * 128:(kc + 1) * 128], xT[:, kc, 0:TT], start=(kc == 0), stop=(kc == 7), inc=(kc == 7))
                lbc = lb[:, l * 4 + h:l * 4 + h + 1]
                k.act(E[:, 0:TT], zf[:, 0:TT], AF.Exp, scale=-1.0)
                k.act(h1[:, 0:TT], E[:, 0:TT], AF.Ln, bias=cst[:, 0:1], scale=1.0)
                k.act(E[:, 0:TT], E[:, 0:TT], AF.Ln, bias=cst[:, 0:1], scale=lbc)
                k.tt("dve", E[:, 0:TT], E[:, 0:TT], h1[:, 0:TT], ALU.subtract)
                k.act(h2[:, 0:TT], zf[:, 0:TT], AF.Sigmoid, scale=-1.0)
                k.ts("dve", h2[:, 0:TT], h2[:, 0:TT], oml[:, l * 4 + h:l * 4 + h + 1], ALU.mult)
                k.act(h3[:, 0:TT], zq[:, 0:TT], AF.Silu)
                for c in range(TT // 64):
                    k.op("dve", lambda g, c=c: g.tensor_tensor_scan(h1.ap[:, c * 64:(c + 1) * 64], ones.t[:, 0:64],
                                                                   E.ap[:, c * 64:(c + 1) * 64], 0.0, ALU.mult, ALU.add),
                         r=[ones, E], w=[h1])
                k.act(E[:, 0:TT], h1[:, 0:TT], AF.Exp)
                k.act(h1[:, 0:TT], h1[:, 0:TT], AF.Exp, scale=-1.0)
                if nch == 2:
                    qv = h3[:, 0:TT].re("p (t u s) -> p t u s", u=2, s=64)
                    ev = E[:, 0:TT].re("p (t u s) -> p t u s", u=2, s=64)
                    k.tt("dve", QAh[:, 0:TT].re("p (t u s) -> p t u s", u=2, s=64)[:, :, 0, :], qv[:, :, 0, :], ev[:, :, 0, :], ALU.mult)
                    k.tt("dve", QBh[:, 0:TT].re("p (t u s) -> p t u s", u=2, s=64)[:, :, 1, :], qv[:, :, 1, :], ev[:, :, 1, :], ALU.mult)
                else:
                    k.tt("dve", QAh[:, 0:TT], h3[:, 0:TT], E[:, 0:TT], ALU.mult)
                k.tt("dve", KTh[:, 0:TT], h2[:, 0:TT], h1[:, 0:TT], ALU.mult)
                k.copy("pool", st_["Sbf"][:, :], Sst_v[h][:, :])

            def head_chain(h):
                st_ = sets[h % 2]
                E, QAh, QBh, KTh = st_["E"], st_["QA"], st_["QB"], st_["KT"]
                scm_, Ktok_, Sbf_, pt_ = st_["scm"], st_["Ktok"], st_["Sbf"], st_["pt"]
                for t in range(NT):
                    sl = slice(t * TP, (t + 1) * TP)
                    sc = rbank()
                    k.mm(sc[:TP, 0:64], KTh[:, sl], QAh[:, t * TP:t * TP + 64])
                    if nch == 2:
                        k.mm(sc[:TP, 64:128], KTh[:, sl], QBh[:, t * TP + 64:t * TP + 128])
                    k.tt("dve", scm_[:TP, 0:TP], sc[:TP, 0:TP], bmask[:TP, 0:TP], ALU.mult)
                    k.tr(pt_[:TP, 0:128], KTh[:, sl], ident[:, :])
                    k.copy("act", Ktok_[:TP, :], pt_[:TP, 0:128])
                    ob = oacc[t][:TP, h * 128:(h + 1) * 128]
                    for u in range(nch):
                        us = slice(u * 64, (u + 1) * 64)
                        vsl = vb[us, t * 512 + h * 128:t * 512 + (h + 1) * 128]
                        k.mm(ob, scm_[us, 0:TP], vsl, start=(not started[t] and u == 0), stop=False, skip_group_check=True)
                        started[t] = True
                        qsrc = QAh if u == 0 else QBh
                        k.mm(ob, qsrc[:, sl], Sbf_[:, :], start=False, stop=(u == nch - 1), skip_group_check=True)
                        kv = rbank()
                        k.mm(kv[:, 0:128], Ktok_[us, :], vsl)
                        k.tt("dve", Sst_v[h][:, :], kv[:, 0:128], Sst_v[h][:, :], ALU.add)
                        ecol = t * TP + u * 64 + 63
                        k.act(Sst_v[h][:, :], Sst_v[h][:, :], AF.Identity, scale=E[:, ecol:ecol + 1])
                        k.copy("pool", Sbf_[:, :], Sst_v[h][:, :])
                        yield (t, u)

            started = [False] * NT
            for hp in range(2):
                ha, hb_ = 2 * hp, 2 * hp + 1
                head_pre(ha)
                head_pre(hb_)
                ga, gb_ = head_chain(ha), head_chain(hb_)
                for _ in range(NT * nch):
                    next(ga, None)
                    next(gb_, None)
                for _ in ga:
                    pass
                for _ in gb_:
                    pass
            for t in range(NT):
                ob = oacc[t]
                for h in range(4):
                    k.act(on[:TP, 0:128], ob[:TP, h * 128:(h + 1) * 128], AF.Square, accum=sm[:TP, 16 + h:17 + h])
                k.act(sm[:TP, 20:24], sm[:TP, 16:20], AF.Sqrt, bias=cst[:TP, 1:2], scale=1.0 / 128.0)
                k.op("dve", lambda g: g.reciprocal(sm.t[:TP, 24:28], sm.t[:TP, 20:24]), r=[sm], w=[sm])
                for h in range(4):
                    k.ts("dve", on[:TP, h * 128:(h + 1) * 128], ob[:TP, h * 128:(h + 1) * 128], sm[:TP, 24 + h:25 + h], ALU.mult)
                k.tt("dve", on[:TP, :], on[:TP, :], gB[:TP, :], ALU.mult)
                k.tt("dve", ybb[:TP, :], on[:TP, :], sg[:TP, t * 512:(t + 1) * 512], ALU.mult)
                for c in range(4):
                    k.tr(psb[1][:, c * 128:c * 128 + TP], ybb[:TP, c * 128:(c + 1) * 128], ident[:TP, :TP])
                k.copy("act", ybT[:, :, t * TP:(t + 1) * TP], psb[1][:, 0:512].re("p (c t) -> p c t", c=4)[:, :, 0:TP])

        def sconv(l, TT):
            rot["set"] = [0, 1, 2, 3]
            h0, h1, h2, h3 = hs_
            for j in range(4):
                wb_ = loadw(l, "c%d" % j, "fm"); wc_ = loadw(l, "c%d" % (4 + j), "fm"); wx_ = loadw(l, "c%d" % (8 + j), "fm")
                pbk, pck, pxk = rbank(), rbank(), rbank()
                for (pp, ww) in ((pbk, wb_), (pck, wc_), (pxk, wx_)):
                    for kc in range(8):
                        k.mm(pp[:, 0:TT], ww[:, kc * 128:(kc + 1) * 128], xT[:, kc, 0:TT], start=(kc == 0), stop=(kc == 7), inc=(kc == 7))
                uext = V(AR1.t[:, 0:1024], h0.buf)
                k.copy("pool", uext[:, 0:2], uh[:, 2 * j:2 * j + 2])
                k.copy("act", h2[:, 0:TT], pxk[:, 0:TT])
                k.op("dve", lambda g: g.tensor_tensor(uext.ap[:, 2:2 + TT], pck.t[:, 0:TT], h2.ap[:, 0:TT], ALU.mult),
                     r=[pck, h2], w=[h0, h1])
                k.op("pool", lambda g, j=j: g.tensor_copy(uh.t[:, 2 * j:2 * j + 2], uext.ap[:, TT:TT + 2]), r=[h0, h1], w=[uh])
                k.op("dve", lambda g, j=j: g.tensor_scalar(h3.ap[:, 0:TT], uext.ap[:, 2:2 + TT], scw.t[:, 4 * j + 2:4 * j + 3],
                                                          scw.t[:, 4 * j + 3:4 * j + 4], ALU.mult, ALU.add),
                     r=[h0, h1, scw], w=[h3])
                k.op("dve", lambda g, j=j: g.scalar_tensor_tensor(h3.ap[:, 0:TT], uext.ap[:, 1:1 + TT], scw.t[:, 4 * j + 1:4 * j + 2],
                                                                 h3.ap[:, 0:TT], ALU.mult, ALU.add),
                     r=[h0, h1, scw, h3], w=[h3])
                k.op("dve", lambda g, j=j: g.scalar_tensor_tensor(h3.ap[:, 0:TT], uext.ap[:, 0:TT], scw.t[:, 4 * j:4 * j + 1],
                                                                 h3.ap[:, 0:TT], ALU.mult, ALU.add),
                     r=[h0, h1, scw, h3], w=[h3])
                k.tt("dve", ycT[:, j, 0:TT], pbk[:, 0:TT], h3[:, 0:TT], ALU.mult)

        def merge(l, TP, NT):
            TT = TP * NT
            rot["set"] = [0, 1, 2, 3, 4, 5]
            ysrc = [yaT, ybT, ycT]
            for cc in range(8):
                for b in range(3):
                    wg_ = loadw(l, "g%d" % (b * 8 + cc), "fm")
                    wb_ = loadw(l, "br%d_%d" % (b, cc), "f4")
                    gp = rbank()
                    for kc in range(8):
                        k.mm(gp[:, 0:TT], wg_[:, kc * 128:(kc + 1) * 128], xT[:, kc, 0:TT], start=(kc == 0), stop=(kc == 7), inc=(kc == 7))
                    bp = rbank()
                    for kc in range(4):
                        k.mm(bp[:, 0:TT], wb_[:, kc * 128:(kc + 1) * 128], ysrc[b][:, kc, 0:TT], start=(kc == 0), stop=(kc == 3), inc=(kc == 3))
                    k.act(sgm[:, 0:TT], gp[:, 0:TT], AF.Sigmoid)
                    if b == 0:
                        k.tt("dve", mg[:, 0:TT], sgm[:, 0:TT], bp[:, 0:TT], ALU.mult)
                    elif b == 1:
                        k.tt("dve", tmpm[:, 0:TT], sgm[:, 0:TT], bp[:, 0:TT], ALU.mult)
                        k.tt("pool", mg[:, 0:TT], mg[:, 0:TT], tmpm[:, 0:TT], ALU.add)
                    else:
                        k.tt("dve", tmpm[:, 0:TT], sgm[:, 0:TT], bp[:, 0:TT], ALU.mult)
                        k.tt("pool", gT[:, cc * 512:cc * 512 + TT], mg[:, 0:TT], tmpm[:, 0:TT], ALU.add)
            for c in range(2):
                wo_ = loadw(l, "wout%d" % c, "tm")
                for t in range(NT):
                    pb = rbank()
                    for kc in range(8):
                        k.mm(pb[:TP, :], gT[:, kc * 512 + t * TP:kc * 512 + (t + 1) * TP], wo_[:, kc * 512:(kc + 1) * 512],
                             start=(kc == 0), stop=(kc == 7), inc=(kc == 7))
                    xv = xs_v[t][:TP, c * 512:(c + 1) * 512]
                    k.stt(xv, xv, ALPHA, pb[:TP, :], ALU.mult, ALU.add)
            for t in range(NT):
                layer_norm(TP, t, 0)

        def ffn(l, TP, NT):
            TT = TP * NT
            rot["set"] = [0, 1, 2, 3, 4, 5]
            for j in range(22):
                for wi, chn_ in enumerate((j, 22 + j)):
                    w_ = loadw(l, "up%d" % chn_, "fm")
                    hp = rbank()
                    for kc in range(8):
                        k.mm(hp[:, 0:TT], w_[:, kc * 128:(kc + 1) * 128], xT[:, kc, 0:TT], start=(kc == 0), stop=(kc == 7), inc=(kc == 7))
                    u_ = ue[wi]
                    a_ = fac[wi]
                    k.copy("pool", u_[:, 0:2], fh[:, 2 * chn_:2 * chn_ + 2])
                    k.copy("act", u_[:, 2:2 + TT], hp[:, 0:TT])
                    k.copy("pool", fh[:, 2 * chn_:2 * chn_ + 2], u_[:, TT:TT + 2])
                    k.act(a_[:, 0:TT], hp[:, 0:TT], AF.Identity, bias=fcw[:, 4 * chn_ + 3:4 * chn_ + 4],
                          scale=fcw[:, 4 * chn_ + 2:4 * chn_ + 3])
                    k.stt(a_[:, 0:TT], u_[:, 1:1 + TT], fcw[:, 4 * chn_ + 1:4 * chn_ + 2], a_[:, 0:TT], ALU.mult, ALU.add)
                    k.stt(a_[:, 0:TT], u_[:, 0:TT], fcw[:, 4 * chn_:4 * chn_ + 1], a_[:, 0:TT], ALU.mult, ALU.add)
                k.act(sa[:, 0:TT], fac[0][:, 0:TT], AF.Silu)
                k.tt("dve", gT[:, j * 512:j * 512 + TT], sa[:, 0:TT], fac[1][:, 0:TT], ALU.mult)
            acc = [psf[2 + t] for t in range(NT)]
            for c in range(2):
                kc0 = 0
                for kg, nk in enumerate((8, 8, 6)):
                    wd_ = loadw(l, "wd%d_%d" % (c, kg), "tm")
                    for t in range(NT):
                        for kk in range(nk):
                            kc = kc0 + kk
                            k.mm(acc[t][:TP, :], gT[:, kc * 512 + t * TP:kc * 512 + (t + 1) * TP], wd_[:, kk * 512:(kk + 1) * 512],
                                 start=(kc == 0), stop=(kc == 21), inc=(kk == nk - 1))
                    kc0 += nk
                for t in range(NT):
                    xv = xs_v[t][:TP, c * 512:(c + 1) * 512]
                    k.stt(xv, xv, ALPHA, acc[t][:TP, :], ALU.mult, ALU.add)
            for t in range(NT):
                layer_norm(TP, t, 1)

        def run_group(l, grp):
            if grp == "p":
                TP, NT, nmac, Lbase = 128, 4, S // 512, 0
                xsrc = xp if l == 0 else xmid_p
                xdst = xmid_p if l == 0 else yp
                okd, ovd, okid, ohd, oscd, offd, ropd = okp, ovp, okip, ohp, oscp, offp, ropep
                xmb = xmbuf_p
            else:
                TP, NT, nmac, Lbase = 64, 1, 1, P
                xsrc = xs if l == 0 else xmid_s
                xdst = xmid_s if l == 0 else ys
                okd, ovd, okid, ohd, oscd, offd, ropd = oks, ovs, okis, ohs, oscs, offs, ropes
                xmb = xmbuf_s
            TT = TP * NT
            k.dma("sp", ch("gB"), gB[:, :], gB_d[l])
            k.dma("sp", ch("scw"), scw[:, :], scw_d[l])
            k.dma("sp", ch("fcw"), fcw[:, :], fcw_d[l])
            k.dma("sp", ch("lnp"), lnp[:, :], lnp_d[l])
            if grp == "p":
                k.op("pool", lambda g: g.memset(Sst.t[:, :, :], 0.0), r=[], w=[Sst] + Sst_v)
                k.memset("pool", uh[:, :], 0.0)
                k.memset("pool", fh[:, :], 0.0)
            else:
                k.dma("sp", ch("Sst"), Sst[:, :, :], sh[l].rearrange("h d v -> d h v"), w=Sst_v)
                k.dma("sp", ch("uh"), uh[:, :], ssc[l])
                k.dma("sp", ch("fh"), fh[:, :], sff[l])
                claim(useA, useB + useC)
                for kt in range(P // 128):
                    rows = slice(kt * 128, (kt + 1) * 128)
                    k.dma("sp", ch("cst0"), att_f[:, 0:128], ck[l, rows, :])
                    k.dma("sp", ch("cst1"), att_f[:, 128:256], cv[l, rows, :])
                    k.dma("sp", ch("cst2"), att_f[:, 256:320], cki[l, rows, :])
                    k.copy("pool", att_b[:, 0:128], att_f[:, 0:128])
                    k.copy("pool", att_b[:, 128:192], att_f[:, 256:320])
                    k.copy("pool", att_b[:, 192:256], att_f[:, 256:320])
                    k.copy("pool", Vb[:, kt, :], att_f[:, 128:256])
                    k.tr(psb[0][:, 0:128], att_b[:, 0:128], ident[:, :])
                    k.tr(psb[0][:, 128:256], att_b[:, 128:256], ident[:, :])
                    k.copy("dve", kT[:, kt * 128:(kt + 1) * 128], psb[0][:, 0:128])
                    Bn = (kt * 128) // 512
                    hf = Bn % 2
                    c0 = (Bn // 2) * 512 + (kt * 128) % 512
                    k.copy("act", kiTp[hf * 64:hf * 64 + 64, c0:c0 + 128], psb[0][hf * 64:hf * 64 + 64, 128:256])
            import os
            SL = int(os.environ.get("SL", "9"))
            for mt in range(nmac):
                if grp == "s" and SL < 1:
                    break
                tok0 = mt * TT
                claim(useA, useB + useC)
                for t in range(NT):
                    k.dma("sp", ch("x_sb%d" % t), xs_v[t][:TP, :], xsrc[tok0 + t * TP:tok0 + (t + 1) * TP, :],
                          r=[xmb[mt][t]] if l == 1 else [])
                make_xT(TP, NT)
                if cfg.stop >= 2 and not (grp == "s" and SL < 2):
                    prev = None
                    for t in range(NT):
                        row0 = tok0 + t * TP
                        cur = attn_A(l, t, TP, Lbase + row0, grp == "p", ropd[row0:row0 + TP, :], okd, ovd, okid, row0)
                        gb = attn_B(cur)
                        if prev is not None:
                            gc = attn_Cmain(prev)
                            nC = 2 * ((prev["Lb"] + 127) // 128)
                            done_c = 0
                            yi = 0
                            for _ in gb:
                                yi += 1
                                tgt = min(nC, (yi * nC) // max(1, cur.get("ny", cfg.NIT)))
                                while done_c < tgt:
                                    next(gc, None)
                                    done_c += 1
                            for _ in gc:
                                pass
                            attn_Bfinal(cur)
                            attn_Ctail(prev)
                        else:
                            for _ in gb:
                                pass
                            attn_Bfinal(cur)
                        prev = cur
                    for _ in attn_Cmain(prev):
                        pass
                    attn_Ctail(prev)
                claim(useB, useA)
                if cfg.stop >= 3:
                    hgrn(l, TP, NT)
                if cfg.stop >= 4:
                    sconv(l, TT)
                claim(useC, useA + useB)
                if cfg.stop >= 5:
                    merge(l, TP, NT)
                    make_xT(TP, NT)
                if cfg.stop >= 6:
                    ffn(l, TP, NT)
                for t in range(NT):
                    k.dma("pool", ch("st_x%d" % t), xdst[tok0 + t * TP:tok0 + (t + 1) * TP, :], xs_v[t][:TP, :],
                          w=[xmb[mt][t]] if l == 0 else [])
            k.dma("pool", ch("st_S"), ohd[l].rearrange("h d v -> d h v"), Sst[:, :, :], r=Sst_v)
            k.dma("pool", ch("st_uh"), oscd[l], uh[:, :])
            k.dma("pool", ch("st_fh"), offd[l], fh[:, :])

        xmbuf_p = [[Buf("xmp%d_%d" % (i, t)) for t in range(4)] for i in range(S // 512)]
        xmbuf_s = [[Buf("xms")]]
        for grp in ("p", "s"):
            for l in range(2):
                import os as _os
                if _os.environ.get("ONLYS") and not (grp == "s" and l == 0):
                    continue
                if cfg.stop >= 9 or (cfg.stop >= 1 and grp == "p" and l == 0) or (cfg.stop >= 7 and grp == "p") or (cfg.stop >= 8 and l == 0):
                    run_group(l, grp)
        k.wait_all("pool")
        k.wait_all("sp")
        print("instructions:", k.ninst, "channels:", k.nchan, "sbuf_left:", nc.sbuf_bytes_remaining)
    return nc


def kernel_cfg(cfg, inputs, n_cores=8):
    f32 = np.float32
    g = lambda n: np.asarray(inputs[n], dtype=f32)
    x_prompt, x_sample = g("x_prompt"), g("x_sample")
    S, DS, P = cfg.S, cfg.DS, cfg.P
    wfl = host_weights(g("w_in"), g("w_branch"), g("w_out"), g("w_up"), g("w_down"))
    lbl = np.ascontiguousarray(g("hgrn_lb_logits").reshape(2, 4, 128).transpose(2, 0, 1).reshape(128, 8))
    gBv = np.ascontiguousarray(np.broadcast_to(np.tile(g("hgrn_norm_g"), (1, 4))[:, None, :], (2, 128, 512)))
    scw = np.concatenate([g("sconv_w"), g("sconv_b")[:, None, :]], axis=1)
    scw = np.ascontiguousarray(scw.reshape(2, 4, 4, 128).transpose(0, 3, 2, 1).reshape(2, 128, 16))
    fcw = np.concatenate([g("ffn_conv_w"), g("ffn_conv_b")[:, None, :]], axis=1)
    fcw = np.ascontiguousarray(fcw.reshape(2, 4, 44, 128).transpose(0, 3, 2, 1).reshape(2, 128, 176))
    lnp = np.stack([g("ln1_g"), g("ln1_b"), g("ln2_g"), g("ln2_b")], axis=1).reshape(2, 1, 4096)
    lnp = np.ascontiguousarray(np.broadcast_to(lnp, (2, 128, 4096)))
    ropep = rope_table(np.arange(S))
    ropes = rope_table(P + np.arange(DS))
    ck = g("cache_attn_k").reshape(2, -1, P, 128)
    cv = g("cache_attn_v").reshape(2, -1, P, 128)
    cki = g("cache_idx_k")
    sh = g("state_hgrn")
    ssc = g("state_sconv")
    sff = g("state_ffn_conv")
    nb = x_prompt.shape[0]
    in_maps = []
    for c in range(n_cores):
        pb = (c // 2) % nb
        sb = c % x_sample.shape[0]
        ssc_t = np.ascontiguousarray(ssc[:, sb].reshape(2, 2, 4, 128).transpose(0, 3, 2, 1).reshape(2, 128, 8))
        sff_t = np.ascontiguousarray(sff[:, sb].reshape(2, 2, 44, 128).transpose(0, 3, 2, 1).reshape(2, 128, 88))
        in_maps.append({
            "xp": np.ascontiguousarray(x_prompt[pb]), "xs": np.ascontiguousarray(x_sample[sb]),
            "ck": np.ascontiguousarray(ck[:, sb]), "cv": np.ascontiguousarray(cv[:, sb]),
            "cki": np.ascontiguousarray(cki[:, sb]), "sh": np.ascontiguousarray(sh[:, sb]),
            "ssc": ssc_t, "sff": sff_t, "wfl": wfl, "lbl": lbl, "gB": gBv, "scw": scw, "fcw": fcw, "lnp": lnp,
            "ropep": ropep, "ropes": ropes,
        })
    nc = build(cfg)
    res = run_bass_kernel_spmd(nc, in_maps, core_ids=list(range(n_cores)))
    R = res.results
    NB, NSB = x_prompt.shape[0], x_sample.shape[0]
    pc = [2 * b for b in range(NB)]

    def st_t(a, nchk):
        return a.reshape(2, 128, nchk, 2).transpose(0, 3, 2, 1).reshape(2, 2, nchk * 128)

    y_p = np.stack([R[c]["yp"] for c in pc], 0)
    y_s = np.stack([R[c]["ys"] for c in range(NSB)], 0)
    k_p = np.stack([R[c]["okp"] for c in pc], 1).reshape(2, NB, S, 2, 64)
    v_p = np.stack([R[c]["ovp"] for c in pc], 1).reshape(2, NB, S, 2, 64)
    ki_p = np.stack([R[c]["okip"] for c in pc], 1)
    h_p = np.stack([R[c]["ohp"] for c in pc], 1)
    sc_p = np.stack([st_t(R[c]["oscp"], 4) for c in pc], 1)
    ff_p = np.stack([st_t(R[c]["offp"], 44) for c in pc], 1)
    k_s = np.stack([R[c]["oks"] for c in range(NSB)], 1).reshape(2, NSB, DS, 2, 64)
    v_s = np.stack([R[c]["ovs"] for c in range(NSB)], 1).reshape(2, NSB, DS, 2, 64)
    ki_s = np.stack([R[c]["okis"] for c in range(NSB)], 1)
    h_s = np.stack([R[c]["ohs"] for c in range(NSB)], 1)
    sc_s = np.stack([st_t(R[c]["oscs"], 4) for c in range(NSB)], 1)
    ff_s = np.stack([st_t(R[c]["offs"], 44) for c in range(NSB)], 1)
    outs = (y_p, y_s, k_p, v_p, ki_p, h_p, sc_p, ff_p, k_s, v_s, ki_s, h_s, sc_s, ff_s)
    return tuple(np.ascontiguousarray(o, dtype=np.float32) for o in outs)


def kernel(**inputs):
    return kernel_cfg(Cfg(), inputs)
```

```python
import numpy as np
import concourse.bass as bass
import concourse.mybir as mybir
from concourse.bass_utils import run_bass_kernel_spmd
from contextlib import ExitStack

F32 = mybir.dt.float32
BF16 = mybir.dt.bfloat16
AF = mybir.ActivationFunctionType
ALU = mybir.AluOpType


class Buf:
    __slots__ = ("name", "w", "r", "excl")

    def __init__(self, name):
        self.name = name
        self.w = None
        self.r = {}
        self.excl = False


class V:
    __slots__ = ("ap", "buf")

    def __init__(self, ap, buf):
        self.ap = ap
        self.buf = buf

    def __getitem__(self, idx):
        return V(self.ap[idx], self.buf)

    def re(self, pat, **kw):
        return V(self.ap.rearrange(pat, **kw), self.buf)


class T:
    def __init__(self, k, name, shape, dtype, psum=False, buf=None):
        if psum:
            self.t = k.es.enter_context(k.nc.psum_tensor("t_" + name, shape, dtype))
        else:
            self.t = k.es.enter_context(k.nc.sbuf_tensor("t_" + name, shape, dtype))
        self.buf = buf if buf is not None else Buf(name)
        self.buf.excl = bool(psum)

    def __getitem__(self, idx):
        return V(self.t[idx], self.buf)


class K:
    def __init__(self, nc, es):
        self.nc = nc
        self.es = es
        self.eng = {"pe": nc.tensor, "act": nc.scalar, "dve": nc.vector, "pool": nc.gpsimd, "sp": nc.sync}
        self.sem = {}
        self.cnt = {}
        self.seen = {e: {} for e in self.eng}
        self.ekey = {}
        self.eep = {}
        for e in ("pe", "act", "dve", "pool"):
            self.sem[e] = es.enter_context(nc.semaphore("s_" + e))
            self.cnt[e] = 0
            self.ekey[e] = e
            self.eep[e] = 0
        self.nchan = 0
        self.ninst = 0

    def chan(self, name):
        key = "ch_%d_%s" % (self.nchan, name)
        self.nchan += 1
        self.sem[key] = self.es.enter_context(self.nc.semaphore(key))
        self.cnt[key] = 0
        return key

    def _wait(self, e, deps):
        eng = self.eng[e]
        best = {}
        for key, val in deps:
            if e == "pe" and key.startswith("pe"):
                continue
            if best.get(key, 0) < val:
                best[key] = val
        for key, val in best.items():
            if self.seen[e].get(key, 0) < val:
                eng.wait_ge(self.sem[key], val)
                self.seen[e][key] = val

    def op(self, e, fn, r=(), w=(), chan=None, inc=True):
        rb = [x.buf if not isinstance(x, Buf) else x for x in r if x is not None]
        wb = [x.buf if not isinstance(x, Buf) else x for x in w if x is not None]
        wb = wb + [b for b in rb if b.excl]
        rb = [b for b in rb if not b.excl]
        deps = []
        for b in rb:
            if b.w is not None:
                deps.append(b.w)
        for b in wb:
            if b.w is not None:
                deps.append(b.w)
            deps.extend(b.r.items())
        self._wait(e, deps)
        inst = fn(self.eng[e])
        self.ninst += 1
        if chan is None:
            ek = self.ekey[e]
            if self.cnt[ek] >= 30000:
                self.eep[e] += 1
                ek = "%s_%d" % (e, self.eep[e])
                self.ekey[e] = ek
                self.sem[ek] = self.es.enter_context(self.nc.semaphore("s_" + ek))
                self.cnt[ek] = 0
            if inc:
                self.cnt[ek] += 1
                inst.then_inc(self.sem[ek], 1)
                tk = (ek, self.cnt[ek])
            else:
                tk = (ek, self.cnt[ek] + 1)
        else:
            self.cnt[chan] += 16
            inst.then_inc(self.sem[chan], 16)
            tk = (chan, self.cnt[chan])
        for b in rb:
            if b.r.get(tk[0], 0) < tk[1]:
                b.r[tk[0]] = tk[1]
        for b in wb:
            b.w = tk
            b.r = {}
        return inst

    def mm(self, o, lhsT, rhs, start=True, stop=True, extra_r=(), inc=True, **kw):
        return self.op("pe", lambda g: g.matmul(o.ap, lhsT.ap, rhs.ap, start=start, stop=stop, **kw),
                       r=[lhsT, rhs] + list(extra_r), w=[o], inc=inc)

    def tr(self, o, in_, ident):
        return self.op("pe", lambda g: g.transpose(o.ap, in_.ap, ident.ap), r=[in_, ident], w=[o])

    def act(self, o, in_, func, bias=None, scale=None, accum=None, e="act"):
        kw = {}
        rr = [in_]
        if bias is not None:
            if isinstance(bias, V):
                kw["bias"] = bias.ap
                rr.append(bias)
            else:
                kw["bias"] = bias
        if scale is not None:
            if isinstance(scale, V):
                kw["scale"] = scale.ap
                rr.append(scale)
            else:
                kw["scale"] = scale
        ww = [o]
        if accum is not None:
            kw["accum_out"] = accum.ap
            ww.append(accum)
        return self.op("act", lambda g: g.activation(o.ap, in_.ap, func, **kw), r=rr, w=ww)

    def tt(self, e, o, a, b, op):
        return self.op(e, lambda g: g.tensor_tensor(o.ap, a.ap, b.ap, op), r=[a, b], w=[o])

    def ts(self, e, o, a, s1, op0, s2=None, op1=None, accum=None):
        rr = [a]
        v1 = s1
        v2 = s2
        if isinstance(s1, V):
            rr.append(s1)
            v1 = s1.ap
        if isinstance(s2, V):
            rr.append(s2)
            v2 = s2.ap
        ww = [o]
        kw = {}
        if accum is not None:
            kw["accum_out"] = accum.ap
            ww.append(accum)
        if op1 is None:
            return self.op(e, lambda g: g.tensor_scalar(o.ap, a.ap, v1, None, op0, **kw), r=rr, w=ww)
        return self.op(e, lambda g: g.tensor_scalar(o.ap, a.ap, v1, v2, op0, op1, **kw), r=rr, w=ww)

    def stt(self, o, a, s, b, op0, op1):
        rr = [a, b]
        v = s
        if isinstance(s, V):
            rr.append(s)
            v = s.ap
        return self.op("dve", lambda g: g.scalar_tensor_tensor(o.ap, a.ap, v, b.ap, op0, op1), r=rr, w=[o])

    def copy(self, e, o, a):
        if e == "act":
            return self.op(e, lambda g: g.copy(o.ap, a.ap), r=[a], w=[o])
        return self.op(e, lambda g: g.tensor_copy(o.ap, a.ap), r=[a], w=[o])

    def memset(self, e, o, val):
        return self.op(e, lambda g: g.memset(o.ap, val), r=[], w=[o])

    def dma(self, e, chan, o, i, r=(), w=(), **kw):
        oa = o.ap if isinstance(o, V) else o
        ia = i.ap if isinstance(i, V) else i
        rr = list(r) + ([i] if isinstance(i, V) else [])
        ww = list(w) + ([o] if isinstance(o, V) else [])
        return self.op(e, lambda g: g.dma_start(out=oa, in_=ia, **kw), r=rr, w=ww, chan=chan)

    def wait_all(self, e):
        deps = [(key, c) for key, c in self.cnt.items() if c > 0]
        self._wait(e, deps)

D = 1024
DFF = 2816
ALPHA = 4.0 ** 0.25
LN_EPS = 1e-5
NEG = -1e30
MASKNEG = -30000.0
O_HQ, O_HF, O_HI, O_HG, O_CB, O_CC, O_CX, O_G = 1092, 1604, 2116, 2628, 3140, 3652, 4164, 4676


class Cfg:
    def __init__(self, S=8192, DS=64, P=2048, TOPK=256, NIT=28, stop=9):
        self.S, self.DS, self.P, self.TOPK, self.NIT = S, DS, P, TOPK, NIT
        self.stop = stop
        self.sub = 9
        self.LMAX = max(S, P + DS)
        self.LMAX = ((self.LMAX + 1023) // 1024) * 1024
        self.NKT = self.LMAX // 128


def _tile(W, c0, width, k0=0, nk=None):
    K = W.shape[0]
    if nk is None:
        nk = K // 128
    sub = W[k0 * 128:(k0 + nk) * 128, c0:c0 + width]
    return np.ascontiguousarray(sub.reshape(nk, 128, width).transpose(1, 0, 2).reshape(128, nk * width))


def weight_layout():
    items = []
    for j in range(8):
        items.append(("hqf%d" % j, 1024))
    for j in range(12):
        items.append(("c%d" % j, 1024))
    for j in range(24):
        items.append(("g%d" % j, 1024))
    for j in range(44):
        items.append(("up%d" % j, 1024))
    for b in range(3):
        for j in range(8):
            items.append(("br%d_%d" % (b, j), 512))
    items += [("att0", 4096), ("att1", 4096), ("att2", 8 * 68), ("hig0", 4096), ("hig1", 4096),
              ("wout0", 4096), ("wout1", 4096)]
    for c in range(2):
        for kg, nk in enumerate((8, 8, 6)):
            items.append(("wd%d_%d" % (c, kg), nk * 512))
    off = {}
    o = 0
    for n, w in items:
        off[n] = (o, w)
        o += w
    return items, off, o


def host_weights(w_in, w_branch, w_out, w_up, w_down):
    items, off, tot = weight_layout()
    out = np.empty((2, 128, tot), np.float32)
    for l in range(2):
        parts = []
        for j in range(8):
            parts.append(_tile(w_in[l], O_HQ + 128 * j, 128))
        for j in range(12):
            parts.append(_tile(w_in[l], O_CB + 128 * j, 128))
        for j in range(24):
            parts.append(_tile(w_in[l], O_G + 128 * j, 128))
        for j in range(44):
            parts.append(_tile(w_up[l], 128 * j, 128))
        perm = np.concatenate([np.r_[c * 64:(c + 1) * 64, (4 + c) * 64:(5 + c) * 64] for c in range(4)])
        for b in range(3):
            Wb = w_branch[l, b][perm] if b == 0 else w_branch[l, b]
            for j in range(8):
                parts.append(_tile(Wb, 128 * j, 128))
        parts += [_tile(w_in[l], 0, 512), _tile(w_in[l], 512, 512), _tile(w_in[l], 1024, 68),
                  _tile(w_in[l], O_HI, 512), _tile(w_in[l], O_HG, 512),
                  _tile(w_out[l], 0, 512), _tile(w_out[l], 512, 512)]
        for c in range(2):
            for k0, nk in ((0, 8), (8, 8), (16, 6)):
                parts.append(_tile(w_down[l], 512 * c, 512, k0, nk))
        out[l] = np.concatenate(parts, axis=1)
    return out


def rope_table(pos):
    half = 8
    inv = (500000.0 ** (-np.arange(half, dtype=np.float32) / np.float32(half))).astype(np.float32)
    ang = pos.astype(np.float32)[:, None] * inv[None, :]
    cos = np.cos(ang).astype(np.float32)
    sin = np.sin(ang).astype(np.float32)
    C16 = np.concatenate([cos, cos], axis=1)
    S16 = np.concatenate([-sin, sin], axis=1)
    tab = np.stack([np.tile(C16[:, None, :], (1, 8, 1)), np.tile(S16[:, None, :], (1, 8, 1))], axis=1)
    return np.ascontiguousarray(tab.reshape(len(pos), 256))


def build(cfg):
    S, DS, P = cfg.S, cfg.DS, cfg.P
    LMAX, NKT = cfg.LMAX, cfg.NKT
    nc = bass.Bass("TRN2", target_bir_lowering=False)
    _, woff, WTOT = weight_layout()

    def din(name, shape, dt=F32):
        return nc.dram_tensor(name, list(shape), dt, kind="ExternalInput").ap()

    def dout(name, shape, dt=F32):
        return nc.dram_tensor(name, list(shape), dt, kind="ExternalOutput").ap()

    xp = din("xp", [S, D]); xs = din("xs", [DS, D])
    ck = din("ck", [2, P, 128]); cv = din("cv", [2, P, 128]); cki = din("cki", [2, P, 64])
    sh = din("sh", [2, 4, 128, 128]); ssc = din("ssc", [2, 128, 8]); sff = din("sff", [2, 128, 88])
    wfl = din("wfl", [2, 128, WTOT])
    lbl = din("lbl", [128, 8]); gB_d = din("gB", [2, 128, 512])
    scw_d = din("scw", [2, 128, 16]); fcw_d = din("fcw", [2, 128, 176])
    lnp_d = din("lnp", [2, 128, 4096])
    ropep = din("ropep", [S, 256]); ropes = din("ropes", [DS, 256])

    yp = dout("yp", [S, D]); ys = dout("ys", [DS, D])
    okp = dout("okp", [2, S, 128]); ovp = dout("ovp", [2, S, 128]); okip = dout("okip", [2, S, 64])
    ohp = dout("ohp", [2, 4, 128, 128]); oscp = dout("oscp", [2, 128, 8]); offp = dout("offp", [2, 128, 88])
    oks = dout("oks", [2, DS, 128]); ovs = dout("ovs", [2, DS, 128]); okis = dout("okis", [2, DS, 64])
    ohs = dout("ohs", [2, 4, 128, 128]); oscs = dout("oscs", [2, 128, 8]); offs = dout("offs", [2, 128, 88])

    wbf = nc.dram_tensor("wbf", [2, 128, WTOT], BF16, kind="Internal").ap()
    xmid_p = nc.dram_tensor("xmid_p", [S, D], F32, kind="Internal").ap()
    xmid_s = nc.dram_tensor("xmid_s", [DS, D], F32, kind="Internal").ap()

    es = ExitStack()
    with es:
        k = K(nc, es)
        kT = T(k, "kT", [128, LMAX], BF16)
        kiTp = T(k, "kiTp", [128, LMAX // 2], BF16)
        Vb = T(k, "Vb", [128, NKT, 128], BF16)
        ones_bf = T(k, "ones_bf", [128, 64], BF16)
        AR1 = T(k, "AR1", [128, 8192], F32)
        AR2 = T(k, "AR2", [128, 4096], F32)
        x_sb = T(k, "x_sb", [128, 4, D], F32)
        xs_v = [V(x_sb.t[:, t, :], Buf("xs%d" % t)) for t in range(4)]
        xT = T(k, "xT", [128, 8, 512], BF16)
        wfm = [T(k, "wfm%d" % i, [128, 1024], BF16) for i in range(4)]
        wf4 = [T(k, "wf4%d" % i, [128, 512], BF16) for i in range(2)]
        wtm = [T(k, "wtm%d" % i, [128, 4096], BF16) for i in range(2)]
        wsm = T(k, "wsm", [128, 544], BF16)
        yaT = T(k, "yaT", [128, 4, 512], BF16)
        ybT = T(k, "ybT", [128, 4, 512], BF16)
        ycT = T(k, "ycT", [128, 4, 512], BF16)
        att_f = T(k, "att_f", [128, 1092], F32)
        att_b = T(k, "att_b", [128, 1280], BF16)
        qTs = [T(k, "qT%d" % i, [128, 512], BF16) for i in range(2)]
        junk = T(k, "junk", [128, 4992], mybir.dt.uint8)
        junkA = T(k, "junkA", [128, 3328], mybir.dt.int8)
        smA = T(k, "smA", [128, 4], F32)
        smM = T(k, "smM", [128, 4], F32)
        smC = T(k, "smC", [128, 4], F32)
        smD = T(k, "smD", [128, 4], F32)
        qiT = T(k, "qiT", [128, 512], BF16)
        rr = [T(k, "rr%d" % i, [128, 512], F32) for i in range(2)]
        pT = [T(k, "pT%d" % i, [128, 512], BF16) for i in range(3)]
        rope = T(k, "rope", [128, 256], F32)
        ra = T(k, "ra", [128, 128], F32)
        rb = T(k, "rb", [128, 128], F32)
        xbfs = [T(k, "xbf%d" % i, [128, D], BF16) for i in range(2)]
        sm = T(k, "sm", [128, 64], F32)
        bst = T(k, "bst", [128, 16], F32)
        Sst = T(k, "Sst", [128, 4, 128], F32)
        Sst_v = [V(Sst.t[:, h, :], Buf("Sst%d" % h)) for h in range(4)]
        Sbf = T(k, "Sbf", [128, 128], BF16)
        scm = T(k, "scm", [128, 128], BF16)
        Ktok = T(k, "Ktok", [128, 128], BF16)
        Sbf1 = T(k, "Sbf1", [128, 128], BF16)
        scm1 = T(k, "scm1", [128, 128], BF16)
        Ktok1 = T(k, "Ktok1", [128, 128], BF16)
        uh = T(k, "uh", [128, 8], F32)
        fh = T(k, "fh", [128, 88], F32)
        lb = T(k, "lb", [128, 8], F32)
        oml = T(k, "oml", [128, 8], F32)
        lbraw = T(k, "lbraw", [128, 8], F32)
        gB = T(k, "gB", [128, 512], F32)
        scw = T(k, "scw", [128, 16], F32)
        fcw = T(k, "fcw", [128, 176], F32)
        lnp = T(k, "lnp", [128, 4096], F32)
        ident = T(k, "ident", [128, 128], BF16)
        I4p = T(k, "I4p", [128, 512], BF16)
        I4s = T(k, "I4s", [128, 256], BF16)
        bmask = T(k, "bmask", [128, 128], F32)
        ones = T(k, "ones", [128, 64], F32)
        cst = T(k, "cst", [128, 8], F32)
        psf = [T(k, "psf%d" % i, [128, 512], F32, psum=True) for i in range(6)]
        psb = [T(k, "psb%d" % i, [128, 1024], BF16, psum=True) for i in range(2)]

        def aview(ar, off, n, dt, name):
            ap = ar.t[:, off:off + n]
            if dt == BF16:
                ap = ap.bitcast(BF16)
            return V(ap, Buf(name))

        Sc = V(AR1.t[:, :], Buf("Sc"))
        Mb = V(AR2.t[:, :].bitcast(BF16), Buf("Mb"))
        hs_ = [aview(AR1, 512 * i, 512, F32, "h%d" % i) for i in range(4)]
        sg = aview(AR1, 2048, 2048, F32, "sg")
        vb = aview(AR1, 4096, 1024, BF16, "vb")
        QA = aview(AR1, 5120, 256, BF16, "QA")
        QB = aview(AR1, 5376, 256, BF16, "QB")
        KT = aview(AR1, 5632, 256, BF16, "KT")
        on = aview(AR1, 5888, 512, F32, "on")
        ybb = aview(AR1, 6400, 256, BF16, "ybb")
        E1 = aview(AR1, 6656, 512, F32, "E1")
        QA1 = aview(AR1, 7168, 256, BF16, "QA1")
        QB1 = aview(AR1, 7424, 256, BF16, "QB1")
        KT1 = aview(AR1, 7680, 256, BF16, "KT1")
        gT = aview(AR1, 0, 5632, BF16, "gT")
        ue = [aview(AR1, 5632, 520, F32, "ue0"), aview(AR1, 6152, 520, F32, "ue1")]
        fac = [aview(AR1, 6672, 512, F32, "fac0"), aview(AR1, 7184, 512, F32, "fac1")]
        sa = aview(AR2, 0, 512, F32, "sa")
        mg = aview(AR2, 512, 512, F32, "mg")
        sgm = aview(AR2, 1024, 512, F32, "sgm")
        tmpm = aview(AR2, 1536, 512, F32, "tmpm")
        useA = [Sc, Mb]
        useB = hs_ + [sg, vb, QA, QB, KT, on, ybb, E1, QA1, QB1, KT1]
        useC = [gT] + ue + fac + [sa, mg, sgm, tmpm]

        def claim(new, old):
            deps = {}
            for v in old:
                b = v.buf
                if b.w is not None:
                    deps[b.w[0]] = max(deps.get(b.w[0], 0), b.w[1])
                for kk, vv in b.r.items():
                    deps[kk] = max(deps.get(kk, 0), vv)
            for v in new:
                v.buf.w = None
                v.buf.r = dict(deps)

        chn = {}

        def ch(name):
            if name not in chn:
                chn[name] = k.chan(name)
            return chn[name]

        k.memset("pool", ident[:, :], 1.0)
        k.op("pool", lambda g: g.affine_select(ident.t[:, :], ident.t[:, :], [[1, 128]], ALU.is_equal, 0.0,
                                               base=0, channel_multiplier=-1), r=[ident], w=[ident])
        k.memset("pool", I4s[:, :], 0.0)
        for j in range(4):
            k.copy("pool", I4p[:, j * 128:(j + 1) * 128], ident[:, :])
            k.copy("pool", I4s[0:64, j * 64:(j + 1) * 64], ident[0:64, 0:64])
        k.memset("pool", bmask[:, :], 1.0)
        k.op("pool", lambda g: g.affine_select(bmask.t[:, :], bmask.t[:, :], [[1, 128]], ALU.is_ge, 0.0,
                                               base=0, channel_multiplier=-1), r=[bmask], w=[bmask])
        k.memset("pool", bmask[0:64, 64:128], 0.0)
        k.memset("pool", ones[:, :], 1.0)
        k.memset("pool", cst[:, 0:1], 1.0)
        k.memset("pool", cst[:, 1:2], LN_EPS)
        k.memset("dve", Vb[:, :, :], 0.0)
        k.memset("dve", ones_bf[:, :], 1.0)
        k.memset("pool", kT[:, :], 0.0)
        k.memset("pool", kiTp[:, :], 0.0)
        k.memset("pool", qiT[:, :], 0.0)
        k.memset("dve", QA[:, :], 0.0)
        k.memset("dve", QB[:, :], 0.0)
        k.dma("sp", ch("lbraw"), lbraw[:, :], lbl)
        k.memset("dve", lb[:, 0:4], 0.0)
        k.tt("dve", lb[:, 4:8], lbraw[:, 4:8], lbraw[:, 0:4], ALU.subtract)
        k.act(lb[:, 4:8], lb[:, 4:8], AF.Sigmoid)
        k.ts("dve", oml[:, :], lb[:, :], -1.0, ALU.mult, 1.0, ALU.add)

        wbuf = [Buf("wbf0"), Buf("wbf1")]
        NSPL = 16
        step = (WTOT + NSPL - 1) // NSPL
        cw = [ch("wcast0"), ch("wcast1")]
        for l in range(2):
            for i in range(NSPL):
                a, b = i * step, min(WTOT, (i + 1) * step)
                if a >= b:
                    continue
                k.dma("pool", cw[l], wbf[l, :, a:b], wfl[l, :, a:b], w=[wbuf[l]])

        wq = {"fm": 0, "f4": 0, "tm": 0}

        def loadw(l, name, kind):
            o, w = woff[name]
            if kind == "fm":
                t = wfm[wq["fm"] % 4]; wq["fm"] += 1
            elif kind == "f4":
                t = wf4[wq["f4"] % 2]; wq["f4"] += 1
            elif kind == "tm":
                t = wtm[wq["tm"] % 2]; wq["tm"] += 1
            else:
                t = wsm
            k.dma("sp", ch("w_" + t.buf.name), t[:, 0:w], wbf[l, :, o:o + w], r=[wbuf[l]])
            return t

        rot = {"i": 0, "set": [0, 1, 2, 3]}

        def rbank():
            b = psf[rot["set"][rot["i"] % len(rot["set"])]]
            rot["i"] += 1
            return b

        def make_xT(TP, NT):
            for t in range(NT):
                xbf = xbfs[t % 2]
                k.copy("act", xbf[:TP, :], xs_v[t][:TP, :])
                for half in range(2):
                    pb = psb[half]
                    for c in range(4):
                        k.tr(pb[:, c * 128:c * 128 + TP], xbf[:TP, (half * 4 + c) * 128:(half * 4 + c + 1) * 128],
                             ident[:TP, :TP])
                    src = pb[:, 0:512].re("p (c t) -> p c t", c=4)[:, :, 0:TP]
                    k.copy("act" if half == 0 else "dve", xT[:, half * 4:(half + 1) * 4, t * TP:(t + 1) * TP], src)

        def layer_norm(TP, t, which):
            xv = xs_v[t][:TP, :]
            for c in range(2):
                k.op("dve", lambda g, c=c: g.bn_stats(bst.t[:TP, c * 6:(c + 1) * 6], xs_v[t].ap[:TP, c * 512:(c + 1) * 512]),
                     r=[xs_v[t]], w=[bst])
            k.op("dve", lambda g: g.bn_aggr(sm.t[:TP, 32:34], bst.t[:TP, 0:12]), r=[bst], w=[sm])
            k.act(sm[:TP, 34:35], sm[:TP, 33:34], AF.Sqrt, bias=cst[:TP, 1:2], scale=1.0)
            k.op("dve", lambda g: g.reciprocal(sm.t[:TP, 35:36], sm.t[:TP, 34:35]), r=[sm], w=[sm])
            k.ts("dve", xv, xv, sm[:TP, 32:33], ALU.subtract, sm[:TP, 35:36], ALU.mult)
            k.tt("dve", xv, xv, lnp[:TP, which * 2048:which * 2048 + 1024], ALU.mult)
            k.tt("dve", xv, xv, lnp[:TP, which * 2048 + 1024:which * 2048 + 2048], ALU.add)

        def rotary(TP, ps, out, nh, c16off=0):
            pv = ps.re("p (h d) -> p h d", d=64)
            ov = out.re("p (h d) -> p h d", d=64)
            Cv = rope[:TP, 0:nh * 16].re("p (h d) -> p h d", d=16)
            Sv = rope[:TP, 128:128 + nh * 16].re("p (h d) -> p h d", d=16)
            av = ra[:TP, 0:nh * 16].re("p (h d) -> p h d", d=16)
            bv = rb[:TP, 0:nh * 16].re("p (h d) -> p h d", d=16)
            import os
            rv = os.environ.get("ROTV", "")
            if rv != "noact":
                k.copy("act", ov[:, :, 16:64], pv[:, :, 16:64])
            if rv == "nodve":
                return
            k.tt("dve", av, pv[:, :, 0:16], Cv, ALU.mult)
            k.tt("dve", bv[:, :, 0:8], pv[:, :, 8:16], Sv[:, :, 0:8], ALU.mult)
            k.tt("dve", bv[:, :, 8:16], pv[:, :, 0:8], Sv[:, :, 8:16], ALU.mult)
            k.tt("dve", ov[:, :, 0:16], av, bv, ALU.add)

        def attn_A(l, t, TP, Lprev, masked, rope_src, okd, ovd, okid, row0):
            Lb = Lprev + TP
            kt_new = Lprev // 128
            qTc = qTs[t % 2]
            k.dma("sp", ch("rope"), rope[:TP, :], rope_src)
            w0 = loadw(l, "att0", "tm"); w1 = loadw(l, "att1", "tm"); w2 = loadw(l, "att2", "sm")
            rot["set"] = [0, 1, 2, 3]
            pA, pB, pC = rbank(), rbank(), rbank()
            for (pp, ww, wd) in ((pA, w0, 512), (pB, w1, 512), (pC, w2, 68)):
                for kc in range(8):
                    k.mm(pp[:TP, 0:wd], xT[:, kc, t * TP:(t + 1) * TP], ww[:, kc * wd:(kc + 1) * wd],
                         start=(kc == 0), stop=(kc == 7), inc=(kc == 7))
            rotary(TP, pA[:TP, 0:512], att_f[:TP, 0:512], 8)
            rotary(TP, pB[:TP, 0:512], att_f[:TP, 512:1024], 8)
            k.copy("act", att_f[:TP, 640:768].re("p (h d) -> p h d", d=64)[:, :, 0:16],
                   pB[:TP, 128:256].re("p (h d) -> p h d", d=64)[:, :, 0:16])
            rotary(TP, pC[:TP, 0:64], att_f[:TP, 1024:1088], 1)
            k.copy("act", att_f[:TP, 1088:1092], pC[:TP, 64:68])
            k.dma("pool", ch("st_att"), okd[l, row0:row0 + TP, :], att_f[:TP, 512:640])
            k.dma("pool", ch("st_att"), ovd[l, row0:row0 + TP, :], att_f[:TP, 640:768])
            k.dma("pool", ch("st_att"), okid[l, row0:row0 + TP, :], att_f[:TP, 1024:1088])
            k.copy("pool", att_b[:TP, 0:512].re("p (j g d) -> p j g d", j=4, g=2),
                   att_f[:TP, 0:512].re("p (g j d) -> p j g d", g=2, j=4))
            k.copy("pool", att_b[:TP, 512:640], att_f[:TP, 512:640])
            qdv = att_b[:TP, 640:1152].re("p (h u d) -> p h u d", h=4, u=2)
            qsv = att_f[:TP, 768:1024].re("p (h d) -> p h d", h=4)
            k.copy("pool", qdv[:, :, 0, :], qsv)
            k.copy("pool", qdv[:, :, 1, :], qsv)
            k.copy("pool", att_b[:TP, 1152:1216], att_f[:TP, 1024:1088])
            k.copy("pool", att_b[:TP, 1216:1280], att_f[:TP, 1024:1088])
            k.copy("pool", Vb[:TP, kt_new, :], att_f[:TP, 640:768])
            for j in range(4):
                k.tr(psb[0][:, j * 128:j * 128 + TP], att_b[:TP, j * 128:(j + 1) * 128], ident[:TP, :TP])
            k.act(qTc[:, 0:4 * TP].re("p (j t) -> p j t", j=4),
                  psb[0][:, 0:512].re("p (j t) -> p j t", j=4)[:, :, 0:TP], AF.Copy, scale=0.125)
            k.tr(psb[1][:, 0:TP], att_b[:TP, 512:640], ident[:TP, :TP])
            for h in range(4):
                k.tr(psb[1][:, 128 * (1 + h):128 * (1 + h) + TP], att_b[:TP, 640 + 128 * h:640 + 128 * (h + 1)],
                     ident[:TP, :TP])
            k.tr(psb[1][:, 640:640 + TP], att_b[:TP, 1152:1280], ident[:TP, :TP])
            k.copy("dve", kT[:, Lprev:Lprev + TP], psb[1][:, 0:TP])
            k.copy("act", qiT[:, 0:4 * TP].re("p (j t) -> p j t", j=4),
                   psb[1][:, 128:640].re("p (j t) -> p j t", j=4)[:, :, 0:TP])
            Bn = Lprev // 512
            hf = Bn % 2
            c0 = (Bn // 2) * 512 + (Lprev % 512)
            k.copy("dve", kiTp[hf * 64:hf * 64 + 64, c0:c0 + TP], psb[1][hf * 64:hf * 64 + 64, 640:640 + TP])
            nblk = (Lb + 511) // 512
            for B in range(nblk):
                wB = min(512, Lb - 512 * B)
                hb = B % 2
                cb = (B // 2) * 512
                for h in range(4):
                    pb = rbank()
                    k.mm(pb[:, 0:wB], qiT[hb * 64:hb * 64 + 64, h * TP:h * TP + 128],
                         kiTp[hb * 64:hb * 64 + 64, cb:cb + wB])
                    r_ = rr[(B * 4 + h) % 2]
                    k.act(r_[:TP, 0:wB], pb[:TP, 0:wB], AF.Relu)
                    if h == 0:
                        k.ts("dve", Sc[:TP, 512 * B:512 * B + wB], r_[:TP, 0:wB], att_f[:TP, 1088:1089], ALU.mult)
                    else:
                        k.stt(Sc[:TP, 512 * B:512 * B + wB], r_[:TP, 0:wB], att_f[:TP, 1088 + h:1089 + h],
                              Sc[:TP, 512 * B:512 * B + wB], ALU.mult, ALU.add)
            if masked:
                k.memset("dve", Sc[0:64, Lb - 64:Lb], NEG)
            return dict(t=t, TP=TP, Lb=Lb, qT=qTc)

        def attn_B(c):
            TP, Lb = c["TP"], c["Lb"]
            R = 512.0
            NIT = cfg.NIT
            m, lo = smM[:TP, 0:1], smM[:TP, 1:2]
            cnt, dd, cn2 = smC[:TP, 0:1], smD[:TP, 0:1], smD[:TP, 1:2]
            accA = smA[:TP, 0:1]
            thr = cfg.TOPK - 0.5
            import os as _os
            Ld = Lb if Lb < int(_os.environ.get("SPLIT_MIN", "1536")) else ((int(0.60 * Lb) + 63) // 64) * 64
            nA = Lb - Ld
            nch = 0 if nA == 0 else (1 if nA < int(_os.environ.get("ACT_MIN", "1024")) else int(_os.environ.get("ACT_CH", "1")))
            bounds = []
            if nch:
                stepc = ((nA + nch - 1) // nch + 63) // 64 * 64
                a = Ld
                while a < Lb:
                    bounds.append((a, min(Lb, a + stepc)))
                    a += stepc
            c["ny"] = NIT * (len(bounds) + 1)
            k.memset("dve", m, 0.0)
            for i in range(NIT):
                k.ts("dve", junk[:TP, 0:Ld], Sc[:TP, 0:Ld], m, ALU.is_ge, None, ALU.add, accum=cnt)
                cuse, tuse = cnt, thr
                if nA > 0:
                    for ci_, (a, b) in enumerate(bounds):
                        k.act(junkA[:TP, 0:b - a], Sc[:TP, a:b], AF.Sign, bias=m, scale=-1.0, accum=smA[:TP, ci_:ci_ + 1])
                        yield ("c", i)
                    prev = cnt
                    for ci_ in range(len(bounds)):
                        k.stt(cn2, smA[:TP, ci_:ci_ + 1], -0.5, prev, ALU.mult, ALU.add)
                        prev = cn2
                    cuse, tuse = cn2, thr - 0.5 * nA
                if i < NIT - 1:
                    cn = R / (2.0 ** (i + 1))
                    k.ts("dve", dd, cuse, tuse, ALU.is_ge, 2.0 * cn, ALU.mult)
                    k.stt(m, dd, -cn, m, ALU.add, ALU.add)
                else:
                    ci = R / (2.0 ** i)
                    k.ts("dve", dd, cuse, tuse, ALU.is_lt, -ci, ALU.mult)
                    k.tt("dve", lo, m, dd, ALU.add)
                yield ("i", i)

        def attn_Bfinal(c):
            TP, Lb = c["TP"], c["Lb"]
            lo = smM[:TP, 1:2]
            k.ts("dve", Mb[:TP, 0:Lb], Sc[:TP, 0:Lb], lo, ALU.is_lt, MASKNEG, ALU.mult)
            if TP < 128:
                k.memset("dve", Mb[TP:128, 0:Lb], 0.0)

        def attn_Cmain(c):
            TP, Lb, qTc = c["TP"], c["Lb"], c["qT"]
            I4 = I4p if TP == 128 else I4s
            nkt = (Lb + 127) // 128
            po = [psf[4], psf[5]]
            rot["set"] = [0, 1, 2, 3]

            def qk_pair(kt):
                kw = min(128, Lb - 128 * kt)
                pls = [rbank(), rbank()]
                for g in range(2):
                    k.mm(pls[g][:, 0:4 * TP], kT[g * 64:(g + 1) * 64, kt * 128:kt * 128 + 128],
                         qTc[g * 64:(g + 1) * 64, 0:4 * TP], start=True, stop=True)
                for g in range(2):
                    k.mm(pls[g][:kw, 0:4 * TP], Mb[:, kt * 128:kt * 128 + kw], I4[:, 0:4 * TP], start=False, stop=True,
                         skip_group_check=True)
                return pls

            cur = qk_pair(0)
            for kt in range(nkt):
                kw = min(128, Lb - 128 * kt)
                nxt = qk_pair(kt + 1) if kt + 1 < nkt else None
                ps = []
                for g in range(2):
                    p_ = pT[(2 * kt + g) % 3]
                    k.act(p_[:kw, 0:4 * TP], cur[g][:kw, 0:4 * TP], AF.Exp)
                    ps.append(p_)
                yield kt
                for g in range(2):
                    gs = slice(g * 64, (g + 1) * 64)
                    k.mm(po[0][gs, 0:4 * TP], Vb[:kw, kt, gs], ps[g][:kw, 0:4 * TP], start=(kt == 0), stop=(kt == nkt - 1))
                for g in range(2):
                    gs = slice(g * 64, (g + 1) * 64)
                    k.mm(po[1][gs, 0:4 * TP], ones_bf[:kw, 0:64], ps[g][:kw, 0:4 * TP], start=(kt == 0), stop=(kt == nkt - 1))
                yield kt
                cur = nxt

        def attn_Ctail(c):
            TP, t = c["TP"], c["t"]
            po = [psf[4], psf[5]]
            rd = rr[0]
            k.op("dve", lambda gg: gg.reciprocal(rd.t[:, 0:4 * TP], po[1].t[:, 0:4 * TP]), r=[po[1]], w=[rd])
            k.tt("dve", yaT[:, :, t * TP:(t + 1) * TP], po[0][:, 0:4 * TP].re("p (c q) -> p c q", c=4),
                 rd[:, 0:4 * TP].re("p (c q) -> p c q", c=4), ALU.mult)

        def hgrn(l, TP, NT):
            TT = TP * NT
            nch = TP // 64
            rot["set"] = [0, 1]
            for c, nm in ((0, "hig0"), (1, "hig1")):
                w_ = loadw(l, nm, "tm")
                for t in range(NT):
                    pb = rbank()
                    for kc in range(8):
                        k.mm(pb[:TP, :], xT[:, kc, t * TP:(t + 1) * TP], w_[:, kc * 512:(kc + 1) * 512],
                             start=(kc == 0), stop=(kc == 7), inc=(kc == 7))
                    if c == 0:
                        k.copy("act", vb[:TP, t * 512:(t + 1) * 512], pb[:TP, :])
                    else:
                        k.act(sg[:TP, t * 512:(t + 1) * 512], pb[:TP, :], AF.Silu)
            oacc = [psf[2 + t] for t in range(NT)]
            h0, h1, h2, h3 = hs_
            sets = [dict(E=h0, QA=QA, QB=QB, KT=KT, scm=scm, Ktok=Ktok, Sbf=Sbf, pt=psb[0]),
                    dict(E=E1, QA=QA1, QB=QB1, KT=KT1, scm=scm1, Ktok=Ktok1, Sbf=Sbf1, pt=psb[1])]
            if nch == 2:
                for st_ in sets:
                    k.memset("pool", st_["QA"][:, 0:TT].re("p (t u s) -> p t u s", u=2, s=64)[:, :, 1, :], 0.0)
                    k.memset("pool", st_["QB"][:, 0:TT].re("p (t u s) -> p t u s", u=2, s=64)[:, :, 0, :], 0.0)

            def head_pre(h):
                st_ = sets[h % 2]
                E, QAh, QBh, KTh = st_["E"], st_["QA"], st_["QB"], st_["KT"]
                wq_ = loadw(l, "hqf%d" % h, "fm")
                wf_ = loadw(l, "hqf%d" % (4 + h), "fm")
                zq = rbank()
                for kc in range(8):
                    k.mm(zq[:, 0:TT], wq_[:, kc * 128:(kc + 1) * 128], xT[:, kc, 0:TT], start=(kc == 0), stop=(kc == 7), inc=(kc == 7))
                zf = rbank()
                for kc in range(8):
                    k.mm(zf[:, 0:TT], wf_[:, kc * 128:(kc + 1) * 128], xT[:, kc, 0:TT], start=(kc == 0), stop=(kc == 7), inc=(kc == 7))
                lbc = lb[:, l * 4 + h:l * 4 + h + 1]
                k.act(E[:, 0:TT], zf[:, 0:TT], AF.Exp, scale=-1.0)
                k.act(h1[:, 0:TT], E[:, 0:TT], AF.Ln, bias=cst[:, 0:1], scale=1.0)
                k.act(E[:, 0:TT], E[:, 0:TT], AF.Ln, bias=cst[:, 0:1], scale=lbc)
                k.tt("dve", E[:, 0:TT], E[:, 0:TT], h1[:, 0:TT], ALU.subtract)
                k.act(h2[:, 0:TT], zf[:, 0:TT], AF.Sigmoid, scale=-1.0)
                k.ts("dve", h2[:, 0:TT], h2[:, 0:TT], oml[:, l * 4 + h:l * 4 + h + 1], ALU.mult)
                k.act(h3[:, 0:TT], zq[:, 0:TT], AF.Silu)
                for c in range(TT // 64):
                    k.op("dve", lambda g, c=c: g.tensor_tensor_scan(h1.ap[:, c * 64:(c + 1) * 64], ones.t[:, 0:64],
                                                                   E.ap[:, c * 64:(c + 1) * 64], 0.0, ALU.mult, ALU.add),
                         r=[ones, E], w=[h1])
                k.act(E[:, 0:TT], h1[:, 0:TT], AF.Exp)
                k.act(h1[:, 0:TT], h1[:, 0:TT], AF.Exp, scale=-1.0)
                if nch == 2:
                    qv = h3[:, 0:TT].re("p (t u s) -> p t u s", u=2, s=64)
                    ev = E[:, 0:TT].re("p (t u s) -> p t u s", u=2, s=64)
                    k.tt("dve", QAh[:, 0:TT].re("p (t u s) -> p t u s", u=2, s=64)[:, :, 0, :], qv[:, :, 0, :], ev[:, :, 0, :], ALU.mult)
                    k.tt("dve", QBh[:, 0:TT].re("p (t u s) -> p t u s", u=2, s=64)[:, :, 1, :], qv[:, :, 1, :], ev[:, :, 1, :], ALU.mult)
                else:
                    k.tt("dve", QAh[:, 0:TT], h3[:, 0:TT], E[:, 0:TT], ALU.mult)
                k.tt("dve", KTh[:, 0:TT], h2[:, 0:TT], h1[:, 0:TT], ALU.mult)
                k.copy("pool", st_["Sbf"][:, :], Sst_v[h][:, :])

            def head_chain(h):
                st_ = sets[h % 2]
                E, QAh, QBh, KTh = st_["E"], st_["QA"], st_["QB"], st_["KT"]
                scm_, Ktok_, Sbf_, pt_ = st_["scm"], st_["Ktok"], st_["Sbf"], st_["pt"]
                for t in range(NT):
                    sl = slice(t * TP, (t + 1) * TP)
                    sc = rbank()
                    k.mm(sc[:TP, 0:64], KTh[:, sl], QAh[:, t * TP:t * TP + 64])
                    if nch == 2:
                        k.mm(sc[:TP, 64:128], KTh[:, sl], QBh[:, t * TP + 64:t * TP + 128])
                    k.tt("dve", scm_[:TP, 0:TP], sc[:TP, 0:TP], bmask[:TP, 0:TP], ALU.mult)
                    k.tr(pt_[:TP, 0:128], KTh[:, sl], ident[:, :])
                    k.copy("act", Ktok_[:TP, :], pt_[:TP, 0:128])
                    ob = oacc[t][:TP, h * 128:(h + 1) * 128]
                    for u in range(nch):
                        us = slice(u * 64, (u + 1) * 64)
                        vsl = vb[us, t * 512 + h * 128:t * 512 + (h + 1) * 128]
                        k.mm(ob, scm_[us, 0:TP], vsl, start=(not started[t] and u == 0), stop=False, skip_group_check=True)
                        started[t] = True
                        qsrc = QAh if u == 0 else QBh
                        k.mm(ob, qsrc[:, sl], Sbf_[:, :], start=False, stop=(u == nch - 1), skip_group_check=True)
                        kv = rbank()
                        k.mm(kv[:, 0:128], Ktok_[us, :], vsl)
                        k.tt("dve", Sst_v[h][:, :], kv[:, 0:128], Sst_v[h][:, :], ALU.add)
                        ecol = t * TP + u * 64 + 63
                        k.act(Sst_v[h][:, :], Sst_v[h][:, :], AF.Identity, scale=E[:, ecol:ecol + 1])
                        k.copy("pool", Sbf_[:, :], Sst_v[h][:, :])
                        yield (t, u)

            started = [False] * NT
            for hp in range(2):
                ha, hb_ = 2 * hp, 2 * hp + 1
                head_pre(ha)
                head_pre(hb_)
                ga, gb_ = head_chain(ha), head_chain(hb_)
                for _ in range(NT * nch):
                    next(ga, None)
                    next(gb_, None)
                for _ in ga:
                    pass
                for _ in gb_:
                    pass
            for t in range(NT):
                ob = oacc[t]
                for h in range(4):
                    k.act(on[:TP, 0:128], ob[:TP, h * 128:(h + 1) * 128], AF.Square, accum=sm[:TP, 16 + h:17 + h])
                k.act(sm[:TP, 20:24], sm[:TP, 16:20], AF.Sqrt, bias=cst[:TP, 1:2], scale=1.0 / 128.0)
                k.op("dve", lambda g: g.reciprocal(sm.t[:TP, 24:28], sm.t[:TP, 20:24]), r=[sm], w=[sm])
                for h in range(4):
                    k.ts("dve", on[:TP, h * 128:(h + 1) * 128], ob[:TP, h * 128:(h + 1) * 128], sm[:TP, 24 + h:25 + h], ALU.mult)
                k.tt("dve", on[:TP, :], on[:TP, :], gB[:TP, :], ALU.mult)
                k.tt("dve", ybb[:TP, :], on[:TP, :], sg[:TP, t * 512:(t + 1) * 512], ALU.mult)
                for c in range(4):
                    k.tr(psb[1][:, c * 128:c * 128 + TP], ybb[:TP, c * 128:(c + 1) * 128], ident[:TP, :TP])
                k.copy("act", ybT[:, :, t * TP:(t + 1) * TP], psb[1][:, 0:512].re("p (c t) -> p c t", c=4)[:, :, 0:TP])

        def sconv(l, TT):
            rot["set"] = [0, 1, 2, 3]
            h0, h1, h2, h3 = hs_
            for j in range(4):
                wb_ = loadw(l, "c%d" % j, "fm"); wc_ = loadw(l, "c%d" % (4 + j), "fm"); wx_ = loadw(l, "c%d" % (8 + j), "fm")
                pbk, pck, pxk = rbank(), rbank(), rbank()
                for (pp, ww) in ((pbk, wb_), (pck, wc_), (pxk, wx_)):
                    for kc in range(8):
                        k.mm(pp[:, 0:TT], ww[:, kc * 128:(kc + 1) * 128], xT[:, kc, 0:TT], start=(kc == 0), stop=(kc == 7), inc=(kc == 7))
                uext = V(AR1.t[:, 0:1024], h0.buf)
                k.copy("pool", uext[:, 0:2], uh[:, 2 * j:2 * j + 2])
                k.copy("act", h2[:, 0:TT], pxk[:, 0:TT])
                k.op("dve", lambda g: g.tensor_tensor(uext.ap[:, 2:2 + TT], pck.t[:, 0:TT], h2.ap[:, 0:TT], ALU.mult),
                     r=[pck, h2], w=[h0, h1])
                k.op("pool", lambda g, j=j: g.tensor_copy(uh.t[:, 2 * j:2 * j + 2], uext.ap[:, TT:TT + 2]), r=[h0, h1], w=[uh])
                k.op("dve", lambda g, j=j: g.tensor_scalar(h3.ap[:, 0:TT], uext.ap[:, 2:2 + TT], scw.t[:, 4 * j + 2:4 * j + 3],
                                                          scw.t[:, 4 * j + 3:4 * j + 4], ALU.mult, ALU.add),
                     r=[h0, h1, scw], w=[h3])
                k.op("dve", lambda g, j=j: g.scalar_tensor_tensor(h3.ap[:, 0:TT], uext.ap[:, 1:1 + TT], scw.t[:, 4 * j + 1:4 * j + 2],
                                                                 h3.ap[:, 0:TT], ALU.mult, ALU.add),
                     r=[h0, h1, scw, h3], w=[h3])
                k.op("dve", lambda g, j=j: g.scalar_tensor_tensor(h3.ap[:, 0:TT], uext.ap[:, 0:TT], scw.t[:, 4 * j:4 * j + 1],
                                                                 h3.ap[:, 0:TT], ALU.mult, ALU.add),
                     r=[h0, h1, scw, h3], w=[h3])
                k.tt("dve", ycT[:, j, 0:TT], pbk[:, 0:TT], h3[:, 0:TT], ALU.mult)

        def merge(l, TP, NT):
            TT = TP * NT
            rot["set"] = [0, 1, 2, 3, 4, 5]
            ysrc = [yaT, ybT, ycT]
            for cc in range(8):
                for b in range(3):
                    wg_ = loadw(l, "g%d" % (b * 8 + cc), "fm")
                    wb_ = loadw(l, "br%d_%d" % (b, cc), "f4")
                    gp = rbank()
                    for kc in range(8):
                        k.mm(gp[:, 0:TT], wg_[:, kc * 128:(kc + 1) * 128], xT[:, kc, 0:TT], start=(kc == 0), stop=(kc == 7), inc=(kc == 7))
                    bp = rbank()
                    for kc in range(4):
                        k.mm(bp[:, 0:TT], wb_[:, kc * 128:(kc + 1) * 128], ysrc[b][:, kc, 0:TT], start=(kc == 0), stop=(kc == 3), inc=(kc == 3))
                    k.act(sgm[:, 0:TT], gp[:, 0:TT], AF.Sigmoid)
                    if b == 0:
                        k.tt("dve", mg[:, 0:TT], sgm[:, 0:TT], bp[:, 0:TT], ALU.mult)
                    elif b == 1:
                        k.tt("dve", tmpm[:, 0:TT], sgm[:, 0:TT], bp[:, 0:TT], ALU.mult)
                        k.tt("pool", mg[:, 0:TT], mg[:, 0:TT], tmpm[:, 0:TT], ALU.add)
                    else:
                        k.tt("dve", tmpm[:, 0:TT], sgm[:, 0:TT], bp[:, 0:TT], ALU.mult)
                        k.tt("pool", gT[:, cc * 512:cc * 512 + TT], mg[:, 0:TT], tmpm[:, 0:TT], ALU.add)
            for c in range(2):
                wo_ = loadw(l, "wout%d" % c, "tm")
                for t in range(NT):
                    pb = rbank()
                    for kc in range(8):
                        k.mm(pb[:TP, :], gT[:, kc * 512 + t * TP:kc * 512 + (t + 1) * TP], wo_[:, kc * 512:(kc + 1) * 512],
                             start=(kc == 0), stop=(kc == 7), inc=(kc == 7))
                    xv = xs_v[t][:TP, c * 512:(c + 1) * 512]
                    k.stt(xv, xv, ALPHA, pb[:TP, :], ALU.mult, ALU.add)
            for t in range(NT):
                layer_norm(TP, t, 0)

        def ffn(l, TP, NT):
            TT = TP * NT
            rot["set"] = [0, 1, 2, 3, 4, 5]
            for j in range(22):
                for wi, chn_ in enumerate((j, 22 + j)):
                    w_ = loadw(l, "up%d" % chn_, "fm")
                    hp = rbank()
                    for kc in range(8):
                        k.mm(hp[:, 0:TT], w_[:, kc * 128:(kc + 1) * 128], xT[:, kc, 0:TT], start=(kc == 0), stop=(kc == 7), inc=(kc == 7))
                    u_ = ue[wi]
                    a_ = fac[wi]
                    k.copy("pool", u_[:, 0:2], fh[:, 2 * chn_:2 * chn_ + 2])
                    k.copy("act", u_[:, 2:2 + TT], hp[:, 0:TT])
                    k.copy("pool", fh[:, 2 * chn_:2 * chn_ + 2], u_[:, TT:TT + 2])
                    k.act(a_[:, 0:TT], hp[:, 0:TT], AF.Identity, bias=fcw[:, 4 * chn_ + 3:4 * chn_ + 4],
                          scale=fcw[:, 4 * chn_ + 2:4 * chn_ + 3])
                    k.stt(a_[:, 0:TT], u_[:, 1:1 + TT], fcw[:, 4 * chn_ + 1:4 * chn_ + 2], a_[:, 0:TT], ALU.mult, ALU.add)
                    k.stt(a_[:, 0:TT], u_[:, 0:TT], fcw[:, 4 * chn_:4 * chn_ + 1], a_[:, 0:TT], ALU.mult, ALU.add)
                k.act(sa[:, 0:TT], fac[0][:, 0:TT], AF.Silu)
                k.tt("dve", gT[:, j * 512:j * 512 + TT], sa[:, 0:TT], fac[1][:, 0:TT], ALU.mult)
            acc = [psf[2 + t] for t in range(NT)]
            for c in range(2):
                kc0 = 0
                for kg, nk in enumerate((8, 8, 6)):
                    wd_ = loadw(l, "wd%d_%d" % (c, kg), "tm")
                    for t in range(NT):
                        for kk in range(nk):
                            kc = kc0 + kk
                            k.mm(acc[t][:TP, :], gT[:, kc * 512 + t * TP:kc * 512 + (t + 1) * TP], wd_[:, kk * 512:(kk + 1) * 512],
                                 start=(kc == 0), stop=(kc == 21), inc=(kk == nk - 1))
                    kc0 += nk
                for t in range(NT):
                    xv = xs_v[t][:TP, c * 512:(c + 1) * 512]
                    k.stt(xv, xv, ALPHA, acc[t][:TP, :], ALU.mult, ALU.add)
            for t in range(NT):
                layer_norm(TP, t, 1)

        def run_group(l, grp):
            if grp == "p":
                TP, NT, nmac, Lbase = 128, 4, S // 512, 0
                xsrc = xp if l == 0 else xmid_p
                xdst = xmid_p if l == 0 else yp
                okd, ovd, okid, ohd, oscd, offd, ropd = okp, ovp, okip, ohp, oscp, offp, ropep
                xmb = xmbuf_p
            else:
                TP, NT, nmac, Lbase = 64, 1, 1, P
                xsrc = xs if l == 0 else xmid_s
                xdst = xmid_s if l == 0 else ys
                okd, ovd, okid, ohd, oscd, offd, ropd = oks, ovs, okis, ohs, oscs, offs, ropes
                xmb = xmbuf_s
            TT = TP * NT
            k.dma("sp", ch("gB"), gB[:, :], gB_d[l])
            k.dma("sp", ch("scw"), scw[:, :], scw_d[l])
            k.dma("sp", ch("fcw"), fcw[:, :], fcw_d[l])
            k.dma("sp", ch("lnp"), lnp[:, :], lnp_d[l])
            if grp == "p":
                k.op("pool", lambda g: g.memset(Sst.t[:, :, :], 0.0), r=[], w=[Sst] + Sst_v)
                k.memset("pool", uh[:, :], 0.0)
                k.memset("pool", fh[:, :], 0.0)
            else:
                k.dma("sp", ch("Sst"), Sst[:, :, :], sh[l].rearrange("h d v -> d h v"), w=Sst_v)
                k.dma("sp", ch("uh"), uh[:, :], ssc[l])
                k.dma("sp", ch("fh"), fh[:, :], sff[l])
                claim(useA, useB + useC)
                for kt in range(P // 128):
                    rows = slice(kt * 128, (kt + 1) * 128)
                    k.dma("sp", ch("cst0"), att_f[:, 0:128], ck[l, rows, :])
                    k.dma("sp", ch("cst1"), att_f[:, 128:256], cv[l, rows, :])
                    k.dma("sp", ch("cst2"), att_f[:, 256:320], cki[l, rows, :])
                    k.copy("pool", att_b[:, 0:128], att_f[:, 0:128])
                    k.copy("pool", att_b[:, 128:192], att_f[:, 256:320])
                    k.copy("pool", att_b[:, 192:256], att_f[:, 256:320])
                    k.copy("pool", Vb[:, kt, :], att_f[:, 128:256])
                    k.tr(psb[0][:, 0:128], att_b[:, 0:128], ident[:, :])
                    k.tr(psb[0][:, 128:256], att_b[:, 128:256], ident[:, :])
                    k.copy("dve", kT[:, kt * 128:(kt + 1) * 128], psb[0][:, 0:128])
                    Bn = (kt * 128) // 512
                    hf = Bn % 2
                    c0 = (Bn // 2) * 512 + (kt * 128) % 512
                    k.copy("act", kiTp[hf * 64:hf * 64 + 64, c0:c0 + 128], psb[0][hf * 64:hf * 64 + 64, 128:256])
            import os
            SL = int(os.environ.get("SL", "9"))
            for mt in range(nmac):
                if grp == "s" and SL < 1:
                    break
                tok0 = mt * TT
                claim(useA, useB + useC)
                for t in range(NT):
                    k.dma("sp", ch("x_sb%d" % t), xs_v[t][:TP, :], xsrc[tok0 + t * TP:tok0 + (t + 1) * TP, :],
                          r=[xmb[mt][t]] if l == 1 else [])
                make_xT(TP, NT)
                if cfg.stop >= 2 and not (grp == "s" and SL < 2):
                    prev = None
                    for t in range(NT):
                        row0 = tok0 + t * TP
                        cur = attn_A(l, t, TP, Lbase + row0, grp == "p", ropd[row0:row0 + TP, :], okd, ovd, okid, row0)
                        gb = attn_B(cur)
                        if prev is not None:
                            gc = attn_Cmain(prev)
                            nC = 2 * ((prev["Lb"] + 127) // 128)
                            done_c = 0
                            yi = 0
                            for _ in gb:
                                yi += 1
                                tgt = min(nC, (yi * nC) // max(1, cur.get("ny", cfg.NIT)))
                                while done_c < tgt:
                                    next(gc, None)
                                    done_c += 1
                            for _ in gc:
                                pass
                            attn_Bfinal(cur)
                            attn_Ctail(prev)
                        else:
                            for _ in gb:
                                pass
                            attn_Bfinal(cur)
                        prev = cur
                    for _ in attn_Cmain(prev):
                        pass
                    attn_Ctail(prev)
                claim(useB, useA)
                if cfg.stop >= 3:
                    hgrn(l, TP, NT)
                if cfg.stop >= 4:
                    sconv(l, TT)
                claim(useC, useA + useB)
                if cfg.stop >= 5:
                    merge(l, TP, NT)
                    make_xT(TP, NT)
                if cfg.stop >= 6:
                    ffn(l, TP, NT)
                for t in range(NT):
                    k.dma("pool", ch("st_x%d" % t), xdst[tok0 + t * TP:tok0 + (t + 1) * TP, :], xs_v[t][:TP, :],
                          w=[xmb[mt][t]] if l == 0 else [])
            k.dma("pool", ch("st_S"), ohd[l].rearrange("h d v -> d h v"), Sst[:, :, :], r=Sst_v)
            k.dma("pool", ch("st_uh"), oscd[l], uh[:, :])
            k.dma("pool", ch("st_fh"), offd[l], fh[:, :])

        xmbuf_p = [[Buf("xmp%d_%d" % (i, t)) for t in range(4)] for i in range(S // 512)]
        xmbuf_s = [[Buf("xms")]]
        for grp in ("p", "s"):
            for l in range(2):
                import os as _os
                if _os.environ.get("ONLYS") and not (grp == "s" and l == 0):
                    continue
                if cfg.stop >= 9 or (cfg.stop >= 1 and grp == "p" and l == 0) or (cfg.stop >= 7 and grp == "p") or (cfg.stop >= 8 and l == 0):
                    run_group(l, grp)
        k.wait_all("pool")
        k.wait_all("sp")
        print("instructions:", k.ninst, "channels:", k.nchan, "sbuf_left:", nc.sbuf_bytes_remaining)
    return nc


def kernel_cfg(cfg, inputs, n_cores=8):
    f32 = np.float32
    g = lambda n: np.asarray(inputs[n], dtype=f32)
    x_prompt, x_sample = g("x_prompt"), g("x_sample")
    S, DS, P = cfg.S, cfg.DS, cfg.P
    wfl = host_weights(g("w_in"), g("w_branch"), g("w_out"), g("w_up"), g("w_down"))
    lbl = np.ascontiguousarray(g("hgrn_lb_logits").reshape(2, 4, 128).transpose(2, 0, 1).reshape(128, 8))
    gBv = np.ascontiguousarray(np.broadcast_to(np.tile(g("hgrn_norm_g"), (1, 4))[:, None, :], (2, 128, 512)))
    scw = np.concatenate([g("sconv_w"), g("sconv_b")[:, None, :]], axis=1)
    scw = np.ascontiguousarray(scw.reshape(2, 4, 4, 128).transpose(0, 3, 2, 1).reshape(2, 128, 16))
    fcw = np.concatenate([g("ffn_conv_w"), g("ffn_conv_b")[:, None, :]], axis=1)
    fcw = np.ascontiguousarray(fcw.reshape(2, 4, 44, 128).transpose(0, 3, 2, 1).reshape(2, 128, 176))
    lnp = np.stack([g("ln1_g"), g("ln1_b"), g("ln2_g"), g("ln2_b")], axis=1).reshape(2, 1, 4096)
    lnp = np.ascontiguousarray(np.broadcast_to(lnp, (2, 128, 4096)))
    ropep = rope_table(np.arange(S))
    ropes = rope_table(P + np.arange(DS))
    ck = g("cache_attn_k").reshape(2, -1, P, 128)
    cv = g("cache_attn_v").reshape(2, -1, P, 128)
    cki = g("cache_idx_k")
    sh = g("state_hgrn")
    ssc = g("state_sconv")
    sff = g("state_ffn_conv")
    nb = x_prompt.shape[0]
    in_maps = []
    for c in range(n_cores):
        pb = (c // 2) % nb
        sb = c % x_sample.shape[0]
        ssc_t = np.ascontiguousarray(ssc[:, sb].reshape(2, 2, 4, 128).transpose(0, 3, 2, 1).reshape(2, 128, 8))
        sff_t = np.ascontiguousarray(sff[:, sb].reshape(2, 2, 44, 128).transpose(0, 3, 2, 1).reshape(2, 128, 88))
        in_maps.append({
            "xp": np.ascontiguousarray(x_prompt[pb]), "xs": np.ascontiguousarray(x_sample[sb]),
            "ck": np.ascontiguousarray(ck[:, sb]), "cv": np.ascontiguousarray(cv[:, sb]),
            "cki": np.ascontiguousarray(cki[:, sb]), "sh": np.ascontiguousarray(sh[:, sb]),
            "ssc": ssc_t, "sff": sff_t, "wfl": wfl, "lbl": lbl, "gB": gBv, "scw": scw, "fcw": fcw, "lnp": lnp,
            "ropep": ropep, "ropes": ropes,
        })
    nc = build(cfg)
    res = run_bass_kernel_spmd(nc, in_maps, core_ids=list(range(n_cores)))
    R = res.results
    NB, NSB = x_prompt.shape[0], x_sample.shape[0]
    pc = [2 * b for b in range(NB)]

    def st_t(a, nchk):
        return a.reshape(2, 128, nchk, 2).transpose(0, 3, 2, 1).reshape(2, 2, nchk * 128)

    y_p = np.stack([R[c]["yp"] for c in pc], 0)
    y_s = np.stack([R[c]["ys"] for c in range(NSB)], 0)
    k_p = np.stack([R[c]["okp"] for c in pc], 1).reshape(2, NB, S, 2, 64)
    v_p = np.stack([R[c]["ovp"] for c in pc], 1).reshape(2, NB, S, 2, 64)
    ki_p = np.stack([R[c]["okip"] for c in pc], 1)
    h_p = np.stack([R[c]["ohp"] for c in pc], 1)
    sc_p = np.stack([st_t(R[c]["oscp"], 4) for c in pc], 1)
    ff_p = np.stack([st_t(R[c]["offp"], 44) for c in pc], 1)
    k_s = np.stack([R[c]["oks"] for c in range(NSB)], 1).reshape(2, NSB, DS, 2, 64)
    v_s = np.stack([R[c]["ovs"] for c in range(NSB)], 1).reshape(2, NSB, DS, 2, 64)
    ki_s = np.stack([R[c]["okis"] for c in range(NSB)], 1)
    h_s = np.stack([R[c]["ohs"] for c in range(NSB)], 1)
    sc_s = np.stack([st_t(R[c]["oscs"], 4) for c in range(NSB)], 1)
    ff_s = np.stack([st_t(R[c]["offs"], 44) for c in range(NSB)], 1)
    outs = (y_p, y_s, k_p, v_p, ki_p, h_p, sc_p, ff_p, k_s, v_s, ki_s, h_s, sc_s, ff_s)
    return tuple(np.ascontiguousarray(o, dtype=np.float32) for o in outs)


def kernel(**inputs):
    return kernel_cfg(Cfg(), inputs)
```

```python
import numpy as np
import concourse.bass as bass
import concourse.mybir as mybir
from concourse.bass_utils import run_bass_kernel_spmd
from contextlib import ExitStack

F32 = mybir.dt.float32
BF16 = mybir.dt.bfloat16
AF = mybir.ActivationFunctionType
ALU = mybir.AluOpType


class Buf:
    __slots__ = ("name", "w", "r", "excl")

    def __init__(self, name):
        self.name = name
        self.w = None
        self.r = {}
        self.excl = False


class V:
    __slots__ = ("ap", "buf")

    def __init__(self, ap, buf):
        self.ap = ap
        self.buf = buf

    def __getitem__(self, idx):
        return V(self.ap[idx], self.buf)

    def re(self, pat, **kw):
        return V(self.ap.rearrange(pat, **kw), self.buf)


class T:
    def __init__(self, k, name, shape, dtype, psum=False, buf=None):
        if psum:
            self.t = k.es.enter_context(k.nc.psum_tensor("t_" + name, shape, dtype))
        else:
            self.t = k.es.enter_context(k.nc.sbuf_tensor("t_" + name, shape, dtype))
        self.buf = buf if buf is not None else Buf(name)
        self.buf.excl = bool(psum)

    def __getitem__(self, idx):
        return V(self.t[idx], self.buf)


class K:
    def __init__(self, nc, es):
        self.nc = nc
        self.es = es
        self.eng = {"pe": nc.tensor, "act": nc.scalar, "dve": nc.vector, "pool": nc.gpsimd, "sp": nc.sync}
        self.sem = {}
        self.cnt = {}
        self.seen = {e: {} for e in self.eng}
        self.ekey = {}
        self.eep = {}
        for e in ("pe", "act", "dve", "pool"):
            self.sem[e] = es.enter_context(nc.semaphore("s_" + e))
            self.cnt[e] = 0
            self.ekey[e] = e
            self.eep[e] = 0
        self.nchan = 0
        self.ninst = 0

    def chan(self, name):
        key = "ch_%d_%s" % (self.nchan, name)
        self.nchan += 1
        self.sem[key] = self.es.enter_context(self.nc.semaphore(key))
        self.cnt[key] = 0
        return key

    def _wait(self, e, deps):
        eng = self.eng[e]
        best = {}
        for key, val in deps:
            if e == "pe" and key.startswith("pe"):
                continue
            if best.get(key, 0) < val:
                best[key] = val
        for key, val in best.items():
            if self.seen[e].get(key, 0) < val:
                eng.wait_ge(self.sem[key], val)
                self.seen[e][key] = val

    def op(self, e, fn, r=(), w=(), chan=None, inc=True):
        rb = [x.buf if not isinstance(x, Buf) else x for x in r if x is not None]
        wb = [x.buf if not isinstance(x, Buf) else x for x in w if x is not None]
        wb = wb + [b for b in rb if b.excl]
        rb = [b for b in rb if not b.excl]
        deps = []
        for b in rb:
            if b.w is not None:
                deps.append(b.w)
        for b in wb:
            if b.w is not None:
                deps.append(b.w)
            deps.extend(b.r.items())
        self._wait(e, deps)
        inst = fn(self.eng[e])
        self.ninst += 1
        if chan is None:
            ek = self.ekey[e]
            if self.cnt[ek] >= 30000:
                self.eep[e] += 1
                ek = "%s_%d" % (e, self.eep[e])
                self.ekey[e] = ek
                self.sem[ek] = self.es.enter_context(self.nc.semaphore("s_" + ek))
                self.cnt[ek] = 0
            if inc:
                self.cnt[ek] += 1
                inst.then_inc(self.sem[ek], 1)
                tk = (ek, self.cnt[ek])
            else:
                tk = (ek, self.cnt[ek] + 1)
        else:
            self.cnt[chan] += 16
            inst.then_inc(self.sem[chan], 16)
            tk = (chan, self.cnt[chan])
        for b in rb:
            if b.r.get(tk[0], 0) < tk[1]:
                b.r[tk[0]] = tk[1]
        for b in wb:
            b.w = tk
            b.r = {}
        return inst

    def mm(self, o, lhsT, rhs, start=True, stop=True, extra_r=(), inc=True, **kw):
        return self.op("pe", lambda g: g.matmul(o.ap, lhsT.ap, rhs.ap, start=start, stop=stop, **kw),
                       r=[lhsT, rhs] + list(extra_r), w=[o], inc=inc)

    def tr(self, o, in_, ident):
        return self.op("pe", lambda g: g.transpose(o.ap, in_.ap, ident.ap), r=[in_, ident], w=[o])

    def act(self, o, in_, func, bias=None, scale=None, accum=None, e="act"):
        kw = {}
        rr = [in_]
        if bias is not None:
            if isinstance(bias, V):
                kw["bias"] = bias.ap
                rr.append(bias)
            else:
                kw["bias"] = bias
        if scale is not None:
            if isinstance(scale, V):
                kw["scale"] = scale.ap
                rr.append(scale)
            else:
                kw["scale"] = scale
        ww = [o]
        if accum is not None:
            kw["accum_out"] = accum.ap
            ww.append(accum)
        return self.op("act", lambda g: g.activation(o.ap, in_.ap, func, **kw), r=rr, w=ww)

    def tt(self, e, o, a, b, op):
        return self.op(e, lambda g: g.tensor_tensor(o.ap, a.ap, b.ap, op), r=[a, b], w=[o])

    def ts(self, e, o, a, s1, op0, s2=None, op1=None, accum=None):
        rr = [a]
        v1 = s1
        v2 = s2
        if isinstance(s1, V):
            rr.append(s1)
            v1 = s1.ap
        if isinstance(s2, V):
            rr.append(s2)
            v2 = s2.ap
        ww = [o]
        kw = {}
        if accum is not None:
            kw["accum_out"] = accum.ap
            ww.append(accum)
        if op1 is None:
            return self.op(e, lambda g: g.tensor_scalar(o.ap, a.ap, v1, None, op0, **kw), r=rr, w=ww)
        return self.op(e, lambda g: g.tensor_scalar(o.ap, a.ap, v1, v2, op0, op1, **kw), r=rr, w=ww)

    def stt(self, o, a, s, b, op0, op1):
        rr = [a, b]
        v = s
        if isinstance(s, V):
            rr.append(s)
            v = s.ap
        return self.op("dve", lambda g: g.scalar_tensor_tensor(o.ap, a.ap, v, b.ap, op0, op1), r=rr, w=[o])

    def copy(self, e, o, a):
        if e == "act":
            return self.op(e, lambda g: g.copy(o.ap, a.ap), r=[a], w=[o])
        return self.op(e, lambda g: g.tensor_copy(o.ap, a.ap), r=[a], w=[o])

    def memset(self, e, o, val):
        return self.op(e, lambda g: g.memset(o.ap, val), r=[], w=[o])

    def dma(self, e, chan, o, i, r=(), w=(), **kw):
        oa = o.ap if isinstance(o, V) else o
        ia = i.ap if isinstance(i, V) else i
        rr = list(r) + ([i] if isinstance(i, V) else [])
        ww = list(w) + ([o] if isinstance(o, V) else [])
        return self.op(e, lambda g: g.dma_start(out=oa, in_=ia, **kw), r=rr, w=ww, chan=chan)

    def wait_all(self, e):
        deps = [(key, c) for key, c in self.cnt.items() if c > 0]
        self._wait(e, deps)

D = 1024
DFF = 2816
ALPHA = 4.0 ** 0.25
LN_EPS = 1e-5
NEG = -1e30
MASKNEG = -30000.0
O_HQ, O_HF, O_HI, O_HG, O_CB, O_CC, O_CX, O_G = 1092, 1604, 2116, 2628, 3140, 3652, 4164, 4676


class Cfg:
    def __init__(self, S=8192, DS=64, P=2048, TOPK=256, NIT=28, stop=9):
        self.S, self.DS, self.P, self.TOPK, self.NIT = S, DS, P, TOPK, NIT
        self.stop = stop
        self.sub = 9
        self.LMAX = max(S, P + DS)
        self.LMAX = ((self.LMAX + 1023) // 1024) * 1024
        self.NKT = self.LMAX // 128


def _tile(W, c0, width, k0=0, nk=None):
    K = W.shape[0]
    if nk is None:
        nk = K // 128
    sub = W[k0 * 128:(k0 + nk) * 128, c0:c0 + width]
    return np.ascontiguousarray(sub.reshape(nk, 128, width).transpose(1, 0, 2).reshape(128, nk * width))


def weight_layout():
    items = []
    for j in range(8):
        items.append(("hqf%d" % j, 1024))
    for j in range(12):
        items.append(("c%d" % j, 1024))
    for j in range(24):
        items.append(("g%d" % j, 1024))
    for j in range(44):
        items.append(("up%d" % j, 1024))
    for b in range(3):
        for j in range(8):
            items.append(("br%d_%d" % (b, j), 512))
    items += [("att0", 4096), ("att1", 4096), ("att2", 8 * 68), ("hig0", 4096), ("hig1", 4096),
              ("wout0", 4096), ("wout1", 4096)]
    for c in range(2):
        for kg, nk in enumerate((8, 8, 6)):
            items.append(("wd%d_%d" % (c, kg), nk * 512))
    off = {}
    o = 0
    for n, w in items:
        off[n] = (o, w)
        o += w
    return items, off, o


def host_weights(w_in, w_branch, w_out, w_up, w_down):
    items, off, tot = weight_layout()
    out = np.empty((2, 128, tot), np.float32)
    for l in range(2):
        parts = []
        for j in range(8):
            parts.append(_tile(w_in[l], O_HQ + 128 * j, 128))
        for j in range(12):
            parts.append(_tile(w_in[l], O_CB + 128 * j, 128))
        for j in range(24):
            parts.append(_tile(w_in[l], O_G + 128 * j, 128))
        for j in range(44):
            parts.append(_tile(w_up[l], 128 * j, 128))
        perm = np.concatenate([np.r_[c * 64:(c + 1) * 64, (4 + c) * 64:(5 + c) * 64] for c in range(4)])
        for b in range(3):
            Wb = w_branch[l, b][perm] if b == 0 else w_branch[l, b]
            for j in range(8):
                parts.append(_tile(Wb, 128 * j, 128))
        parts += [_tile(w_in[l], 0, 512), _tile(w_in[l], 512, 512), _tile(w_in[l], 1024, 68),
                  _tile(w_in[l], O_HI, 512), _tile(w_in[l], O_HG, 512),
                  _tile(w_out[l], 0, 512), _tile(w_out[l], 512, 512)]
        for c in range(2):
            for k0, nk in ((0, 8), (8, 8), (16, 6)):
                parts.append(_tile(w_down[l], 512 * c, 512, k0, nk))
        out[l] = np.concatenate(parts, axis=1)
    return out


def rope_table(pos):
    half = 8
    inv = (500000.0 ** (-np.arange(half, dtype=np.float32) / np.float32(half))).astype(np.float32)
    ang = pos.astype(np.float32)[:, None] * inv[None, :]
    cos = np.cos(ang).astype(np.float32)
    sin = np.sin(ang).astype(np.float32)
    C16 = np.concatenate([cos, cos], axis=1)
    S16 = np.concatenate([-sin, sin], axis=1)
    tab = np.stack([np.tile(C16[:, None, :], (1, 8, 1)), np.tile(S16[:, None, :], (1, 8, 1))], axis=1)
    return np.ascontiguousarray(tab.reshape(len(pos), 256))


def build(cfg):
    S, DS, P = cfg.S, cfg.DS, cfg.P
    LMAX, NKT = cfg.LMAX, cfg.NKT
    nc = bass.Bass("TRN2", target_bir_lowering=False)
    _, woff, WTOT = weight_layout()

    def din(name, shape, dt=F32):
        return nc.dram_tensor(name, list(shape), dt, kind="ExternalInput").ap()

    def dout(name, shape, dt=F32):
        return nc.dram_tensor(name, list(shape), dt, kind="ExternalOutput").ap()

    xp = din("xp", [S, D]); xs = din("xs", [DS, D])
    ck = din("ck", [2, P, 128]); cv = din("cv", [2, P, 128]); cki = din("cki", [2, P, 64])
    sh = din("sh", [2, 4, 128, 128]); ssc = din("ssc", [2, 128, 8]); sff = din("sff", [2, 128, 88])
    wfl = din("wfl", [2, 128, WTOT])
    lbl = din("lbl", [128, 8]); gB_d = din("gB", [2, 128, 512])
    scw_d = din("scw", [2, 128, 16]); fcw_d = din("fcw", [2, 128, 176])
    lnp_d = din("lnp", [2, 128, 4096])
    ropep = din("ropep", [S, 256]); ropes = din("ropes", [DS, 256])

    yp = dout("yp", [S, D]); ys = dout("ys", [DS, D])
    okp = dout("okp", [2, S, 128]); ovp = dout("ovp", [2, S, 128]); okip = dout("okip", [2, S, 64])
    ohp = dout("ohp", [2, 4, 128, 128]); oscp = dout("oscp", [2, 128, 8]); offp = dout("offp", [2, 128, 88])
    oks = dout("oks", [2, DS, 128]); ovs = dout("ovs", [2, DS, 128]); okis = dout("okis", [2, DS, 64])
    ohs = dout("ohs", [2, 4, 128, 128]); oscs = dout("oscs", [2, 128, 8]); offs = dout("offs", [2, 128, 88])

    wbf = nc.dram_tensor("wbf", [2, 128, WTOT], BF16, kind="Internal").ap()
    xmid_p = nc.dram_tensor("xmid_p", [S, D], F32, kind="Internal").ap()
    xmid_s = nc.dram_tensor("xmid_s", [DS, D], F32, kind="Internal").ap()

    es = ExitStack()
    with es:
        k = K(nc, es)
        kT = T(k, "kT", [128, LMAX], BF16)
        kiTp = T(k, "kiTp", [128, LMAX // 2], BF16)
        Vb = T(k, "Vb", [128, NKT, 128], BF16)
        ones_bf = T(k, "ones_bf", [128, 64], BF16)
        AR1 = T(k, "AR1", [128, 8192], F32)
        AR2 = T(k, "AR2", [128, 4096], F32)
        x_sb = T(k, "x_sb", [128, 4, D], F32)
        xs_v = [V(x_sb.t[:, t, :], Buf("xs%d" % t)) for t in range(4)]
        xT = T(k, "xT", [128, 8, 512], BF16)
        wfm = [T(k, "wfm%d" % i, [128, 1024], BF16) for i in range(4)]
        wf4 = [T(k, "wf4%d" % i, [128, 512], BF16) for i in range(2)]
        wtm = [T(k, "wtm%d" % i, [128, 4096], BF16) for i in range(2)]
        wsm = T(k, "wsm", [128, 544], BF16)
        yaT = T(k, "yaT", [128, 4, 512], BF16)
        ybT = T(k, "ybT", [128, 4, 512], BF16)
        ycT = T(k, "ycT", [128, 4, 512], BF16)
        att_f = T(k, "att_f", [128, 1092], F32)
        att_b = T(k, "att_b", [128, 1280], BF16)
        qTs = [T(k, "qT%d" % i, [128, 512], BF16) for i in range(2)]
        junk = T(k, "junk", [128, 4992], mybir.dt.uint8)
        junkA = T(k, "junkA", [128, 3328], mybir.dt.int8)
        smA = T(k, "smA", [128, 4], F32)
        smM = T(k, "smM", [128, 4], F32)
        smC = T(k, "smC", [128, 4], F32)
        smD = T(k, "smD", [128, 4], F32)
        qiT = T(k, "qiT", [128, 512], BF16)
        rr = [T(k, "rr%d" % i, [128, 512], F32) for i in range(2)]
        pT = [T(k, "pT%d" % i, [128, 512], BF16) for i in range(3)]
        rope = T(k, "rope", [128, 256], F32)
        ra = T(k, "ra", [128, 128], F32)
        rb = T(k, "rb", [128, 128], F32)
        xbfs = [T(k, "xbf%d" % i, [128, D], BF16) for i in range(2)]
        sm = T(k, "sm", [128, 64], F32)
        bst = T(k, "bst", [128, 16], F32)
        Sst = T(k, "Sst", [128, 4, 128], F32)
        Sst_v = [V(Sst.t[:, h, :], Buf("Sst%d" % h)) for h in range(4)]
        Sbf = T(k, "Sbf", [128, 128], BF16)
        scm = T(k, "scm", [128, 128], BF16)
        Ktok = T(k, "Ktok", [128, 128], BF16)
        Sbf1 = T(k, "Sbf1", [128, 128], BF16)
        scm1 = T(k, "scm1", [128, 128], BF16)
        Ktok1 = T(k, "Ktok1", [128, 128], BF16)
        uh = T(k, "uh", [128, 8], F32)
        fh = T(k, "fh", [128, 88], F32)
        lb = T(k, "lb", [128, 8], F32)
        oml = T(k, "oml", [128, 8], F32)
        lbraw = T(k, "lbraw", [128, 8], F32)
        gB = T(k, "gB", [128, 512], F32)
        scw = T(k, "scw", [128, 16], F32)
        fcw = T(k, "fcw", [128, 176], F32)
        lnp = T(k, "lnp", [128, 4096], F32)
        ident = T(k, "ident", [128, 128], BF16)
        I4p = T(k, "I4p", [128, 512], BF16)
        I4s = T(k, "I4s", [128, 256], BF16)
        bmask = T(k, "bmask", [128, 128], F32)
        ones = T(k, "ones", [128, 64], F32)
        cst = T(k, "cst", [128, 8], F32)
        psf = [T(k, "psf%d" % i, [128, 512], F32, psum=True) for i in range(6)]
        psb = [T(k, "psb%d" % i, [128, 1024], BF16, psum=True) for i in range(2)]

        def aview(ar, off, n, dt, name):
            ap = ar.t[:, off:off + n]
            if dt == BF16:
                ap = ap.bitcast(BF16)
            return V(ap, Buf(name))

        Sc = V(AR1.t[:, :], Buf("Sc"))
        Mb = V(AR2.t[:, :].bitcast(BF16), Buf("Mb"))
        hs_ = [aview(AR1, 512 * i, 512, F32, "h%d" % i) for i in range(4)]
        sg = aview(AR1, 2048, 2048, F32, "sg")
        vb = aview(AR1, 4096, 1024, BF16, "vb")
        QA = aview(AR1, 5120, 256, BF16, "QA")
        QB = aview(AR1, 5376, 256, BF16, "QB")
        KT = aview(AR1, 5632, 256, BF16, "KT")
        on = aview(AR1, 5888, 512, F32, "on")
        ybb = aview(AR1, 6400, 256, BF16, "ybb")
        E1 = aview(AR1, 6656, 512, F32, "E1")
        QA1 = aview(AR1, 7168, 256, BF16, "QA1")
        QB1 = aview(AR1, 7424, 256, BF16, "QB1")
        KT1 = aview(AR1, 7680, 256, BF16, "KT1")
        gT = aview(AR1, 0, 5632, BF16, "gT")
        ue = [aview(AR1, 5632, 520, F32, "ue0"), aview(AR1, 6152, 520, F32, "ue1")]
        fac = [aview(AR1, 6672, 512, F32, "fac0"), aview(AR1, 7184, 512, F32, "fac1")]
        sa = aview(AR2, 0, 512, F32, "sa")
        mg = aview(AR2, 512, 512, F32, "mg")
        sgm = aview(AR2, 1024, 512, F32, "sgm")
        tmpm = aview(AR2, 1536, 512, F32, "tmpm")
        useA = [Sc, Mb]
        useB = hs_ + [sg, vb, QA, QB, KT, on, ybb, E1, QA1, QB1, KT1]
        useC = [gT] + ue + fac + [sa, mg, sgm, tmpm]

        def claim(new, old):
            deps = {}
            for v in old:
                b = v.buf
                if b.w is not None:
                    deps[b.w[0]] = max(deps.get(b.w[0], 0), b.w[1])
                for kk, vv in b.r.items():
                    deps[kk] = max(deps.get(kk, 0), vv)
            for v in new:
                v.buf.w = None
                v.buf.r = dict(deps)

        chn = {}

        def ch(name):
            if name not in chn:
                chn[name] = k.chan(name)
            return chn[name]

        k.memset("pool", ident[:, :], 1.0)
        k.op("pool", lambda g: g.affine_select(ident.t[:, :], ident.t[:, :], [[1, 128]], ALU.is_equal, 0.0,
                                               base=0, channel_multiplier=-1), r=[ident], w=[ident])
        k.memset("pool", I4s[:, :], 0.0)
        for j in range(4):
            k.copy("pool", I4p[:, j * 128:(j + 1) * 128], ident[:, :])
            k.copy("pool", I4s[0:64, j * 64:(j + 1) * 64], ident[0:64, 0:64])
        k.memset("pool", bmask[:, :], 1.0)
        k.op("pool", lambda g: g.affine_select(bmask.t[:, :], bmask.t[:, :], [[1, 128]], ALU.is_ge, 0.0,
                                               base=0, channel_multiplier=-1), r=[bmask], w=[bmask])
        k.memset("pool", bmask[0:64, 64:128], 0.0)
        k.memset("pool", ones[:, :], 1.0)
        k.memset("pool", cst[:, 0:1], 1.0)
        k.memset("pool", cst[:, 1:2], LN_EPS)
        k.memset("dve", Vb[:, :, :], 0.0)
        k.memset("dve", ones_bf[:, :], 1.0)
        k.memset("pool", kT[:, :], 0.0)
        k.memset("pool", kiTp[:, :], 0.0)
        k.memset("pool", qiT[:, :], 0.0)
        k.memset("dve", QA[:, :], 0.0)
        k.memset("dve", QB[:, :], 0.0)
        k.dma("sp", ch("lbraw"), lbraw[:, :], lbl)
        k.memset("dve", lb[:, 0:4], 0.0)
        k.tt("dve", lb[:, 4:8], lbraw[:, 4:8], lbraw[:, 0:4], ALU.subtract)
        k.act(lb[:, 4:8], lb[:, 4:8], AF.Sigmoid)
        k.ts("dve", oml[:, :], lb[:, :], -1.0, ALU.mult, 1.0, ALU.add)

        oA0, oA1, oG = woff["att0"][0], woff["wd0_0"][0], woff["g0"][0]
        wranges = [("A", oA0, oA1), ("B", 0, oG), ("C", oG, oA0), ("D", oA1, WTOT)]
        wbuf0 = {nm: Buf("wbf0" + nm) for nm, _, _ in wranges}
        wbuf1 = Buf("wbf1")
        CSTEP = 9600
        for nm, a0, a1 in wranges:
            cch = ch("wcast0" + nm)
            a = a0
            while a < a1:
                b = min(a1, a + CSTEP)
                k.dma("pool", cch, wbf[0, :, a:b], wfl[0, :, a:b], w=[wbuf0[nm]])
                a = b
        cch1 = ch("wcast1")
        a = 0
        while a < WTOT:
            b = min(WTOT, a + CSTEP)
            k.dma("pool", cch1, wbf[1, :, a:b], wfl[1, :, a:b], w=[wbuf1])
            a = b

        def wdep(l, o):
            if l == 1:
                return wbuf1
            for nm, a0, a1 in wranges:
                if a0 <= o < a1:
                    return wbuf0[nm]
            raise AssertionError(o)

        wq = {"fm": 0, "f4": 0, "tm": 0}

        def loadw(l, name, kind):
            o, w = woff[name]
            if kind == "fm":
                t = wfm[wq["fm"] % 4]; wq["fm"] += 1
            elif kind == "f4":
                t = wf4[wq["f4"] % 2]; wq["f4"] += 1
            elif kind == "tm":
                t = wtm[wq["tm"] % 2]; wq["tm"] += 1
            else:
                t = wsm
            k.dma("sp", ch("w_" + t.buf.name), t[:, 0:w], wbf[l, :, o:o + w], r=[wdep(l, o)])
            return t

        rot = {"i": 0, "set": [0, 1, 2, 3]}

        def rbank():
            b = psf[rot["set"][rot["i"] % len(rot["set"])]]
            rot["i"] += 1
            return b

        def make_xT(TP, NT):
            for t in range(NT):
                xbf = xbfs[t % 2]
                k.copy("act", xbf[:TP, :], xs_v[t][:TP, :])
                for half in range(2):
                    pb = psb[half]
                    for c in range(4):
                        k.tr(pb[:, c * 128:c * 128 + TP], xbf[:TP, (half * 4 + c) * 128:(half * 4 + c + 1) * 128],
                             ident[:TP, :TP])
                    src = pb[:, 0:512].re("p (c t) -> p c t", c=4)[:, :, 0:TP]
                    k.copy("act" if half == 0 else "dve", xT[:, half * 4:(half + 1) * 4, t * TP:(t + 1) * TP], src)

        def layer_norm(TP, t, which):
            xv = xs_v[t][:TP, :]
            for c in range(2):
                k.op("dve", lambda g, c=c: g.bn_stats(bst.t[:TP, c * 6:(c + 1) * 6], xs_v[t].ap[:TP, c * 512:(c + 1) * 512]),
                     r=[xs_v[t]], w=[bst])
            k.op("dve", lambda g: g.bn_aggr(sm.t[:TP, 32:34], bst.t[:TP, 0:12]), r=[bst], w=[sm])
            k.act(sm[:TP, 34:35], sm[:TP, 33:34], AF.Sqrt, bias=cst[:TP, 1:2], scale=1.0)
            k.op("dve", lambda g: g.reciprocal(sm.t[:TP, 35:36], sm.t[:TP, 34:35]), r=[sm], w=[sm])
            k.ts("dve", xv, xv, sm[:TP, 32:33], ALU.subtract, sm[:TP, 35:36], ALU.mult)
            k.tt("dve", xv, xv, lnp[:TP, which * 2048:which * 2048 + 1024], ALU.mult)
            k.tt("dve", xv, xv, lnp[:TP, which * 2048 + 1024:which * 2048 + 2048], ALU.add)

        def rotary(TP, ps, out, nh, c16off=0):
            pv = ps.re("p (h d) -> p h d", d=64)
            ov = out.re("p (h d) -> p h d", d=64)
            Cv = rope[:TP, 0:nh * 16].re("p (h d) -> p h d", d=16)
            Sv = rope[:TP, 128:128 + nh * 16].re("p (h d) -> p h d", d=16)
            av = ra[:TP, 0:nh * 16].re("p (h d) -> p h d", d=16)
            bv = rb[:TP, 0:nh * 16].re("p (h d) -> p h d", d=16)
            import os
            rv = os.environ.get("ROTV", "")
            if rv != "noact":
                k.copy("act", ov[:, :, 16:64], pv[:, :, 16:64])
            if rv == "nodve":
                return
            k.tt("dve", av, pv[:, :, 0:16], Cv, ALU.mult)
            k.tt("dve", bv[:, :, 0:8], pv[:, :, 8:16], Sv[:, :, 0:8], ALU.mult)
            k.tt("dve", bv[:, :, 8:16], pv[:, :, 0:8], Sv[:, :, 8:16], ALU.mult)
            k.tt("dve", ov[:, :, 0:16], av, bv, ALU.add)

        def attn_A(l, t, TP, Lprev, masked, rope_src, okd, ovd, okid, row0):
            Lb = Lprev + TP
            kt_new = Lprev // 128
            qTc = qTs[t % 2]
            k.dma("sp", ch("rope"), rope[:TP, :], rope_src)
            w0 = loadw(l, "att0", "tm"); w1 = loadw(l, "att1", "tm"); w2 = loadw(l, "att2", "sm")
            rot["set"] = [0, 1, 2, 3]
            pA, pB, pC = rbank(), rbank(), rbank()
            for (pp, ww, wd) in ((pA, w0, 512), (pB, w1, 512), (pC, w2, 68)):
                for kc in range(8):
                    k.mm(pp[:TP, 0:wd], xT[:, kc, t * TP:(t + 1) * TP], ww[:, kc * wd:(kc + 1) * wd],
                         start=(kc == 0), stop=(kc == 7), inc=(kc == 7))
            rotary(TP, pA[:TP, 0:512], att_f[:TP, 0:512], 8)
            rotary(TP, pB[:TP, 0:512], att_f[:TP, 512:1024], 8)
            k.copy("act", att_f[:TP, 640:768].re("p (h d) -> p h d", d=64)[:, :, 0:16],
                   pB[:TP, 128:256].re("p (h d) -> p h d", d=64)[:, :, 0:16])
            rotary(TP, pC[:TP, 0:64], att_f[:TP, 1024:1088], 1)
            k.copy("act", att_f[:TP, 1088:1092], pC[:TP, 64:68])
            k.dma("pool", ch("st_att"), okd[l, row0:row0 + TP, :], att_f[:TP, 512:640])
            k.dma("pool", ch("st_att"), ovd[l, row0:row0 + TP, :], att_f[:TP, 640:768])
            k.dma("pool", ch("st_att"), okid[l, row0:row0 + TP, :], att_f[:TP, 1024:1088])
            k.copy("pool", att_b[:TP, 0:512].re("p (j g d) -> p j g d", j=4, g=2),
                   att_f[:TP, 0:512].re("p (g j d) -> p j g d", g=2, j=4))
            k.copy("pool", att_b[:TP, 512:640], att_f[:TP, 512:640])
            qdv = att_b[:TP, 640:1152].re("p (h u d) -> p h u d", h=4, u=2)
            qsv = att_f[:TP, 768:1024].re("p (h d) -> p h d", h=4)
            k.copy("pool", qdv[:, :, 0, :], qsv)
            k.copy("pool", qdv[:, :, 1, :], qsv)
            k.copy("pool", att_b[:TP, 1152:1216], att_f[:TP, 1024:1088])
            k.copy("pool", att_b[:TP, 1216:1280], att_f[:TP, 1024:1088])
            k.copy("pool", Vb[:TP, kt_new, :], att_f[:TP, 640:768])
            for j in range(4):
                k.tr(psb[0][:, j * 128:j * 128 + TP], att_b[:TP, j * 128:(j + 1) * 128], ident[:TP, :TP])
            k.act(qTc[:, 0:4 * TP].re("p (j t) -> p j t", j=4),
                  psb[0][:, 0:512].re("p (j t) -> p j t", j=4)[:, :, 0:TP], AF.Copy, scale=0.125)
            k.tr(psb[1][:, 0:TP], att_b[:TP, 512:640], ident[:TP, :TP])
            for h in range(4):
                k.tr(psb[1][:, 128 * (1 + h):128 * (1 + h) + TP], att_b[:TP, 640 + 128 * h:640 + 128 * (h + 1)],
                     ident[:TP, :TP])
            k.tr(psb[1][:, 640:640 + TP], att_b[:TP, 1152:1280], ident[:TP, :TP])
            k.copy("dve", kT[:, Lprev:Lprev + TP], psb[1][:, 0:TP])
            k.copy("act", qiT[:, 0:4 * TP].re("p (j t) -> p j t", j=4),
                   psb[1][:, 128:640].re("p (j t) -> p j t", j=4)[:, :, 0:TP])
            Bn = Lprev // 512
            hf = Bn % 2
            c0 = (Bn // 2) * 512 + (Lprev % 512)
            k.copy("dve", kiTp[hf * 64:hf * 64 + 64, c0:c0 + TP], psb[1][hf * 64:hf * 64 + 64, 640:640 + TP])
            nblk = (Lb + 511) // 512
            for B in range(nblk):
                wB = min(512, Lb - 512 * B)
                hb = B % 2
                cb = (B // 2) * 512
                for h in range(4):
                    pb = rbank()
                    k.mm(pb[:, 0:wB], qiT[hb * 64:hb * 64 + 64, h * TP:h * TP + 128],
                         kiTp[hb * 64:hb * 64 + 64, cb:cb + wB])
                    r_ = rr[(B * 4 + h) % 2]
                    k.act(r_[:TP, 0:wB], pb[:TP, 0:wB], AF.Relu)
                    if h == 0:
                        k.ts("dve", Sc[:TP, 512 * B:512 * B + wB], r_[:TP, 0:wB], att_f[:TP, 1088:1089], ALU.mult)
                    else:
                        k.stt(Sc[:TP, 512 * B:512 * B + wB], r_[:TP, 0:wB], att_f[:TP, 1088 + h:1089 + h],
                              Sc[:TP, 512 * B:512 * B + wB], ALU.mult, ALU.add)
            if masked:
                k.memset("dve", Sc[0:64, Lb - 64:Lb], NEG)
            return dict(t=t, TP=TP, Lb=Lb, qT=qTc)

        def attn_B(c):
            TP, Lb = c["TP"], c["Lb"]
            R = 512.0
            NIT = cfg.NIT
            m, lo = smM[:TP, 0:1], smM[:TP, 1:2]
            cnt, dd, cn2 = smC[:TP, 0:1], smD[:TP, 0:1], smD[:TP, 1:2]
            accA = smA[:TP, 0:1]
            thr = cfg.TOPK - 0.5
            import os as _os
            Ld = Lb if Lb < int(_os.environ.get("SPLIT_MIN", "1536")) else ((int(0.60 * Lb) + 63) // 64) * 64
            nA = Lb - Ld
            nch = 0 if nA == 0 else (1 if nA < int(_os.environ.get("ACT_MIN", "1024")) else int(_os.environ.get("ACT_CH", "1")))
            bounds = []
            if nch:
                stepc = ((nA + nch - 1) // nch + 63) // 64 * 64
                a = Ld
                while a < Lb:
                    bounds.append((a, min(Lb, a + stepc)))
                    a += stepc
            c["ny"] = NIT * (len(bounds) + 1)
            k.memset("dve", m, 0.0)
            for i in range(NIT):
                k.ts("dve", junk[:TP, 0:Ld], Sc[:TP, 0:Ld], m, ALU.is_ge, None, ALU.add, accum=cnt)
                cuse, tuse = cnt, thr
                if nA > 0:
                    for ci_, (a, b) in enumerate(bounds):
                        k.act(junkA[:TP, 0:b - a], Sc[:TP, a:b], AF.Sign, bias=m, scale=-1.0, accum=smA[:TP, ci_:ci_ + 1])
                        yield ("c", i)
                    prev = cnt
                    for ci_ in range(len(bounds)):
                        k.stt(cn2, smA[:TP, ci_:ci_ + 1], -0.5, prev, ALU.mult, ALU.add)
                        prev = cn2
                    cuse, tuse = cn2, thr - 0.5 * nA
                if i < NIT - 1:
                    cn = R / (2.0 ** (i + 1))
                    k.ts("dve", dd, cuse, tuse, ALU.is_ge, 2.0 * cn, ALU.mult)
                    k.stt(m, dd, -cn, m, ALU.add, ALU.add)
                else:
                    ci = R / (2.0 ** i)
                    k.ts("dve", dd, cuse, tuse, ALU.is_lt, -ci, ALU.mult)
                    k.tt("dve", lo, m, dd, ALU.add)
                yield ("i", i)

        def attn_Bfinal(c):
            TP, Lb = c["TP"], c["Lb"]
            lo = smM[:TP, 1:2]
            k.ts("dve", Mb[:TP, 0:Lb], Sc[:TP, 0:Lb], lo, ALU.is_lt, MASKNEG, ALU.mult)
            if TP < 128:
                k.memset("dve", Mb[TP:128, 0:Lb], 0.0)

        def attn_Cmain(c):
            TP, Lb, qTc = c["TP"], c["Lb"], c["qT"]
            I4 = I4p if TP == 128 else I4s
            nkt = (Lb + 127) // 128
            po = [psf[4], psf[5]]
            rot["set"] = [0, 1, 2, 3]

            def qk_pair(kt):
                kw = min(128, Lb - 128 * kt)
                pls = [rbank(), rbank()]
                for g in range(2):
                    k.mm(pls[g][:, 0:4 * TP], kT[g * 64:(g + 1) * 64, kt * 128:kt * 128 + 128],
                         qTc[g * 64:(g + 1) * 64, 0:4 * TP], start=True, stop=True)
                for g in range(2):
                    k.mm(pls[g][:kw, 0:4 * TP], Mb[:, kt * 128:kt * 128 + kw], I4[:, 0:4 * TP], start=False, stop=True,
                         skip_group_check=True)
                return pls

            cur = qk_pair(0)
            for kt in range(nkt):
                kw = min(128, Lb - 128 * kt)
                nxt = qk_pair(kt + 1) if kt + 1 < nkt else None
                ps = []
                for g in range(2):
                    p_ = pT[(2 * kt + g) % 3]
                    k.act(p_[:kw, 0:4 * TP], cur[g][:kw, 0:4 * TP], AF.Exp)
                    ps.append(p_)
                yield kt
                for g in range(2):
                    gs = slice(g * 64, (g + 1) * 64)
                    k.mm(po[0][gs, 0:4 * TP], Vb[:kw, kt, gs], ps[g][:kw, 0:4 * TP], start=(kt == 0), stop=(kt == nkt - 1))
                for g in range(2):
                    gs = slice(g * 64, (g + 1) * 64)
                    k.mm(po[1][gs, 0:4 * TP], ones_bf[:kw, 0:64], ps[g][:kw, 0:4 * TP], start=(kt == 0), stop=(kt == nkt - 1))
                yield kt
                cur = nxt

        def attn_Ctail(c):
            TP, t = c["TP"], c["t"]
            po = [psf[4], psf[5]]
            rd = rr[0]
            k.op("dve", lambda gg: gg.reciprocal(rd.t[:, 0:4 * TP], po[1].t[:, 0:4 * TP]), r=[po[1]], w=[rd])
            k.tt("dve", yaT[:, :, t * TP:(t + 1) * TP], po[0][:, 0:4 * TP].re("p (c q) -> p c q", c=4),
                 rd[:, 0:4 * TP].re("p (c q) -> p c q", c=4), ALU.mult)

        def hgrn(l, TP, NT):
            TT = TP * NT
            nch = TP // 64
            rot["set"] = [0, 1]
            for c, nm in ((0, "hig0"), (1, "hig1")):
                w_ = loadw(l, nm, "tm")
                for t in range(NT):
                    pb = rbank()
                    for kc in range(8):
                        k.mm(pb[:TP, :], xT[:, kc, t * TP:(t + 1) * TP], w_[:, kc * 512:(kc + 1) * 512],
                             start=(kc == 0), stop=(kc == 7), inc=(kc == 7))
                    if c == 0:
                        k.copy("act", vb[:TP, t * 512:(t + 1) * 512], pb[:TP, :])
                    else:
                        k.act(sg[:TP, t * 512:(t + 1) * 512], pb[:TP, :], AF.Silu)
            oacc = [psf[2 + t] for t in range(NT)]
            h0, h1, h2, h3 = hs_
            sets = [dict(E=h0, QA=QA, QB=QB, KT=KT, scm=scm, Ktok=Ktok, Sbf=Sbf, pt=psb[0]),
                    dict(E=E1, QA=QA1, QB=QB1, KT=KT1, scm=scm1, Ktok=Ktok1, Sbf=Sbf1, pt=psb[1])]
            if nch == 2:
                for st_ in sets:
                    k.memset("pool", st_["QA"][:, 0:TT].re("p (t u s) -> p t u s", u=2, s=64)[:, :, 1, :], 0.0)
                    k.memset("pool", st_["QB"][:, 0:TT].re("p (t u s) -> p t u s", u=2, s=64)[:, :, 0, :], 0.0)

            def head_pre(h):
                st_ = sets[h % 2]
                E, QAh, QBh, KTh = st_["E"], st_["QA"], st_["QB"], st_["KT"]
                wq_ = loadw(l, "hqf%d" % h, "fm")
                wf_ = loadw(l, "hqf%d" % (4 + h), "fm")
                zq = rbank()
                for kc in range(8):
                    k.mm(zq[:, 0:TT], wq_[:, kc * 128:(kc + 1) * 128], xT[:, kc, 0:TT], start=(kc == 0), stop=(kc == 7), inc=(kc == 7))
                zf = rbank()
                for kc in range(8):
                    k.mm(zf[:, 0:TT], wf_[:, kc * 128:(kc + 1) * 128], xT[:, kc, 0:TT], start=(kc == 0), stop=(kc == 7), inc=(kc == 7))
                lbc = lb[:, l * 4 + h:l * 4 + h + 1]
                k.act(E[:, 0:TT], zf[:, 0:TT], AF.Exp, scale=-1.0)
                k.act(h1[:, 0:TT], E[:, 0:TT], AF.Ln, bias=cst[:, 0:1], scale=1.0)
                k.act(E[:, 0:TT], E[:, 0:TT], AF.Ln, bias=cst[:, 0:1], scale=lbc)
                k.tt("dve", E[:, 0:TT], E[:, 0:TT], h1[:, 0:TT], ALU.subtract)
                k.act(h2[:, 0:TT], zf[:, 0:TT], AF.Sigmoid, scale=-1.0)
                k.ts("dve", h2[:, 0:TT], h2[:, 0:TT], oml[:, l * 4 + h:l * 4 + h + 1], ALU.mult)
                k.act(h3[:, 0:TT], zq[:, 0:TT], AF.Silu)
                for c in range(TT // 64):
                    k.op("dve", lambda g, c=c: g.tensor_tensor_scan(h1.ap[:, c * 64:(c + 1) * 64], ones.t[:, 0:64],
                                                                   E.ap[:, c * 64:(c + 1) * 64], 0.0, ALU.mult, ALU.add),
                         r=[ones, E], w=[h1])
                k.act(E[:, 0:TT], h1[:, 0:TT], AF.Exp)
                k.act(h1[:, 0:TT], h1[:, 0:TT], AF.Exp, scale=-1.0)
                if nch == 2:
                    qv = h3[:, 0:TT].re("p (t u s) -> p t u s", u=2, s=64)
                    ev = E[:, 0:TT].re("p (t u s) -> p t u s", u=2, s=64)
                    k.tt("dve", QAh[:, 0:TT].re("p (t u s) -> p t u s", u=2, s=64)[:, :, 0, :], qv[:, :, 0, :], ev[:, :, 0, :], ALU.mult)
                    k.tt("dve", QBh[:, 0:TT].re("p (t u s) -> p t u s", u=2, s=64)[:, :, 1, :], qv[:, :, 1, :], ev[:, :, 1, :], ALU.mult)
                else:
                    k.tt("dve", QAh[:, 0:TT], h3[:, 0:TT], E[:, 0:TT], ALU.mult)
                k.tt("dve", KTh[:, 0:TT], h2[:, 0:TT], h1[:, 0:TT], ALU.mult)
                k.copy("pool", st_["Sbf"][:, :], Sst_v[h][:, :])

            def head_chain(h):
                st_ = sets[h % 2]
                E, QAh, QBh, KTh = st_["E"], st_["QA"], st_["QB"], st_["KT"]
                scm_, Ktok_, Sbf_, pt_ = st_["scm"], st_["Ktok"], st_["Sbf"], st_["pt"]
                for t in range(NT):
                    sl = slice(t * TP, (t + 1) * TP)
                    sc = rbank()
                    k.mm(sc[:TP, 0:64], KTh[:, sl], QAh[:, t * TP:t * TP + 64])
                    if nch == 2:
                        k.mm(sc[:TP, 64:128], KTh[:, sl], QBh[:, t * TP + 64:t * TP + 128])
                    k.tt("dve", scm_[:TP, 0:TP], sc[:TP, 0:TP], bmask[:TP, 0:TP], ALU.mult)
                    k.tr(pt_[:TP, 0:128], KTh[:, sl], ident[:, :])
                    k.copy("act", Ktok_[:TP, :], pt_[:TP, 0:128])
                    ob = oacc[t][:TP, h * 128:(h + 1) * 128]
                    for u in range(nch):
                        us = slice(u * 64, (u + 1) * 64)
                        vsl = vb[us, t * 512 + h * 128:t * 512 + (h + 1) * 128]
                        k.mm(ob, scm_[us, 0:TP], vsl, start=(not started[t] and u == 0), stop=False, skip_group_check=True)
                        started[t] = True
                        qsrc = QAh if u == 0 else QBh
                        k.mm(ob, qsrc[:, sl], Sbf_[:, :], start=False, stop=(u == nch - 1), skip_group_check=True)
                        kv = rbank()
                        k.mm(kv[:, 0:128], Ktok_[us, :], vsl)
                        k.tt("dve", Sst_v[h][:, :], kv[:, 0:128], Sst_v[h][:, :], ALU.add)
                        ecol = t * TP + u * 64 + 63
                        k.act(Sst_v[h][:, :], Sst_v[h][:, :], AF.Identity, scale=E[:, ecol:ecol + 1])
                        k.copy("pool", Sbf_[:, :], Sst_v[h][:, :])
                        yield (t, u)

            started = [False] * NT
            for hp in range(2):
                ha, hb_ = 2 * hp, 2 * hp + 1
                head_pre(ha)
                head_pre(hb_)
                ga, gb_ = head_chain(ha), head_chain(hb_)
                for _ in range(NT * nch):
                    next(ga, None)
                    next(gb_, None)
                for _ in ga:
                    pass
                for _ in gb_:
                    pass
            for t in range(NT):
                ob = oacc[t]
                for h in range(4):
                    k.act(on[:TP, 0:128], ob[:TP, h * 128:(h + 1) * 128], AF.Square, accum=sm[:TP, 16 + h:17 + h])
                k.act(sm[:TP, 20:24], sm[:TP, 16:20], AF.Sqrt, bias=cst[:TP, 1:2], scale=1.0 / 128.0)
                k.op("dve", lambda g: g.reciprocal(sm.t[:TP, 24:28], sm.t[:TP, 20:24]), r=[sm], w=[sm])
                for h in range(4):
                    k.ts("dve", on[:TP, h * 128:(h + 1) * 128], ob[:TP, h * 128:(h + 1) * 128], sm[:TP, 24 + h:25 + h], ALU.mult)
                k.tt("dve", on[:TP, :], on[:TP, :], gB[:TP, :], ALU.mult)
                k.tt("dve", ybb[:TP, :], on[:TP, :], sg[:TP, t * 512:(t + 1) * 512], ALU.mult)
                for c in range(4):
                    k.tr(psb[1][:, c * 128:c * 128 + TP], ybb[:TP, c * 128:(c + 1) * 128], ident[:TP, :TP])
                k.copy("act", ybT[:, :, t * TP:(t + 1) * TP], psb[1][:, 0:512].re("p (c t) -> p c t", c=4)[:, :, 0:TP])

        def sconv(l, TT):
            rot["set"] = [0, 1, 2, 3]
            h0, h1, h2, h3 = hs_
            for j in range(4):
                wb_ = loadw(l, "c%d" % j, "fm"); wc_ = loadw(l, "c%d" % (4 + j), "fm"); wx_ = loadw(l, "c%d" % (8 + j), "fm")
                pbk, pck, pxk = rbank(), rbank(), rbank()
                for (pp, ww) in ((pbk, wb_), (pck, wc_), (pxk, wx_)):
                    for kc in range(8):
                        k.mm(pp[:, 0:TT], ww[:, kc * 128:(kc + 1) * 128], xT[:, kc, 0:TT], start=(kc == 0), stop=(kc == 7), inc=(kc == 7))
                uext = V(AR1.t[:, 0:1024], h0.buf)
                k.copy("pool", uext[:, 0:2], uh[:, 2 * j:2 * j + 2])
                k.copy("act", h2[:, 0:TT], pxk[:, 0:TT])
                k.op("dve", lambda g: g.tensor_tensor(uext.ap[:, 2:2 + TT], pck.t[:, 0:TT], h2.ap[:, 0:TT], ALU.mult),
                     r=[pck, h2], w=[h0, h1])
                k.op("pool", lambda g, j=j: g.tensor_copy(uh.t[:, 2 * j:2 * j + 2], uext.ap[:, TT:TT + 2]), r=[h0, h1], w=[uh])
                k.op("dve", lambda g, j=j: g.tensor_scalar(h3.ap[:, 0:TT], uext.ap[:, 2:2 + TT], scw.t[:, 4 * j + 2:4 * j + 3],
                                                          scw.t[:, 4 * j + 3:4 * j + 4], ALU.mult, ALU.add),
                     r=[h0, h1, scw], w=[h3])
                k.op("dve", lambda g, j=j: g.scalar_tensor_tensor(h3.ap[:, 0:TT], uext.ap[:, 1:1 + TT], scw.t[:, 4 * j + 1:4 * j + 2],
                                                                 h3.ap[:, 0:TT], ALU.mult, ALU.add),
                     r=[h0, h1, scw, h3], w=[h3])
                k.op("dve", lambda g, j=j: g.scalar_tensor_tensor(h3.ap[:, 0:TT], uext.ap[:, 0:TT], scw.t[:, 4 * j:4 * j + 1],
                                                                 h3.ap[:, 0:TT], ALU.mult, ALU.add),
                     r=[h0, h1, scw, h3], w=[h3])
                k.tt("dve", ycT[:, j, 0:TT], pbk[:, 0:TT], h3[:, 0:TT], ALU.mult)

        def merge(l, TP, NT):
            TT = TP * NT
            rot["set"] = [0, 1, 2, 3, 4, 5]
            ysrc = [yaT, ybT, ycT]
            for cc in range(8):
                for b in range(3):
                    wg_ = loadw(l, "g%d" % (b * 8 + cc), "fm")
                    wb_ = loadw(l, "br%d_%d" % (b, cc), "f4")
                    gp = rbank()
                    for kc in range(8):
                        k.mm(gp[:, 0:TT], wg_[:, kc * 128:(kc + 1) * 128], xT[:, kc, 0:TT], start=(kc == 0), stop=(kc == 7), inc=(kc == 7))
                    bp = rbank()
                    for kc in range(4):
                        k.mm(bp[:, 0:TT], wb_[:, kc * 128:(kc + 1) * 128], ysrc[b][:, kc, 0:TT], start=(kc == 0), stop=(kc == 3), inc=(kc == 3))
                    k.act(sgm[:, 0:TT], gp[:, 0:TT], AF.Sigmoid)
                    if b == 0:
                        k.tt("dve", mg[:, 0:TT], sgm[:, 0:TT], bp[:, 0:TT], ALU.mult)
                    elif b == 1:
                        k.tt("dve", tmpm[:, 0:TT], sgm[:, 0:TT], bp[:, 0:TT], ALU.mult)
                        k.tt("pool", mg[:, 0:TT], mg[:, 0:TT], tmpm[:, 0:TT], ALU.add)
                    else:
                        k.tt("dve", tmpm[:, 0:TT], sgm[:, 0:TT], bp[:, 0:TT], ALU.mult)
                        k.tt("pool", gT[:, cc * 512:cc * 512 + TT], mg[:, 0:TT], tmpm[:, 0:TT], ALU.add)
            for c in range(2):
                wo_ = loadw(l, "wout%d" % c, "tm")
                for t in range(NT):
                    pb = rbank()
                    for kc in range(8):
                        k.mm(pb[:TP, :], gT[:, kc * 512 + t * TP:kc * 512 + (t + 1) * TP], wo_[:, kc * 512:(kc + 1) * 512],
                             start=(kc == 0), stop=(kc == 7), inc=(kc == 7))
                    xv = xs_v[t][:TP, c * 512:(c + 1) * 512]
                    k.stt(xv, xv, ALPHA, pb[:TP, :], ALU.mult, ALU.add)
            for t in range(NT):
                layer_norm(TP, t, 0)

        def ffn(l, TP, NT):
            TT = TP * NT
            rot["set"] = [0, 1, 2, 3, 4, 5]
            for j in range(22):
                for wi, chn_ in enumerate((j, 22 + j)):
                    w_ = loadw(l, "up%d" % chn_, "fm")
                    hp = rbank()
                    for kc in range(8):
                        k.mm(hp[:, 0:TT], w_[:, kc * 128:(kc + 1) * 128], xT[:, kc, 0:TT], start=(kc == 0), stop=(kc == 7), inc=(kc == 7))
                    u_ = ue[wi]
                    a_ = fac[wi]
                    k.copy("pool", u_[:, 0:2], fh[:, 2 * chn_:2 * chn_ + 2])
                    k.copy("act", u_[:, 2:2 + TT], hp[:, 0:TT])
                    k.copy("pool", fh[:, 2 * chn_:2 * chn_ + 2], u_[:, TT:TT + 2])
                    k.act(a_[:, 0:TT], hp[:, 0:TT], AF.Identity, bias=fcw[:, 4 * chn_ + 3:4 * chn_ + 4],
                          scale=fcw[:, 4 * chn_ + 2:4 * chn_ + 3])
                    k.stt(a_[:, 0:TT], u_[:, 1:1 + TT], fcw[:, 4 * chn_ + 1:4 * chn_ + 2], a_[:, 0:TT], ALU.mult, ALU.add)
                    k.stt(a_[:, 0:TT], u_[:, 0:TT], fcw[:, 4 * chn_:4 * chn_ + 1], a_[:, 0:TT], ALU.mult, ALU.add)
                k.act(sa[:, 0:TT], fac[0][:, 0:TT], AF.Silu)
                k.tt("dve", gT[:, j * 512:j * 512 + TT], sa[:, 0:TT], fac[1][:, 0:TT], ALU.mult)
            acc = [psf[2 + t] for t in range(NT)]
            for c in range(2):
                kc0 = 0
                for kg, nk in enumerate((8, 8, 6)):
                    wd_ = loadw(l, "wd%d_%d" % (c, kg), "tm")
                    for t in range(NT):
                        for kk in range(nk):
                            kc = kc0 + kk
                            k.mm(acc[t][:TP, :], gT[:, kc * 512 + t * TP:kc * 512 + (t + 1) * TP], wd_[:, kk * 512:(kk + 1) * 512],
                                 start=(kc == 0), stop=(kc == 21), inc=(kk == nk - 1))
                    kc0 += nk
                for t in range(NT):
                    xv = xs_v[t][:TP, c * 512:(c + 1) * 512]
                    k.stt(xv, xv, ALPHA, acc[t][:TP, :], ALU.mult, ALU.add)
            for t in range(NT):
                layer_norm(TP, t, 1)

        def run_group(l, grp):
            if grp == "p":
                TP, NT, nmac, Lbase = 128, 4, S // 512, 0
                xsrc = xp if l == 0 else xmid_p
                xdst = xmid_p if l == 0 else yp
                okd, ovd, okid, ohd, oscd, offd, ropd = okp, ovp, okip, ohp, oscp, offp, ropep
                xmb = xmbuf_p
            else:
                TP, NT, nmac, Lbase = 64, 1, 1, P
                xsrc = xs if l == 0 else xmid_s
                xdst = xmid_s if l == 0 else ys
                okd, ovd, okid, ohd, oscd, offd, ropd = oks, ovs, okis, ohs, oscs, offs, ropes
                xmb = xmbuf_s
            TT = TP * NT
            k.dma("sp", ch("gB"), gB[:, :], gB_d[l])
            k.dma("sp", ch("scw"), scw[:, :], scw_d[l])
            k.dma("sp", ch("fcw"), fcw[:, :], fcw_d[l])
            k.dma("sp", ch("lnp"), lnp[:, :], lnp_d[l])
            if grp == "p":
                k.op("pool", lambda g: g.memset(Sst.t[:, :, :], 0.0), r=[], w=[Sst] + Sst_v)
                k.memset("pool", uh[:, :], 0.0)
                k.memset("pool", fh[:, :], 0.0)
            else:
                k.dma("sp", ch("Sst"), Sst[:, :, :], sh[l].rearrange("h d v -> d h v"), w=Sst_v)
                k.dma("sp", ch("uh"), uh[:, :], ssc[l])
                k.dma("sp", ch("fh"), fh[:, :], sff[l])
                claim(useA, useB + useC)
                for kt in range(P // 128):
                    rows = slice(kt * 128, (kt + 1) * 128)
                    k.dma("sp", ch("cst0"), att_f[:, 0:128], ck[l, rows, :])
                    k.dma("sp", ch("cst1"), att_f[:, 128:256], cv[l, rows, :])
                    k.dma("sp", ch("cst2"), att_f[:, 256:320], cki[l, rows, :])
                    k.copy("pool", att_b[:, 0:128], att_f[:, 0:128])
                    k.copy("pool", att_b[:, 128:192], att_f[:, 256:320])
                    k.copy("pool", att_b[:, 192:256], att_f[:, 256:320])
                    k.copy("pool", Vb[:, kt, :], att_f[:, 128:256])
                    k.tr(psb[0][:, 0:128], att_b[:, 0:128], ident[:, :])
                    k.tr(psb[0][:, 128:256], att_b[:, 128:256], ident[:, :])
                    k.copy("dve", kT[:, kt * 128:(kt + 1) * 128], psb[0][:, 0:128])
                    Bn = (kt * 128) // 512
                    hf = Bn % 2
                    c0 = (Bn // 2) * 512 + (kt * 128) % 512
                    k.copy("act", kiTp[hf * 64:hf * 64 + 64, c0:c0 + 128], psb[0][hf * 64:hf * 64 + 64, 128:256])
            import os
            SL = int(os.environ.get("SL", "9"))
            for mt in range(nmac):
                if grp == "s" and SL < 1:
                    break
                tok0 = mt * TT
                claim(useA, useB + useC)
                for t in range(NT):
                    k.dma("sp", ch("x_sb%d" % t), xs_v[t][:TP, :], xsrc[tok0 + t * TP:tok0 + (t + 1) * TP, :],
                          r=[xmb[mt][t]] if l == 1 else [])
                make_xT(TP, NT)
                if cfg.stop >= 2 and not (grp == "s" and SL < 2):
                    prev = None
                    for t in range(NT):
                        row0 = tok0 + t * TP
                        cur = attn_A(l, t, TP, Lbase + row0, grp == "p", ropd[row0:row0 + TP, :], okd, ovd, okid, row0)
                        gb = attn_B(cur)
                        if prev is not None:
                            gc = attn_Cmain(prev)
                            nC = 2 * ((prev["Lb"] + 127) // 128)
                            done_c = 0
                            yi = 0
                            for _ in gb:
                                yi += 1
                                tgt = min(nC, (yi * nC) // max(1, cur.get("ny", cfg.NIT)))
                                while done_c < tgt:
                                    next(gc, None)
                                    done_c += 1
                            for _ in gc:
                                pass
                            attn_Bfinal(cur)
                            attn_Ctail(prev)
                        else:
                            for _ in gb:
                                pass
                            attn_Bfinal(cur)
                        prev = cur
                    for _ in attn_Cmain(prev):
                        pass
                    attn_Ctail(prev)
                claim(useB, useA)
                if cfg.stop >= 3:
                    hgrn(l, TP, NT)
                if cfg.stop >= 4:
                    sconv(l, TT)
                claim(useC, useA + useB)
                if cfg.stop >= 5:
                    merge(l, TP, NT)
                    make_xT(TP, NT)
                if cfg.stop >= 6:
                    ffn(l, TP, NT)
                for t in range(NT):
                    k.dma("pool", ch("st_x%d" % t), xdst[tok0 + t * TP:tok0 + (t + 1) * TP, :], xs_v[t][:TP, :],
                          w=[xmb[mt][t]] if l == 0 else [])
            k.dma("pool", ch("st_S"), ohd[l].rearrange("h d v -> d h v"), Sst[:, :, :], r=Sst_v)
            k.dma("pool", ch("st_uh"), oscd[l], uh[:, :])
            k.dma("pool", ch("st_fh"), offd[l], fh[:, :])

        xmbuf_p = [[Buf("xmp%d_%d" % (i, t)) for t in range(4)] for i in range(S // 512)]
        xmbuf_s = [[Buf("xms")]]
        for grp in ("p", "s"):
            for l in range(2):
                import os as _os
                if _os.environ.get("ONLYS") and not (grp == "s" and l == 0):
                    continue
                if cfg.stop >= 9 or (cfg.stop >= 1 and grp == "p" and l == 0) or (cfg.stop >= 7 and grp == "p") or (cfg.stop >= 8 and l == 0):
                    run_group(l, grp)
        k.wait_all("pool")
        k.wait_all("sp")
        print("instructions:", k.ninst, "channels:", k.nchan, "sbuf_left:", nc.sbuf_bytes_remaining)
    return nc


def kernel_cfg(cfg, inputs, n_cores=8):
    f32 = np.float32
    g = lambda n: np.asarray(inputs[n], dtype=f32)
    x_prompt, x_sample = g("x_prompt"), g("x_sample")
    S, DS, P = cfg.S, cfg.DS, cfg.P
    wfl = host_weights(g("w_in"), g("w_branch"), g("w_out"), g("w_up"), g("w_down"))
    lbl = np.ascontiguousarray(g("hgrn_lb_logits").reshape(2, 4, 128).transpose(2, 0, 1).reshape(128, 8))
    gBv = np.ascontiguousarray(np.broadcast_to(np.tile(g("hgrn_norm_g"), (1, 4))[:, None, :], (2, 128, 512)))
    scw = np.concatenate([g("sconv_w"), g("sconv_b")[:, None, :]], axis=1)
    scw = np.ascontiguousarray(scw.reshape(2, 4, 4, 128).transpose(0, 3, 2, 1).reshape(2, 128, 16))
    fcw = np.concatenate([g("ffn_conv_w"), g("ffn_conv_b")[:, None, :]], axis=1)
    fcw = np.ascontiguousarray(fcw.reshape(2, 4, 44, 128).transpose(0, 3, 2, 1).reshape(2, 128, 176))
    lnp = np.stack([g("ln1_g"), g("ln1_b"), g("ln2_g"), g("ln2_b")], axis=1).reshape(2, 1, 4096)
    lnp = np.ascontiguousarray(np.broadcast_to(lnp, (2, 128, 4096)))
    ropep = rope_table(np.arange(S))
    ropes = rope_table(P + np.arange(DS))
    ck = g("cache_attn_k").reshape(2, -1, P, 128)
    cv = g("cache_attn_v").reshape(2, -1, P, 128)
    cki = g("cache_idx_k")
    sh = g("state_hgrn")
    ssc = g("state_sconv")
    sff = g("state_ffn_conv")
    nb = x_prompt.shape[0]
    in_maps = []
    for c in range(n_cores):
        pb = (c // 2) % nb
        sb = c % x_sample.shape[0]
        ssc_t = np.ascontiguousarray(ssc[:, sb].reshape(2, 2, 4, 128).transpose(0, 3, 2, 1).reshape(2, 128, 8))
        sff_t = np.ascontiguousarray(sff[:, sb].reshape(2, 2, 44, 128).transpose(0, 3, 2, 1).reshape(2, 128, 88))
        in_maps.append({
            "xp": np.ascontiguousarray(x_prompt[pb]), "xs": np.ascontiguousarray(x_sample[sb]),
            "ck": np.ascontiguousarray(ck[:, sb]), "cv": np.ascontiguousarray(cv[:, sb]),
            "cki": np.ascontiguousarray(cki[:, sb]), "sh": np.ascontiguousarray(sh[:, sb]),
            "ssc": ssc_t, "sff": sff_t, "wfl": wfl, "lbl": lbl, "gB": gBv, "scw": scw, "fcw": fcw, "lnp": lnp,
            "ropep": ropep, "ropes": ropes,
        })
    nc = build(cfg)
    res = run_bass_kernel_spmd(nc, in_maps, core_ids=list(range(n_cores)))
    R = res.results
    NB, NSB = x_prompt.shape[0], x_sample.shape[0]
    pc = [2 * b for b in range(NB)]

    def st_t(a, nchk):
        return a.reshape(2, 128, nchk, 2).transpose(0, 3, 2, 1).reshape(2, 2, nchk * 128)

    y_p = np.stack([R[c]["yp"] for c in pc], 0)
    y_s = np.stack([R[c]["ys"] for c in range(NSB)], 0)
    k_p = np.stack([R[c]["okp"] for c in pc], 1).reshape(2, NB, S, 2, 64)
    v_p = np.stack([R[c]["ovp"] for c in pc], 1).reshape(2, NB, S, 2, 64)
    ki_p = np.stack([R[c]["okip"] for c in pc], 1)
    h_p = np.stack([R[c]["ohp"] for c in pc], 1)
    sc_p = np.stack([st_t(R[c]["oscp"], 4) for c in pc], 1)
    ff_p = np.stack([st_t(R[c]["offp"], 44) for c in pc], 1)
    k_s = np.stack([R[c]["oks"] for c in range(NSB)], 1).reshape(2, NSB, DS, 2, 64)
    v_s = np.stack([R[c]["ovs"] for c in range(NSB)], 1).reshape(2, NSB, DS, 2, 64)
    ki_s = np.stack([R[c]["okis"] for c in range(NSB)], 1)
    h_s = np.stack([R[c]["ohs"] for c in range(NSB)], 1)
    sc_s = np.stack([st_t(R[c]["oscs"], 4) for c in range(NSB)], 1)
    ff_s = np.stack([st_t(R[c]["offs"], 44) for c in range(NSB)], 1)
    outs = (y_p, y_s, k_p, v_p, ki_p, h_p, sc_p, ff_p, k_s, v_s, ki_s, h_s, sc_s, ff_s)
    return tuple(np.ascontiguousarray(o, dtype=np.float32) for o in outs)


def kernel(**inputs):
    return kernel_cfg(Cfg(), inputs)
```
